# Optimizing a Trainium2 kernel written in Bass

```python
import math
import jax, jax.numpy as jnp
from jax import lax
import numpy as np

D_MODEL = 4096
BATCH = 8
SEQ = 2048
DEPTH = 1
DEC_BATCH = 8
DEC_SEQ = 16
PAST_LEN = 2048

CHUNK = 64
SSD_HEAD_DIM = 64
SSD_HEADS = (D_MODEL // 2) // SSD_HEAD_DIM
SSD_WIDTH = SSD_HEADS * SSD_HEAD_DIM
SSD_GROUPS = 4
SSD_HEADS_PER_GROUP = SSD_HEADS // SSD_GROUPS
SSD_STATE = 128
SSD_CONV = 4
SSD_CONV_CH = SSD_WIDTH + 2 * SSD_GROUPS * SSD_STATE
ATTN_HEAD_DIM = 64
ATTN_HEADS = (D_MODEL // 2) // ATTN_HEAD_DIM
ATTN_KV_HEADS = 4
ATTN_Q_PER_KV = ATTN_HEADS // ATTN_KV_HEADS
ATTN_WIDTH = ATTN_HEADS * ATTN_HEAD_DIM
KV_WIDTH = ATTN_KV_HEADS * ATTN_HEAD_DIM
WINDOW = 128
WINDOW_CHUNKS = WINDOW // CHUNK
MIX_WIDTH = SSD_WIDTH + ATTN_WIDTH
D_IN_PROJ = SSD_WIDTH + SSD_CONV_CH + SSD_HEADS + ATTN_WIDTH + 2 * KV_WIDTH
D_FF = -(-8 * D_MODEL // (3 * 256)) * 256
N_MOD = 6
EPS = 1e-6

kernel_name = "hybrid_ssd_swa_stream_step"


def rms_norm(x, w):
    xf = x.astype(jnp.float32)
    y = xf * lax.rsqrt(jnp.mean(xf * xf, axis=-1, keepdims=True) + EPS)
    return (y * w.astype(jnp.float32)).astype(x.dtype)


def gated_rms_norm(y, z, w):
    g = y.astype(jnp.float32) * jax.nn.silu(z.astype(jnp.float32))
    shp = g.shape
    g = g.reshape(shp[:-1] + (SSD_GROUPS, SSD_WIDTH // SSD_GROUPS))
    g = g * lax.rsqrt(jnp.mean(g * g, axis=-1, keepdims=True) + EPS)
    return (g.reshape(shp) * w.astype(jnp.float32)).astype(z.dtype)


def segsum(a):
    t = a.shape[-1]
    cs = jnp.cumsum(a, axis=-1)
    diff = cs[..., :, None] - cs[..., None, :]
    return jnp.where(jnp.tril(jnp.ones((t, t), dtype=bool)), diff, -jnp.inf)


def ssd_scan(x, dt, a_neg, b_in, c_in, init_state):
    bsz, seq = x.shape[0], x.shape[1]
    blk = min(CHUNK, seq)
    nb = seq // blk
    g, k, p, n = SSD_GROUPS, SSD_HEADS_PER_GROUP, SSD_HEAD_DIM, SSD_STATE
    xbar = (x.astype(jnp.float32) * dt[..., None]).reshape(bsz, nb, blk, g, k, p)
    a = (dt * a_neg).reshape(bsz, nb, blk, g, k).transpose(0, 3, 4, 1, 2)
    a_cs = jnp.cumsum(a, axis=-1)
    decay_in = jnp.exp(segsum(a))
    bb = b_in.astype(jnp.float32).reshape(bsz, nb, blk, g, n)
    cb = c_in.astype(jnp.float32).reshape(bsz, nb, blk, g, n)
    cbt = jnp.einsum("bclgn,bcsgn->bcgls", cb, bb)
    y_diag = jnp.einsum("bcgls,bgkcls,bcsgkp->bclgkp", cbt, decay_in, xbar)
    decay_to_end = jnp.exp(a_cs[..., -1:] - a_cs)
    blk_states = jnp.einsum("bcsgn,bgkcs,bcsgkp->bcgkpn", bb, decay_to_end, xbar)
    init = init_state.astype(jnp.float32).reshape(bsz, 1, g, k, p, n)
    blk_states = jnp.concatenate([init, blk_states], axis=1)
    blk_decay = jnp.exp(segsum(jnp.pad(a_cs[..., -1], ((0, 0), (0, 0), (0, 0), (1, 0)))))
    states = jnp.einsum("bgkzc,bcgkpn->bzgkpn", blk_decay, blk_states)
    y_off = jnp.einsum("bclgn,bcgkpn,bgkcl->bclgkp", cb, states[:, :-1], jnp.exp(a_cs))
    y = (y_diag + y_off).reshape(bsz, seq, SSD_HEADS, SSD_HEAD_DIM)
    final = states[:, -1].reshape(bsz, SSD_HEADS, SSD_HEAD_DIM, SSD_STATE)
    return y, final


def causal_conv(xbc, left, w, bias):
    xp = jnp.concatenate([left.astype(xbc.dtype), xbc], axis=1)
    out = lax.conv_general_dilated(xp, w.astype(xp.dtype)[:, None, :], (1,), "VALID",
                                   dimension_numbers=("NWC", "WIO", "NWC"),
                                   feature_group_count=SSD_CONV_CH)
    return jax.nn.silu(out + bias), xp[:, xp.shape[1] - (SSD_CONV - 1):]


def alibi_slopes():
    return jnp.exp2(-8.0 * jnp.arange(1, ATTN_HEADS + 1, dtype=jnp.float32) / ATTN_HEADS)


def band_mask(q_pos, k_pos):
    qc = q_pos[..., :, None] // CHUNK
    kc = k_pos[..., None, :] // CHUNK
    return (k_pos[..., None, :] >= 0) & (kc <= qc) & (qc - kc <= WINDOW_CHUNKS)


def chunk_window_attend(q, k, v, q_pos, k_pos, sinks):
    scale = ATTN_HEAD_DIM ** -0.5
    s = jnp.einsum("bnqkgd,bnskd->bnkgqs", q, k).astype(jnp.float32) * scale
    slopes = alibi_slopes().reshape(ATTN_KV_HEADS, ATTN_Q_PER_KV)[None, None, :, :, None, None]
    dist = jnp.abs(q_pos[..., :, None] - k_pos[..., None, :]).astype(jnp.float32)[None, :, None, None]
    s = jnp.where(band_mask(q_pos, k_pos)[None, :, None, None], s - slopes * dist, -jnp.inf)
    sink = sinks.astype(jnp.float32).reshape(ATTN_KV_HEADS, ATTN_Q_PER_KV)[None, None, :, :, None, None]
    m = jnp.maximum(jnp.max(s, axis=-1, keepdims=True), sink)
    p = jnp.exp(s - m)
    p = p / (jnp.sum(p, axis=-1, keepdims=True) + jnp.exp(sink - m))
    return jnp.einsum("bnkgqs,bnskd->bnqkgd", p.astype(v.dtype), v)


def attn_prompt(q, k, v, sinks):
    bsz, seq = q.shape[0], q.shape[1]
    nc = seq // CHUNK
    qb = q.reshape(bsz, nc, CHUNK, ATTN_KV_HEADS, ATTN_Q_PER_KV, ATTN_HEAD_DIM)
    pad = ((0, 0), (WINDOW_CHUNKS * CHUNK, 0), (0, 0), (0, 0))
    kp = jnp.pad(k, pad).reshape(bsz, nc + WINDOW_CHUNKS, CHUNK, ATTN_KV_HEADS, ATTN_HEAD_DIM)
    vp = jnp.pad(v, pad).reshape(bsz, nc + WINDOW_CHUNKS, CHUNK, ATTN_KV_HEADS, ATTN_HEAD_DIM)
    kb = jnp.concatenate([kp[:, j:j + nc] for j in range(WINDOW_CHUNKS + 1)], axis=2)
    vb = jnp.concatenate([vp[:, j:j + nc] for j in range(WINDOW_CHUNKS + 1)], axis=2)
    q_pos = jnp.arange(seq).reshape(nc, CHUNK)
    k_pos = (jnp.arange(nc)[:, None] - WINDOW_CHUNKS) * CHUNK + jnp.arange((WINDOW_CHUNKS + 1) * CHUNK)[None, :]
    o = chunk_window_attend(qb, kb, vb, q_pos, k_pos, sinks)
    return o.reshape(bsz, seq, ATTN_WIDTH)


def attn_sample(q, k, v, cache_k, cache_v, sinks):
    bsz, seq = q.shape[0], q.shape[1]
    rows = cache_k.shape[1]
    qb = q.reshape(bsz, 1, seq, ATTN_KV_HEADS, ATTN_Q_PER_KV, ATTN_HEAD_DIM)
    kb = jnp.concatenate([cache_k.astype(k.dtype), k], axis=1)[:, None]
    vb = jnp.concatenate([cache_v.astype(v.dtype), v], axis=1)[:, None]
    q_pos = (PAST_LEN + jnp.arange(seq))[None]
    k_pos = jnp.concatenate([PAST_LEN - rows + jnp.arange(rows), PAST_LEN + jnp.arange(seq)])[None]
    o = chunk_window_attend(qb, kb, vb, q_pos, k_pos, sinks)
    return o.reshape(bsz, seq, ATTN_WIDTH)


def token_mixer(h, lp, ssd_init, conv_left, cache_kv):
    bsz, seq = h.shape[0], h.shape[1]
    proj = h @ lp["w_in"]
    o1 = SSD_WIDTH
    o2 = o1 + SSD_CONV_CH
    o3 = o2 + SSD_HEADS
    o4 = o3 + ATTN_WIDTH
    o5 = o4 + KV_WIDTH
    z, xbc, dt_raw, q, k, v = jnp.split(proj, [o1, o2, o3, o4, o5], axis=-1)
    xbc, conv_state = causal_conv(xbc, conv_left, lp["conv_w"], lp["conv_b"])
    xs, b_ssm, c_ssm = jnp.split(xbc, [SSD_WIDTH, SSD_WIDTH + SSD_GROUPS * SSD_STATE], axis=-1)
    dt = jax.nn.softplus(dt_raw.astype(jnp.float32) + lp["dt_bias"].astype(jnp.float32))
    a_neg = -jnp.exp(lp["a_log"].astype(jnp.float32))
    xs = xs.reshape(bsz, seq, SSD_HEADS, SSD_HEAD_DIM)
    y, ssd_state = ssd_scan(xs, dt, a_neg,
                            b_ssm.reshape(bsz, seq, SSD_GROUPS, SSD_STATE),
                            c_ssm.reshape(bsz, seq, SSD_GROUPS, SSD_STATE), ssd_init)
    y = y + xs.astype(jnp.float32) * lp["d_skip"].astype(jnp.float32)[:, None]
    y = gated_rms_norm(y.reshape(bsz, seq, SSD_WIDTH), z, lp["ssd_norm_w"])
    q = rms_norm(q.reshape(bsz, seq, ATTN_HEADS, ATTN_HEAD_DIM), lp["q_norm_w"])
    k = rms_norm(k.reshape(bsz, seq, ATTN_KV_HEADS, ATTN_HEAD_DIM), lp["k_norm_w"])
    v = v.reshape(bsz, seq, ATTN_KV_HEADS, ATTN_HEAD_DIM)
    if cache_kv is None:
        o = attn_prompt(q, k, v, lp["sinks"])
        keep = min(WINDOW, seq)
        k_state, v_state = k[:, seq - keep:], v[:, seq - keep:]
    else:
        o = attn_sample(q, k, v, cache_kv[0], cache_kv[1], lp["sinks"])
        k_state, v_state = k, v
    out = jnp.concatenate([y, o.astype(y.dtype)], axis=-1) @ lp["w_out"]
    return out, ssd_state.astype(h.dtype), conv_state, k_state, v_state


def layer(x, c, lp, ssd_init, conv_left, cache_kv):
    mod = (jax.nn.silu(c) @ lp["w_ada"] + lp["b_ada"]).reshape(c.shape[0], N_MOD, 1, D_MODEL)
    sh1, sc1, g1, sh2, sc2, g2 = [mod[:, i] for i in range(N_MOD)]
    h = rms_norm(x, lp["g_mix"]) * (1.0 + sc1) + sh1
    mix, ssd_s, conv_s, k_s, v_s = token_mixer(h, lp, ssd_init, conv_left, cache_kv)
    x = x + g1 * mix
    h = rms_norm(x, lp["g_ffn"]) * (1.0 + sc2) + sh2
    gate, up = jnp.split(h @ lp["w_gate_up"], 2, axis=-1)
    x = x + g2 * ((jax.nn.silu(gate) * up) @ lp["w_down"])
    return x, ssd_s, conv_s, k_s, v_s


def setup_inputs(seed: int = 0) -> dict:
    key = jax.random.key(seed)
    ks = jax.random.split(key, 26)
    f32 = jnp.float32

    def nrm(k, shape, scale):
        return jax.random.normal(k, shape, f32) * scale

    rows = min(WINDOW, PAST_LEN)
    dt0 = jnp.exp(jax.random.uniform(ks[14], (DEPTH, SSD_HEADS), f32, math.log(1e-3), math.log(1e-1)))
    return {
        "x_prompt": nrm(ks[0], (BATCH, SEQ, D_MODEL), 1.0),
        "x_sample": nrm(ks[1], (DEC_BATCH, DEC_SEQ, D_MODEL), 1.0),
        "state_ssd": nrm(ks[2], (DEPTH, DEC_BATCH, SSD_HEADS, SSD_HEAD_DIM, SSD_STATE), 0.1),
        "state_conv": nrm(ks[3], (DEPTH, DEC_BATCH, SSD_CONV - 1, SSD_CONV_CH), 1.0),
        "cache_k": nrm(ks[4], (DEPTH, DEC_BATCH, rows, ATTN_KV_HEADS, ATTN_HEAD_DIM), 1.0),
        "cache_v": nrm(ks[5], (DEPTH, DEC_BATCH, rows, ATTN_KV_HEADS, ATTN_HEAD_DIM), 1.0),
        "c_prompt": nrm(ks[6], (BATCH, D_MODEL), 1.0),
        "c_sample": nrm(ks[7], (DEC_BATCH, D_MODEL), 1.0),
        "w_ada": nrm(ks[8], (DEPTH, D_MODEL, N_MOD * D_MODEL), 0.5 * D_MODEL ** -0.5),
        "b_ada": nrm(ks[9], (DEPTH, N_MOD * D_MODEL), 0.01),
        "g_mix": 1.0 + nrm(ks[10], (DEPTH, D_MODEL), 0.02),
        "w_in": nrm(ks[11], (DEPTH, D_MODEL, D_IN_PROJ), D_MODEL ** -0.5),
        "conv_w": nrm(ks[12], (DEPTH, SSD_CONV, SSD_CONV_CH), SSD_CONV ** -0.5),
        "conv_b": nrm(ks[13], (DEPTH, SSD_CONV_CH), 0.01),
        "dt_bias": dt0 + jnp.log(-jnp.expm1(-dt0)),
        "a_log": jnp.log(jax.random.uniform(ks[15], (DEPTH, SSD_HEADS), f32, 1.0, 16.0)),
        "d_skip": 1.0 + nrm(ks[16], (DEPTH, SSD_HEADS), 0.1),
        "ssd_norm_w": 1.0 + nrm(ks[17], (DEPTH, SSD_WIDTH), 0.02),
        "q_norm_w": 1.0 + nrm(ks[18], (DEPTH, ATTN_HEAD_DIM), 0.02),
        "k_norm_w": 1.0 + nrm(ks[19], (DEPTH, ATTN_HEAD_DIM), 0.02),
        "sinks": nrm(ks[20], (DEPTH, ATTN_HEADS), 0.5),
        "w_out": nrm(ks[21], (DEPTH, MIX_WIDTH, D_MODEL), MIX_WIDTH ** -0.5),
        "g_ffn": 1.0 + nrm(ks[22], (DEPTH, D_MODEL), 0.02),
        "w_gate_up": nrm(ks[23], (DEPTH, D_MODEL, 2 * D_FF), D_MODEL ** -0.5),
        "w_down": nrm(ks[24], (DEPTH, D_FF, D_MODEL), D_FF ** -0.5),
    }


def reference(x_prompt, x_sample, state_ssd, state_conv, cache_k, cache_v, c_prompt, c_sample,
              w_ada, b_ada, g_mix, w_in, conv_w, conv_b, dt_bias, a_log, d_skip, ssd_norm_w,
              q_norm_w, k_norm_w, sinks, w_out, g_ffn, w_gate_up, w_down):
    yp, ys = x_prompt, x_sample
    bp = x_prompt.shape[0]
    ssd_p, conv_p, k_p, v_p = [], [], [], []
    ssd_s, conv_s, k_s, v_s = [], [], [], []
    for l in range(DEPTH):
        lp = dict(w_ada=w_ada[l], b_ada=b_ada[l], g_mix=g_mix[l], w_in=w_in[l], conv_w=conv_w[l],
                  conv_b=conv_b[l], dt_bias=dt_bias[l], a_log=a_log[l], d_skip=d_skip[l],
                  ssd_norm_w=ssd_norm_w[l], q_norm_w=q_norm_w[l], k_norm_w=k_norm_w[l],
                  sinks=sinks[l], w_out=w_out[l], g_ffn=g_ffn[l], w_gate_up=w_gate_up[l],
                  w_down=w_down[l])
        zero_ssd = jnp.zeros((bp, SSD_HEADS, SSD_HEAD_DIM, SSD_STATE), x_prompt.dtype)
        zero_conv = jnp.zeros((bp, SSD_CONV - 1, SSD_CONV_CH), x_prompt.dtype)
        yp, s1, s2, s3, s4 = layer(yp, c_prompt, lp, zero_ssd, zero_conv, None)
        ys, t1, t2, t3, t4 = layer(ys, c_sample, lp, state_ssd[l], state_conv[l], (cache_k[l], cache_v[l]))
        ssd_p.append(s1); conv_p.append(s2); k_p.append(s3); v_p.append(s4)
        ssd_s.append(t1); conv_s.append(t2); k_s.append(t3); v_s.append(t4)
    new_ssd_prompt = jnp.stack(ssd_p, axis=0)
    new_conv_prompt = jnp.stack(conv_p, axis=0)
    new_k_prompt = jnp.stack(k_p, axis=0)
    new_v_prompt = jnp.stack(v_p, axis=0)
    new_ssd_sample = jnp.stack(ssd_s, axis=0)
    new_conv_sample = jnp.stack(conv_s, axis=0)
    new_k_sample = jnp.stack(k_s, axis=0)
    new_v_sample = jnp.stack(v_s, axis=0)
    return (yp, ys, new_ssd_prompt, new_conv_prompt, new_k_prompt, new_v_prompt,
            new_ssd_sample, new_conv_sample, new_k_sample, new_v_sample)
```

```python
import os
from contextlib import ExitStack

import numpy as np

import concourse.bass as bass
import concourse.mybir as mybir
from concourse.bass_utils import run_bass_kernel_spmd

F32 = mybir.dt.float32
BF16 = mybir.dt.bfloat16
AF = mybir.ActivationFunctionType
ALU = mybir.AluOpType

D = 4096
SEQ = 2048
DEC = 16
DIN = 7712
DFF = 11008
NMOD = 6
EPS = 1e-6
O_Z, O_X, O_B, O_C, O_DT, O_Q, O_K, O_V = 0, 2048, 4096, 4608, 5120, 5152, 7200, 7456
SLOPES = [float(2.0 ** (-8.0 * (h + 1) / 32.0)) for h in range(32)]

PO = {}
_o = 0
for _n, _w in [("gmix", 32), ("gffn", 32), ("bada", 192), ("cw", 96), ("cb", 24), ("dfeat", 16), ("nw", 16),
               ("wq", 1), ("wk", 1), ("dtb", 32), ("alog", 32), ("sinks", 32), ("wqr", 64), ("wkr", 64)]:
    PO[_n] = (_o, _o + _w)
    _o += _w
NPAR = _o
CO = {"ident": (0, 128), "SL": (128, 192), "U": (192, 256), "bd": (256, 384), "D1": (384, 448), "Dn": (448, 512),
      "ones": (512, 640), "D1m": (640, 704)}
NCONST = 704

ENGS = ("pe", "act", "dve", "pool", "sp")
SEM_LIMIT = 12000


class _Op:
    __slots__ = ("eng", "fn", "reads", "writes", "dma", "idx")

    def __init__(self, eng, fn, reads, writes, dma):
        self.eng = eng
        self.fn = fn
        self.reads = reads
        self.writes = writes
        self.dma = dma


class Prog:
    def __init__(self, n_rr=10):
        self.ops = []
        self.n_rr = n_rr
        self.finals = []
        self.phase = {}

    def add(self, eng, fn, reads=(), writes=(), dma=None):
        reads = list(reads)
        if eng != "pool":
            reads.append("__gb")
        op = _Op(eng, fn, tuple(reads), tuple(writes), dma)
        op.idx = len(self.ops)
        self.ops.append(op)
        return op

    def pe(self, fn, reads=(), writes=()):
        return self.add("pe", fn, reads, writes)

    def act(self, fn, reads=(), writes=()):
        return self.add("act", fn, reads, writes)

    def dve(self, fn, reads=(), writes=()):
        return self.add("dve", fn, reads, writes)

    def dma(self, eng, fn, reads=(), writes=(), group=None, final=False):
        op = self.add(eng, fn, reads, writes, dma=(group if group is not None else "__rr__"))
        if final:
            self.finals.append(op)
        return op

    def region(self, name):
        self.phase[name] = 0

    def fence(self, name, fn):
        op = _Op("dve", fn, (), ("__gb",), None)
        op.idx = len(self.ops)
        self.ops.append(op)

    def build(self, nc, block, st):
        ops = self.ops
        last_w = {}
        readers = {}
        deps = [None] * len(ops)
        for op in ops:
            d = set()
            for r in op.reads:
                w = last_w.get(r)
                if w is not None:
                    d.add(w)
            for w_ in op.writes:
                w = last_w.get(w_)
                if w is not None:
                    d.add(w)
                rl = readers.get(w_)
                if rl:
                    d.update(rl)
            d.discard(op.idx)
            latest = {}
            d2 = set()
            for j in d:
                p = ops[j]
                if p.dma is None:
                    if latest.get(p.eng, -1) < j:
                        latest[p.eng] = j
                else:
                    d2.add(j)
            d2.update(latest.values())
            d = d2
            dd = set()
            for j in d:
                p = ops[j]
                if p.eng == op.eng and p.dma is None and op.dma is None:
                    if op.eng == "pe":
                        continue
                    jr = -1
                    for r in op.reads:
                        w = last_w.get(r)
                        if w is not None and w != op.idx and ops[w].eng == op.eng and ops[w].dma is None and w > jr:
                            jr = w
                    if jr >= 0:
                        dd.add(jr)
                    continue
                dd.add(j)
            deps[op.idx] = dd
            for r in op.reads:
                readers.setdefault(r, []).append(op.idx)
            for w_ in op.writes:
                last_w[w_] = op.idx
                readers[w_] = []
        needs_sig = [False] * len(ops)
        for op in ops:
            for j in deps[op.idx]:
                needs_sig[j] = True
        for op in self.finals:
            needs_sig[op.idx] = True

        def newsem(name):
            return st.enter_context(nc.semaphore(name))

        eng_sem = {e: newsem("pg_%s_0" % e) for e in ENGS}
        eng_gen = {e: 0 for e in ENGS}
        eng_cnt = {e: 0 for e in ENGS}
        dma_sems = {}
        dma_cnt = {}
        rr_names = ["rr%d" % i for i in range(self.n_rr)]
        rr_last = {n: None for n in rr_names}
        rr_i = 0
        token = [None] * len(ops)
        extra = [None] * len(ops)
        semkey = {}

        def key(s):
            k = id(s)
            semkey[k] = s
            return k

        for op in ops:
            if op.dma is not None:
                g = op.dma
                if g == "__rr__":
                    g = rr_names[rr_i % len(rr_names)]
                    rr_i += 1
                    if rr_last[g] is not None:
                        extra[op.idx] = rr_last[g]
                if g not in dma_sems:
                    dma_sems[g] = newsem("pgd_" + g)
                    dma_cnt[g] = 0
                dma_cnt[g] += 16
                token[op.idx] = (key(dma_sems[g]), dma_cnt[g])
                if g in rr_last:
                    rr_last[g] = token[op.idx]
            elif needs_sig[op.idx]:
                if eng_cnt[op.eng] >= SEM_LIMIT:
                    eng_gen[op.eng] += 1
                    eng_sem[op.eng] = newsem("pg_%s_%d" % (op.eng, eng_gen[op.eng]))
                    eng_cnt[op.eng] = 0
                eng_cnt[op.eng] += 1
                token[op.idx] = (key(eng_sem[op.eng]), eng_cnt[op.eng])
        per_eng = {e: [] for e in ENGS}
        for op in ops:
            per_eng[op.eng].append(op)
        stats = {"waits": 0, "ops": len(ops)}
        stats["per_eng"] = {e: len(per_eng[e]) for e in ENGS}
        stats["gens"] = dict(eng_gen)
        stats["cnt"] = dict(eng_cnt)
        stats["dma"] = dict(dma_cnt)

        def emit_engine(ename, eng):
            waited = {}
            for op in per_eng[ename]:
                need = {}
                for j in deps[op.idx]:
                    k, v = token[j]
                    if waited.get(k, 0) >= v:
                        continue
                    if need.get(k, 0) < v:
                        need[k] = v
                if extra[op.idx] is not None:
                    k, v = extra[op.idx]
                    if waited.get(k, 0) < v and need.get(k, 0) < v:
                        need[k] = v
                for k, v in need.items():
                    eng.wait_ge(semkey[k], v)
                    waited[k] = v
                    stats["waits"] += 1
                ins = op.fn(eng)
                tk = token[op.idx]
                if tk is not None:
                    ins.then_inc(semkey[tk[0]], 16 if op.dma is not None else 1)
            if ename == "sp":
                for op in self.finals:
                    k, v = token[op.idx]
                    eng.wait_ge(semkey[k], v)

        @block.tensor
        def _(e):
            emit_engine("pe", e)

        @block.scalar
        def _(e):
            emit_engine("act", e)

        @block.vector
        def _(e):
            emit_engine("dve", e)

        @block.gpsimd
        def _(e):
            emit_engine("pool", e)

        @block.sync
        def _(e):
            emit_engine("sp", e)

        self.stats = stats


class _Stop(Exception):
    pass


KSTOP = float(os.environ.get("KSTOP", "99"))


def _stop(stage):
    if KSTOP <= stage:
        raise _Stop()


def build_nc(run_sample=True, n_prompt_tiles=4):
    nc = bass.Bass("TRN2", target_bir_lowering=False)
    din = lambda n, s: nc.dram_tensor(n, s, F32, kind="ExternalInput").ap()
    dout = lambda n, s: nc.dram_tensor(n, s, F32, kind="ExternalOutput").ap()
    xp_d = din("xp", [SEQ, D])
    xs_d = din("xs", [DEC, D])
    sst_d = din("sst", [128, 2048])
    scv_d = din("scv", [128, 72])
    ckT_d = din("ckT", [128, 512])
    cv_d = din("cv", [128, 256])
    c2T_d = din("c2T", [128, 64])
    par_d = din("par", [128, NPAR])
    cst_d = din("cst", [128, NCONST])
    wada_d = din("w_ada", [D, NMOD * D])
    win_d = din("w_in", [D, DIN])
    wout_d = din("w_out", [D, D])
    wgu_d = din("w_gu", [D, 2 * DFF])
    wdn_d = din("w_down", [DFF, D])
    yp_d = dout("yp", [SEQ, D])
    ys_d = dout("ys", [DEC, D])
    ossd_d = [dout("ossd_p", [128, 2048]), dout("ossd_s", [128, 2048])]
    ocv_d = [dout("ocv_p", [128, 72]), dout("ocv_s", [128, 72])]
    ok_d = [dout("ok_p", [64, 512]), dout("ok_s", [64, 512])]
    ov_d = [dout("ov_p", [128, 256]), dout("ov_s", [128, 256])]
    grow_d = nc.dram_tensor("grow", [2, 2, D], F32, kind="Internal").ap()
    NSLAB = 182
    _wsc = [nc.dram_tensor("wsc%d" % i_, [91, 128, 8192], BF16, kind="Internal").ap() for i_ in range(2)]

    class _WSC:
        def __getitem__(self, key):
            sid = key[0]
            return _wsc[sid // 91][sid % 91, :, :]
    wsc_d = _WSC()

    st = ExitStack()
    with st:
        sb = lambda n, s, dt: st.enter_context(nc.sbuf_tensor("sb_" + n, s, dt))
        par = sb("par", [128, NPAR], F32)
        cst = sb("cst", [128, NCONST], F32)
        R = sb("R", [128, 16384], F32)
        H = sb("H", [128, 32, 512], BF16)
        M = sb("M", [128, 16384], BF16)
        W = [sb("W0", [128, 8192], BF16), sb("W1", [128, 8192], BF16)]
        modT = sb("modT", [128, 192, 2], F32)
        s1T = sb("s1T", [128, 32, 2], F32)
        s2T = sb("s2T", [128, 32, 2], F32)
        scT = sb("scT", [128, 32, 2], BF16)
        c2T = sb("c2Ts", [128, 32, 2], F32)
        onesb = sb("onesb", [128, 128], BF16)
        epsc = sb("epsc", [128, 1], F32)
        negM = sb("negM", [128, 1], F32)
        Ebc = sb("Ebc", [128, 32], F32)
        aneg = sb("aneg", [128, 32], F32)
        wq8 = sb("wq8", [128, 1], F32)
        small = sb("small", [128, 16], F32)
        kcar = sb("kcar", [128, 4, 128], BF16)
        vcar = sb("vcar", [128, 2, 256], BF16)
        halo = sb("halo", [128, 24, 3], F32)
        stT = sb("stT", [128, 2048], F32)
        kf32 = sb("kf32", [128, 4, 128], F32)
        vout = sb("vout", [128, 256], F32)
        ssq = sb("ssq", [128, 8], F32)
        rstd = sb("rstd", [128, 8], F32)
        gp = [sb("gp0", [128, 512], F32), sb("gp1", [128, 512], F32)]
        sg = [sb("sg0", [128, 512], F32), sb("sg1", [128, 512], F32)]
        modrow = [sb("modrow0", [2, 512], F32), sb("modrow1", [2, 512], F32)]
        brow = [sb("brow0", [2, 512], F32), sb("brow1", [2, 512], F32)]
        ps = [st.enter_context(nc.psum_tensor("ps%d" % i, [128, 512], F32)) for i in range(8)]
        dt_x = sb("dt_x", [64, 256], F32)
        dt_e = sb("dt_e", [64, 256], F32)
        dt_t = sb("dt_t", [64, 256], F32)
        dt_a = sb("dt_a", [64, 256], F32)
        dt_s = sb("dt_s", [64, 256], F32)
        badarow_d = din("badarow", [1, NMOD * D])

        P = Prog()
        for rg in ("R", "H", "M"):
            P.region(rg)

        def cc(name):
            a, b = CO[name]
            return cst[:, a:b]

        def pc(name):
            a, b = PO[name]
            return par[:, a:b]

        ident = cc("ident")
        ones = cc("ones")
        SL = cc("SL")
        U = cc("U")
        bd = cc("bd")
        D1 = cc("D1")
        Dn = cc("Dn")
        D1m = cc("D1m")

        def mm(out, lhsT, rhs, start, stop, reads, writes):
            P.pe(lambda e: e.matmul(out, lhsT=lhsT, rhs=rhs, start=start, stop=stop), reads, writes)

        def tr(out, in_, idn, reads, writes):
            P.pe(lambda e: e.transpose(out, in_, idn), reads, writes)

        def actf(out, in_, func, reads, writes, bias=None, scale=None, accum=None):
            kw = {}
            if bias is not None:
                kw["bias"] = bias
            if scale is not None:
                kw["scale"] = scale
            if accum is not None:
                kw["accum_out"] = accum
            P.act(lambda e: e.activation(out=out, in_=in_, func=func, **kw), reads, writes)

        def tt(out, in0, in1, op, reads, writes):
            P.dve(lambda e: e.tensor_tensor(out=out, in0=in0, in1=in1, op=op), reads, writes)

        def ts(out, in0, s1, s2, op0, op1, reads, writes):
            if op1 is None:
                P.dve(lambda e: e.tensor_scalar(out=out, in0=in0, scalar1=s1, scalar2=None, op0=op0), reads, writes)
            else:
                P.dve(lambda e: e.tensor_scalar(out=out, in0=in0, scalar1=s1, scalar2=s2, op0=op0, op1=op1),
                      reads, writes)

        def stt(out, in0, scalar, in1, op0, op1, reads, writes):
            P.dve(lambda e: e.scalar_tensor_tensor(out=out, in0=in0, scalar=scalar, in1=in1, op0=op0, op1=op1),
                  reads, writes)

        def cpy(out, in_, reads, writes, eng="dve"):
            if eng == "dve":
                P.dve(lambda e: e.tensor_copy(out=out, in_=in_), reads, writes)
            else:
                P.act(lambda e: e.copy(out=out, in_=in_), reads, writes)

        def recip(out, in_, reads, writes):
            P.dve(lambda e: e.reciprocal(out=out, in_=in_), reads, writes)

        def mset(ap, val, writes):
            P.dve(lambda e: e.memset(ap, val), (), writes)

        def dma(eng, out, in_, reads, writes, group=None, final=False, slow=False):
            if slow:
                P.dma(eng, lambda e: e.dma_start(out=out, in_=in_, allow_slow_non_contiguous=True), reads, writes,
                      group, final)
            else:
                P.dma(eng, lambda e: e.dma_start(out=out, in_=in_), reads, writes, group, final)

        wstate = {"n": 0, "sid": None, "pass0": True, "nslab": None}

        def wload(pieces):
            slot = wstate["n"] % 2
            wstate["n"] += 1
            sid = wstate["sid"]
            if sid is not None:
                wstate["sid"] += 1
            if sid is None or wstate["pass0"]:
                keys = []
                for i, (dstf, src) in enumerate(pieces):
                    k = ("w", slot, i)
                    keys.append(k)
                    dma("pool", dstf(W[slot]), src, (), [k], group="w%d" % slot)
                if sid is not None:
                    dma("sp", wsc_d[sid, :, :], W[slot][:, :], keys, [("wsc", sid)], group="ws%d" % slot)
                return W[slot], keys
            k = ("w", slot, 0)
            dma("pool", W[slot][:, :], wsc_d[sid, :, :], [("wsc", sid)], [k], group="w%d" % slot)
            return W[slot], [k]

        def wcols(wd, c0, n, kc0=0, nkc=32):
            return wd[kc0 * 128:(kc0 + nkc) * 128, c0:c0 + n].rearrange("(kc p) m -> p kc m", p=128)

        def wview(Wt, nkc, n):
            return Wt[:, 0:nkc * n].rearrange("p (kc m) -> p kc m", m=n)

        dma("sp", par[:], par_d[:, :], (), ["par"])
        dma("sp", cst[:], cst_d[:, :], (), ["cst"])
        dma("sp", c2T[:].rearrange("p a b -> p (a b)"), c2T_d[:, :], (), ["c2T"])
        mset(epsc[:], EPS, ["epsc"])
        cpy(onesb[:], ones, ["cst"], ["onesb"])
        actf(scT[:].rearrange("p a b -> p (a b)"), c2T[:].rearrange("p a b -> p (a b)"), AF.Silu, ["c2T"], ["scT"])
        P.act(lambda e: e.mul(out=wq8[:], in_=pc("wq"), mul=0.125), ["par"], ["wq8"])
        actf(aneg[:], pc("alog"), AF.Exp, ["par"], ["aneg0"])
        ts(aneg[:], aneg[:], -1.0, None, ALU.mult, None, ["aneg0"], ["aneg"])
        P.dve(lambda e: e.tensor_reduce(out=small[:, 2:3], in_=pc("wqr"), axis=mybir.AxisListType.X, op=ALU.max,
                                        apply_absolute_value=True), ["par"], ["sm2"])
        P.dve(lambda e: e.tensor_reduce(out=small[:, 3:4], in_=pc("wkr"), axis=mybir.AxisListType.X, op=ALU.max,
                                        apply_absolute_value=True), ["par"], ["sm3"])
        tt(small[:, 4:5], small[:, 2:3], small[:, 3:4], ALU.mult, ["sm2", "sm3"], ["sm4"])
        ts(negM[:], small[:, 4:5], -8.0, None, ALU.mult, None, ["sm4"], ["negM"])
        actf(Ebc[:], pc("sinks"), AF.Exp, ["par", "negM"], ["Ebc"], bias=negM[:, 0:1], scale=1.0)

        for cg in range(48):
            bank = ps[cg % 2]
            bk = ("ps", cg % 2)
            for hf in range(2):
                Wt, wk_ = wload([(lambda w: wview(w, 16, 512), wcols(wada_d, cg * 512, 512, hf * 16, 16))])
                wv = wview(Wt, 16, 512)
                for kl in range(16):
                    kc = hf * 16 + kl
                    mm(bank[0:2, :], scT[:, kc, :], wv[:, kl, :], kc == 0, kc == 31, wk_ + ["scT"], [bk])
            i = cg % 2
            cpy(modrow[i][:], bank[0:2, :], [bk], [("modrow", i)], eng="act")
            which = cg // 8
            if which in (2, 5):
                gi = 0 if which == 2 else 1
                c0 = (cg % 8) * 512
                dma("sp", brow[i][:], badarow_d[0:1, cg * 512:(cg + 1) * 512].to_broadcast([2, 512]), (),
                    [("brow", i)])
                tt(modrow[i][:], modrow[i][:], brow[i][:], ALU.add, [("modrow", i), ("brow", i)], [("modrow2", i)])
                dma("sp", grow_d[:, gi, c0:c0 + 512], modrow[i][:], [("modrow2", i)], [("grow", gi, cg % 8)])
            else:
                tb = ps[2 + cg % 2]
                tbk = ("ps", 2 + cg % 2)
                for j in range(4):
                    tr(tb[:, j * 2:j * 2 + 2], modrow[i][:, j * 128:(j + 1) * 128], ident[0:2, 0:2],
                       [("modrow", i), "cst"], [tbk])
                a0 = PO["bada"][0] + cg * 4
                tt(modT[:, cg * 4:cg * 4 + 4, :], tb[:, 0:8].rearrange("p (j r) -> p j r", r=2),
                   par[:, a0:a0 + 4].unsqueeze(2).to_broadcast([128, 4, 2]), ALU.add, [tbk, "par"], [("modT", cg)])
        MT_ALL = [("modT", cg) for cg in range(48) if cg // 8 not in (2, 5)]
        stt(s1T[:], modT[:, 32:64, :], 1.0, pc("gmix").unsqueeze(2).to_broadcast([128, 32, 2]), ALU.add, ALU.mult,
            MT_ALL + ["par"], ["s1T"])
        stt(s2T[:], modT[:, 128:160, :], 1.0, pc("gffn").unsqueeze(2).to_broadcast([128, 32, 2]), ALU.add, ALU.mult,
            MT_ALL + ["par"], ["s2T"])
        GROW = [("grow", gi, j) for gi in range(2) for j in range(8)]

        def run_tile(r, t0, T, L, first, last):
            NCH = T // L
            NTC = (T + 127) // 128
            wstate["sid"] = 0
            x_src = xp_d if r == 0 else xs_d
            y_dst = yp_d if r == 0 else ys_d
            Rx = R[:, :].rearrange("p (tc f) -> p tc f", f=4096)
            rows_of = lambda tc: min(128, T - tc * 128)

            P.fence("R", lambda e: e.memset(small[:, 8:9], 0.0))
            Mj = M[:, 0:4096]
            for tc in range(NTC):
                rows = rows_of(tc)
                dma("sp", Rx[0:rows, tc, :], x_src[t0 + tc * 128:t0 + tc * 128 + rows, :], (), [("R", "x", tc)])
                actf(Mj[0:rows, :], Rx[0:rows, tc, :], AF.Square, [("R", "x", tc)], [("M", "junk"), ("ssq", tc)],
                     accum=ssq[0:rows, tc:tc + 1])
                actf(rstd[0:rows, tc:tc + 1], ssq[0:rows, tc:tc + 1], AF.Sqrt, [("ssq", tc), "epsc"], [("rs0", tc)],
                     bias=epsc[0:rows, 0:1], scale=1.0 / D)
                recip(rstd[0:rows, tc:tc + 1], rstd[0:rows, tc:tc + 1], [("rs0", tc)], [("rstd", tc)])
                ts(Rx[0:rows, tc, :], Rx[0:rows, tc, :], rstd[0:rows, tc:tc + 1], None, ALU.mult, None,
                   [("R", "x", tc), ("rstd", tc)], [("R", "xn", tc)])
            for fc in range(32):
                b = fc % 2
                bank = ps[b]
                bk = ("ps", b)
                for tc in range(NTC):
                    rows = rows_of(tc)
                    tr(bank[:, tc * 128:tc * 128 + rows], Rx[0:rows, tc, fc * 128:(fc + 1) * 128],
                       ident[0:rows, 0:rows], [("R", "xn", tc), "cst"], [bk])
                if fc % 2 == 0:
                    actf(H[:, fc, 0:T], bank[:, 0:T], AF.Identity, [bk, "s1T"] + MT_ALL, [("H", "h", fc)],
                         bias=modT[:, 0 + fc, r:r + 1], scale=s1T[:, fc, r:r + 1])
                else:
                    ts(H[:, fc, 0:T], bank[:, 0:T], s1T[:, fc, r:r + 1], modT[:, 0 + fc, r:r + 1], ALU.mult, ALU.add,
                       [bk, "s1T"] + MT_ALL, [("H", "h", fc)])
            HALL = [("H", "h", fc) for fc in range(32)]
            _stop(2)

            P.fence("R", lambda e: e.memset(small[:, 8:9], 0.0))
            Rb = R[:, :].bitcast(BF16)
            kTz = Rb[:, 0:5120].rearrange("p (h e t) -> p h e t", e=2, t=640)
            vv = Rb[:, 5120:7680].rearrange("p (s c) -> p s c", c=256)
            qT = Rb[:, 7680:11776].rearrange("p (b c t) -> p b c t", b=2, t=512)
            P1 = [Rb[:, 11776:12288], Rb[:, 12288:12800]]
            P2 = [Rb[:, 12800:13312], Rb[:, 13312:13824]]
            fo = 6912
            sq = [R[:, fo:fo + 512], R[:, fo + 512:fo + 1024]]
            rs = [R[:, fo + 1024:fo + 1536], R[:, fo + 1536:fo + 2048]]
            sb1 = [R[:, fo + 2048:fo + 2560], R[:, fo + 2560:fo + 3072]]
            sb2 = [R[:, fo + 3072:fo + 3584], R[:, fo + 3584:fo + 4096]]
            rec = R[:, fo + 4096:fo + 4608]
            ktmp = R[:, fo + 4608:fo + 5120]
            vtmp = R[:, fo + 5120:fo + 5376]
            mixT = M[:, :].rearrange("p (c t) -> p c t", t=512)

            def qknorm(bank, bk, i, wcol, wkeys):
                actf(sq[i][:, 0:T], bank[:, 0:T], AF.Square, [bk], [("R", "sq", i)])
                mm(ps[3][:, 0:T], bd, sq[i][:, 0:T], True, True, [("R", "sq", i), "cst"], [("ps", 3)])
                actf(rs[i][:, 0:T], ps[3][:, 0:T], AF.Sqrt, [("ps", 3), "epsc"], [("R", "rs0", i)],
                     bias=epsc[:, 0:1], scale=1.0)
                recip(rs[i][:, 0:T], rs[i][:, 0:T], [("R", "rs0", i)], [("R", "rs", i)])

            P.dve(lambda e: e.memset(Rb[:, 0:5120], 0.0), (), [("R", "kTz0")])
            if r == 1:
                dma("sp", ktmp[:, 0:512], ckT_d[:, :], (), [("R", "ktmp")])
                for e_ in range(2):
                    pr_ = slice(e_ * 64, (e_ + 1) * 64)
                    cpy(kTz[pr_, :, e_, 0:128], ktmp[pr_, 0:512].rearrange("p (h t) -> p h t", t=128),
                        [("R", "ktmp"), ("R", "kTz0")], [("R", "kTc")])
                dma("sp", vtmp[:, 0:256], cv_d[:, :], (), [("R", "vtmp")])
                cpy(vv[:, 0, :], vtmp[:, 0:256], [("R", "vtmp")], [("R", "vv", 0)])
            elif not first:
                for e_ in range(2):
                    pr_ = slice(e_ * 64, (e_ + 1) * 64)
                    cpy(kTz[pr_, :, e_, 0:128], kcar[pr_, :, :], ["kcar", ("R", "kTz0")], [("R", "kTc")])
                cpy(vv[:, 0, :], vcar[:, 0, :], ["vcar"], [("R", "vv", 0)])
                cpy(vv[0:64, 1, :], vcar[0:64, 1, :], ["vcar"], [("R", "vv", 1, "lo")])
            have_prev = (r == 1) or (not first)

            for kh in range(4):
                if kh % 2 == 0:
                    pcs = []
                    for hh in range(2):
                        for u_ in range(2):
                            pcs.append((lambda w, a=hh * 128 + u_ * 64: wview(w, 32, 256)[:, :, a:a + 64],
                                        wcols(win_d, O_K + (kh + hh) * 64, 64)))
                    Wt, wk_ = wload(pcs)
                    wv = wview(Wt, 32, 256)
                b = kh % 3
                bank, bk = ps[b], ("ps", b)
                for kc in range(32):
                    mm(bank[:, 0:T], wv[:, kc, (kh % 2) * 128:(kh % 2 + 1) * 128], H[:, kc, 0:T], kc == 0, kc == 31,
                       wk_ + [("H", "h", kc)], [bk])
                i = kh % 2
                qknorm(bank, bk, i, None, None)
                stt(ktmp[:, 0:T], bank[:, 0:T], pc("wk"), rs[i][:, 0:T], ALU.mult, ALU.mult,
                    [bk, ("R", "rs", i), "par"], [("R", "ktmp")])
                for e_ in range(2):
                    pr_ = slice(e_ * 64, (e_ + 1) * 64)
                    cpy(kTz[pr_, kh, e_, 128:128 + T], ktmp[pr_, 0:T], [("R", "ktmp"), ("R", "kTz0")],
                        [("R", "kT", kh)], eng="act")
                if last:
                    nk = min(128, T)
                    cpy(kf32[:, kh, 0:nk], ktmp[:, T - nk:T], [("R", "ktmp")], [("kf32", kh)], eng="act")
            Wt, wk_ = wload([(lambda w: wview(w, 32, 256), wcols(win_d, O_V, 256))])
            wv = wview(Wt, 32, 256)
            vb = 0
            for j in range(NCH):
                mrows = 2 * L if j < NCH - 1 else L
                b = vb % 3
                vb += 1
                bank, bk = ps[b], ("ps", b)
                for kc in range(32):
                    mm(bank[0:mrows, 0:256], H[:, kc, j * L:j * L + mrows], wv[:, kc, :], kc == 0, kc == 31,
                       wk_ + [("H", "h", kc)], [bk])
                cpy(vv[0:mrows, 2 + j, :], bank[0:mrows, 0:256], [bk], [("R", "vv", 2 + j)], eng="act")
                if last and ((r == 0 and j == NCH - 2) or (r == 1 and j == 0)):
                    cpy(vout[0:mrows, :], bank[0:mrows, 0:256], [bk], ["vout"], eng="act")
            if r == 0:
                if first:
                    P.dve(lambda e: e.memset(vv[0:64, 1, :], 0.0), (), [("R", "vv", 1, "lo")])
                b = vb % 3
                bank, bk = ps[b], ("ps", b)
                for kc in range(32):
                    mm(bank[64:128, 0:256], H[:, kc, 0:64], wv[:, kc, :], kc == 0, kc == 31,
                       wk_ + [("H", "h", kc)], [bk])
                cpy(vv[64:128, 1, :], bank[64:128, 0:256], [bk], [("R", "vv", 1, "hi")], eng="act")
            Wt, wk_ = wload([(lambda w: wview(w, 32, 32), wcols(win_d, O_DT, 32))])
            wv = wview(Wt, 32, 32)
            dtb = ps[7]
            for j in range(NCH):
                for kc in range(32):
                    mm(dtb[0:L, j * 32:(j + 1) * 32], H[:, kc, j * L:(j + 1) * L], wv[:, kc, :], kc == 0, kc == 31,
                       wk_ + [("H", "h", kc)], [("ps", 7)])
            n32 = NCH * 32
            dtv = lambda t_: t_[0:L, 0:n32].rearrange("p (c h) -> p c h", h=32)
            tt(dtv(dt_x), dtv(dtb), pc("dtb")[0:L, :].unsqueeze(1).to_broadcast([L, NCH, 32]), ALU.add,
               [("ps", 7), "par"], ["dt_x"])
            ts(dt_e[0:L, 0:n32], dt_x[0:L, 0:n32], 30.0, None, ALU.min, None, ["dt_x"], ["dt_e0"])
            actf(dt_e[0:L, 0:n32], dt_e[0:L, 0:n32], AF.Exp, ["dt_e0"], ["dt_e1"])
            actf(dt_e[0:L, 0:n32], dt_e[0:L, 0:n32], AF.Ln, ["dt_e1"], ["dt_e2"], bias=1.0)
            tt(dt_t[0:L, 0:n32], dt_e[0:L, 0:n32], dt_x[0:L, 0:n32], ALU.max, ["dt_e2", "dt_x"], ["dt"])
            tt(dtv(dt_a), dtv(dt_t), aneg[0:L, :].unsqueeze(1).to_broadcast([L, NCH, 32]), ALU.mult,
               ["dt", "aneg"], ["dt_a"])
            mm(ps[7][0:L, 256:256 + n32], SL[0:L, 0:L], dt_a[0:L, 0:n32], True, True, ["dt_a", "cst"], [("ps", 7, "b")])
            actf(dt_s[0:L, 0:n32], ps[7][0:L, 256:256 + n32], AF.Exp, [("ps", 7, "b")], ["dte"])
            tt(dt_s[0:L, 0:n32], dt_s[0:L, 0:n32], dt_t[0:L, 0:n32], ALU.mult, ["dte", "dt"], ["dt_s"])

            _stop(3)

            def q_group(kvh, qb):
                for s2 in range(2):
                    c0 = O_Q + kvh * 512 + s2 * 256
                    Wt_, wk2 = wload([(lambda w: wview(w, 32, 256), wcols(win_d, c0, 256))])
                    wv_ = wview(Wt_, 32, 256)
                    for u in range(2):
                        ch = s2 * 2 + u
                        b = ch % 3
                        bank, bk = ps[b], ("ps", b)
                        for kc in range(32):
                            mm(bank[:, 0:T], wv_[:, kc, u * 128:(u + 1) * 128], H[:, kc, 0:T], kc == 0, kc == 31,
                               wk2 + [("H", "h", kc)], [bk])
                        i = ch % 2
                        qknorm(bank, bk, i, None, None)
                        stt(qT[:, qb, ch, 0:T], bank[:, 0:T], wq8[:, 0:1], rs[i][:, 0:T], ALU.mult, ALU.mult,
                            [bk, ("R", "rs", i), "wq8"], [("R", "qT", qb, ch)])

            KT_ALL = [("R", "kT", kh) for kh in range(4)] + [("R", "kTc"), ("R", "kTz0")]
            VV_ALL = [("R", "vv", s) for s in range(10)] + [("R", "vv", 1, "lo"), ("R", "vv", 1, "hi")]

            def attn_qk(kvh, qb, c, i):
                v2 = have_prev or c >= 2
                v1 = have_prev or c >= 1
                p0 = 0
                D1u = D1 if v2 else D1m
                qk = [("R", "qT", qb, ch) for ch in range(4)]
                for h in range(8):
                    e_ = h % 2
                    qs = qT[:, qb, h // 2, c * L:(c + 1) * L]
                    if v1:
                        kbase = c * L if r == 0 else 0
                        mm(ps[4][p0:128, h * L:(h + 1) * L], kTz[:, kvh, e_, kbase + p0:kbase + 128], qs,
                           True, True, KT_ALL + qk, [("ps", 4)])
                    mm(ps[5][0:L, h * L:(h + 1) * L], kTz[:, kvh, e_, 128 + c * L:128 + (c + 1) * L], qs,
                       True, True, KT_ALL + qk, [("ps", 5)])
                for h in range(8):
                    sl = SLOPES[kvh * 8 + h]
                    if v1:
                        stt(sb1[i][p0:128, h * L:(h + 1) * L], D1u[p0:128, 0:L], sl, ps[4][p0:128, h * L:(h + 1) * L],
                            ALU.mult, ALU.add, [("ps", 4), "cst"], [("R", "sb1", i)])
                    stt(sb2[i][0:L, h * L:(h + 1) * L], Dn[0:L, 0:L], sl, ps[5][0:L, h * L:(h + 1) * L],
                        ALU.mult, ALU.add, [("ps", 5), "cst"], [("R", "sb2", i)])
                if v1:
                    actf(P1[i][p0:128, 0:8 * L], sb1[i][p0:128, 0:8 * L], AF.Exp, [("R", "sb1", i), "negM"],
                         [("R", "P1", i)], bias=negM[p0:128, 0:1], scale=1.0)
                actf(P2[i][0:L, 0:8 * L], sb2[i][0:L, 0:8 * L], AF.Exp, [("R", "sb2", i), "negM"], [("R", "P2", i)],
                     bias=negM[0:L, 0:1], scale=1.0)

            def attn_pv(kvh, c, i):
                v2 = have_prev or c >= 2
                v1 = have_prev or c >= 1
                p0 = 0
                for h in range(8):
                    e_ = h % 2
                    o = ps[7][e_ * 64:(e_ + 1) * 64, (h // 2) * L:(h // 2 + 1) * L]
                    if v1:
                        mm(o, vv[p0:128, c, kvh * 64:(kvh + 1) * 64], P1[i][p0:128, h * L:(h + 1) * L], True, False,
                           VV_ALL + [("R", "P1", i)], [("ps", 7)])
                    mm(o, vv[0:L, c + 2, kvh * 64:(kvh + 1) * 64], P2[i][0:L, h * L:(h + 1) * L], not v1, True,
                       VV_ALL + [("R", "P2", i)], [("ps", 7)])
                if v1:
                    mm(ps[6][:, 0:8 * L], onesb[p0:128, :], P1[i][p0:128, 0:8 * L], True, False,
                       [("R", "P1", i), "onesb"], [("ps", 6)])
                mm(ps[6][:, 0:8 * L], onesb[0:L, :], P2[i][0:L, 0:8 * L], not v1, True, [("R", "P2", i), "onesb"],
                   [("ps", 6)])
                tt(rec[:, 0:8 * L].rearrange("p (h t) -> p h t", t=L), ps[6][:, 0:8 * L].rearrange("p (h t) -> p h t", t=L),
                   Ebc[:, kvh * 8:(kvh + 1) * 8].unsqueeze(2).to_broadcast([128, 8, L]), ALU.add, [("ps", 6), "Ebc"],
                   [("R", "rec0")])
                recip(rec[:, 0:8 * L], rec[:, 0:8 * L], [("R", "rec0")], [("R", "rec")])
                for e_ in range(2):
                    pr = slice(e_ * 64, (e_ + 1) * 64)
                    tt(mixT[pr, 16 + kvh * 4:16 + kvh * 4 + 4, c * L:(c + 1) * L],
                       ps[7][pr, 0:4 * L].rearrange("p (a t) -> p a t", t=L),
                       rec[pr, 0:8 * L].rearrange("p (a two t) -> p a two t", two=2, t=L)[:, :, e_, :], ALU.mult,
                       [("ps", 7), ("R", "rec")], [("M", "mix", 16 + kvh * 4, c, e_)])

            q_group(0, 0)
            _stop(3.2)
            for kvh in range(4):
                if kvh + 1 < 4:
                    q_group(kvh + 1, (kvh + 1) % 2)
                qb = kvh % 2
                attn_qk(kvh, qb, 0, 0)
                _stop(3.4)
                for c in range(NCH):
                    if c + 1 < NCH:
                        attn_qk(kvh, qb, c + 1, (c + 1) % 2)
                        _stop(3.5)
                    attn_pv(kvh, c, c % 2)
                    _stop(3.6)
            if r == 0 and not last:
                for e_ in range(2):
                    pr_ = slice(e_ * 64, (e_ + 1) * 64)
                    cpy(kcar[pr_, :, :], kTz[pr_, :, e_, 512:640], KT_ALL, ["kcar"])
                cpy(vcar[:, 0, :], vv[:, 8, :], VV_ALL, ["vcar"])
                cpy(vcar[0:64, 1, :], vv[0:64, 9, :], VV_ALL, ["vcar"])
            LM = int(os.environ.get("KLASTMASK", "15"))
            if last and (LM & 1):
                dma("sp", ok_d[r][:, :], kf32[0:64, :, :].rearrange("p h t -> p (h t)"),
                    [("kf32", kh) for kh in range(4)], [("ok", r)], final=True)
            if last and (LM & 2):
                dma("sp", ov_d[r][:, :], vout[:], ["vout"], [("ov", r)], final=True)

            _stop(4)
            P.fence("R", lambda e: e.memset(small[:, 8:9], 0.0))
            o_ = 0

            def carve(n, dt=F32):
                nonlocal o_
                a = R[:, o_:o_ + n]
                o_ += n
                return a if dt == F32 else a.bitcast(BF16)
            xraw = [carve(516), carve(516)]
            xsT = carve(2048).rearrange("p (c t) -> p c t", t=512)
            ybuf = carve(2048).rearrange("p (c t) -> p c t", t=512)
            BTf = carve(512)
            CTf = carve(512)
            zs = carve(1024, BF16).rearrange("p (c t) -> p c t", t=512)
            xbar = [carve(256, BF16), carve(256, BF16)]
            xbd = [carve(256, BF16), carve(256, BF16)]
            Btok = [carve(64, BF16), carve(64, BF16)]
            CBm = [carve(64), carve(64)]
            lseg = carve(512)
            LT = carve(512)
            GT = [carve(256, BF16), carve(256, BF16)]
            Ecs2 = [carve(512), carve(512)]
            Cdec = [carve(256, BF16), carve(256, BF16)]
            stbf = carve(256, BF16)
            cacc = carve(512)
            gsq = [carve(512), carve(512)]
            grs = carve(512)

            if first and r == 0:
                mset(stT[:], 0.0, [("st", g) for g in range(4)])
                mset(halo[:], 0.0, [("halo", f) for f in range(24)])
            elif r == 1:
                dma("sp", stT[:], sst_d[:, :], (), [("st", g) for g in range(4)])
                dma("sp", halo[:].rearrange("p a b -> p (a b)"), scv_d[:, :], (), [("halo", f) for f in range(24)])

            cvn = {"n": 0}

            def conv_chunk(bank, bk, fcg, out_ap, okey, out_is_act=True):
                i = cvn["n"] % 2
                cvn["n"] += 1
                xr = xraw[i]
                cpy(xr[:, 3:3 + T], bank[:, 0:T], [bk], [("R", "xraw", i)], eng="act")
                cpy(xr[:, 0:3], halo[:, fcg, :], [("halo", fcg)], [("R", "xrawh", i)])
                cw0 = PO["cw"][0] + fcg * 4
                cb0 = PO["cb"][0] + fcg
                XR = [("R", "xraw", i), ("R", "xrawh", i)]
                actf(cacc[:, 0:T], xr[:, 3:3 + T], AF.Identity, XR + ["par"], [("R", "cacc")],
                     bias=par[:, cb0:cb0 + 1], scale=par[:, cw0 + 3:cw0 + 4])
                for j in range(3):
                    stt(cacc[:, 0:T], xr[:, j:j + T], par[:, cw0 + j:cw0 + j + 1], cacc[:, 0:T], ALU.mult, ALU.add,
                        XR + ["par", ("R", "cacc")], [("R", "cacc")])
                cpy(halo[:, fcg, :], xr[:, T:T + 3], XR, [("halo", fcg)])
                actf(out_ap, cacc[:, 0:T], AF.Silu, [("R", "cacc")], [okey])

            for g in range(4):
                gh = g * 8
                Wt, wk_ = wload([(lambda w: wview(w, 32, 256)[:, :, 0:128], wcols(win_d, O_B + g * 128, 128)),
                                 (lambda w: wview(w, 32, 256)[:, :, 128:256], wcols(win_d, O_C + g * 128, 128))])
                wv = wview(Wt, 32, 256)
                for u, (dst, key_, fcg) in enumerate([(BTf, ("R", "BT"), 16 + g), (CTf, ("R", "CT"), 20 + g)]):
                    bank, bk = ps[u], ("ps", u)
                    for kc in range(32):
                        mm(bank[:, 0:T], wv[:, kc, u * 128:(u + 1) * 128], H[:, kc, 0:T], kc == 0, kc == 31,
                           wk_ + [("H", "h", kc)], [bk])
                    conv_chunk(bank, bk, fcg, dst[:, 0:T], key_)
                for s2 in range(2):
                    Wt, wk_ = wload([(lambda w: wview(w, 32, 256), wcols(win_d, O_X + g * 512 + s2 * 256, 256))])
                    wv = wview(Wt, 32, 256)
                    for u in range(2):
                        ch = s2 * 2 + u
                        b = ch % 3
                        bank, bk = ps[b], ("ps", b)
                        for kc in range(32):
                            mm(bank[:, 0:T], wv[:, kc, u * 128:(u + 1) * 128], H[:, kc, 0:T], kc == 0, kc == 31,
                               wk_ + [("H", "h", kc)], [bk])
                        conv_chunk(bank, bk, g * 4 + ch, xsT[:, ch, 0:T], ("R", "xsT", ch))
                        d0 = PO["dfeat"][0] + g * 4 + ch
                        P.act(lambda e, o=ybuf[:, ch, 0:T], i_=xsT[:, ch, 0:T], m=par[:, d0:d0 + 1]:
                              e.mul(out=o, in_=i_, mul=m), [("R", "xsT", ch), "par"], [("R", "ybuf", ch)])
                for s2 in range(2):
                    Wt, wk_ = wload([(lambda w: wview(w, 32, 256), wcols(win_d, O_Z + g * 512 + s2 * 256, 256))])
                    wv = wview(Wt, 32, 256)
                    for u in range(2):
                        ch = s2 * 2 + u
                        b = ch % 3
                        bank, bk = ps[b], ("ps", b)
                        for kc in range(32):
                            mm(bank[:, 0:T], wv[:, kc, u * 128:(u + 1) * 128], H[:, kc, 0:T], kc == 0, kc == 31,
                               wk_ + [("H", "h", kc)], [bk])
                        actf(zs[:, ch, 0:T], bank[:, 0:T], AF.Silu, [bk], [("R", "zs", ch)])
                XS = [("R", "xsT", ch) for ch in range(4)]
                YB = [("R", "ybuf", ch) for ch in range(4)]
                cpy(stbf[:, 0:512], stT[:, g * 512:(g + 1) * 512], [("st", g)], [("R", "stbf")], eng="act")
                def ssd_pre(c, g=g, gh=gh):
                    i = c % 2
                    cs_ = slice(c * L, (c + 1) * L)
                    dsl = slice(c * 32 + gh, c * 32 + gh + 8)
                    for fc in range(4):
                        tr(ps[3][0:L, fc * 128:(fc + 1) * 128], xsT[:, fc, cs_], ident, XS + ["cst"], [("ps", 3)])
                    tt(xbar[i][0:L, 0:512].rearrange("p (h d) -> p h d", d=64),
                       ps[3][0:L, 0:512].rearrange("p (h d) -> p h d", d=64),
                       dt_t[0:L, dsl].unsqueeze(2).to_broadcast([L, 8, 64]), ALU.mult, [("ps", 3), "dt"],
                       [("R", "xbar", i)])
                    tt(xbd[i][0:L, 0:512].rearrange("p (h d) -> p h d", d=64),
                       ps[3][0:L, 0:512].rearrange("p (h d) -> p h d", d=64),
                       dt_s[0:L, dsl].unsqueeze(2).to_broadcast([L, 8, 64]), ALU.mult, [("ps", 3), "dt_s"],
                       [("R", "xbd", i)])
                    tr(ps[4][0:L, 0:128], BTf[:, cs_], ident, [("R", "BT"), "cst"], [("ps", 4, "a")])
                    cpy(Btok[i][0:L, 0:128], ps[4][0:L, 0:128], [("ps", 4, "a")], [("R", "Btok", i)])
                    mm(ps[4][0:L, 128:128 + L], BTf[:, cs_], CTf[:, cs_], True, True, [("R", "BT"), ("R", "CT")],
                       [("ps", 4, "b")])
                    tt(CBm[i][0:L, 0:L], ps[4][0:L, 128:128 + L], U[0:L, 0:L], ALU.mult, [("ps", 4, "b"), "cst"],
                       [("R", "CBm", i)])
                    tt(lseg[0:L, 0:8 * L].rearrange("p (h s) -> p h s", s=L),
                       SL[0:L, 0:L].unsqueeze(1).to_broadcast([L, 8, L]),
                       dt_a[0:L, dsl].unsqueeze(2).to_broadcast([L, 8, L]), ALU.mult, ["dt_a", "cst"], [("R", "lseg")])
                    for h in range(8):
                        mm(ps[5][0:L, h * L:(h + 1) * L], lseg[0:L, h * L:(h + 1) * L], U[0:L, 0:L], True, True,
                           [("R", "lseg"), "cst"], [("ps", 5)])
                    actf(LT[0:L, 0:8 * L], ps[5][0:L, 0:8 * L], AF.Exp, [("ps", 5)], [("R", "LT")])
                    tt(GT[i][0:L, 0:8 * L].rearrange("p (h l) -> p h l", l=L),
                       LT[0:L, 0:8 * L].rearrange("p (h l) -> p h l", l=L),
                       CBm[i][0:L, 0:L].unsqueeze(1).to_broadcast([L, 8, L]), ALU.mult, [("R", "LT"), ("R", "CBm", i)],
                       [("R", "GT", i)])
                    for h in range(8):
                        mm(ps[6][:, h * L:(h + 1) * L],
                           dt_a[0:L, c * 32 + gh + h:c * 32 + gh + h + 1].to_broadcast([L, 128]), U[0:L, 0:L], True, True,
                           ["dt_a", "cst"], [("ps", 6)])
                    actf(Ecs2[i][:, 0:8 * L], ps[6][:, 0:8 * L], AF.Exp, [("ps", 6)], [("R", "Ecs", i)])
                    tt(Cdec[i][:, 0:8 * L].rearrange("p (h l) -> p h l", l=L),
                       Ecs2[i][:, 0:8 * L].rearrange("p (h l) -> p h l", l=L),
                       CTf[:, cs_].unsqueeze(1).to_broadcast([128, 8, L]), ALU.mult, [("R", "Ecs", i), ("R", "CT")],
                       [("R", "Cdec", i)])

                def ssd_post(c, g=g, gh=gh):
                    i = c % 2
                    cs_ = slice(c * L, (c + 1) * L)
                    dsl = slice(c * 32 + gh, c * 32 + gh + 8)
                    mm(ps[7][:, 0:512], Btok[i][0:L, 0:128], xbd[i][0:L, 0:512], True, True,
                       [("R", "Btok", i), ("R", "xbd", i)], [("ps", 7)])
                    for h in range(8):
                        e_ = h % 2
                        o = ps[4][e_ * 64:(e_ + 1) * 64, 256 + (h // 2) * L:256 + (h // 2 + 1) * L]
                        mm(o, xbar[i][0:L, h * 64:(h + 1) * 64], GT[i][0:L, h * L:(h + 1) * L], True, False,
                           [("R", "xbar", i), ("R", "GT", i)], [("ps", 4, "y")])
                        mm(o, stbf[:, h * 64:(h + 1) * 64], Cdec[i][:, h * L:(h + 1) * L], False, True,
                           [("R", "stbf"), ("R", "Cdec", i)], [("ps", 4, "y")])
                    tt(ybuf[:, :, cs_], ybuf[:, :, cs_], ps[4][:, 256:256 + 4 * L].rearrange("p (a t) -> p a t", t=L),
                       ALU.add, YB + [("ps", 4, "y")], YB)
                    tt(stT[:, g * 512:(g + 1) * 512].rearrange("p (h d) -> p h d", d=64),
                       stT[:, g * 512:(g + 1) * 512].rearrange("p (h d) -> p h d", d=64),
                       Ecs2[i][:, 0:8 * L].rearrange("p (h l) -> p h l", l=L)[:, :, L - 1:L].to_broadcast([128, 8, 64]),
                       ALU.mult, [("st", g), ("R", "Ecs", i)], [("st", g)])
                    tt(stT[:, g * 512:(g + 1) * 512], stT[:, g * 512:(g + 1) * 512], ps[7][:, 0:512], ALU.add,
                       [("st", g), ("ps", 7)], [("st", g)])
                    if c + 1 < NCH:
                        cpy(stbf[:, 0:512], stT[:, g * 512:(g + 1) * 512], [("st", g)], [("R", "stbf")], eng="act")

                ssd_pre(0)
                for c in range(NCH):
                    if c + 1 < NCH:
                        ssd_pre(c + 1)
                    ssd_post(c)
                tt(ybuf[:, :, 0:T], ybuf[:, :, 0:T], zs[:, :, 0:T], ALU.mult, YB + [("R", "zs", ch) for ch in range(4)],
                   YB)
                for fc in range(4):
                    i = fc % 2
                    actf(gsq[i][:, 0:T], ybuf[:, fc, 0:T], AF.Square, YB, [("R", "gsq", i)])
                    mm(ps[3][:, 0:T], ones, gsq[i][:, 0:T], fc == 0, fc == 3, [("R", "gsq", i), "cst"], [("ps", 3)])
                actf(grs[:, 0:T], ps[3][:, 0:T], AF.Sqrt, [("ps", 3), "epsc"], [("R", "grs0")], bias=epsc[:, 0:1],
                     scale=1.0 / 512.0)
                recip(grs[:, 0:T], grs[:, 0:T], [("R", "grs0")], [("R", "grs")])
                for fc in range(4):
                    n0 = PO["nw"][0] + g * 4 + fc
                    stt(mixT[:, g * 4 + fc, 0:T], ybuf[:, fc, 0:T], par[:, n0:n0 + 1], grs[:, 0:T], ALU.mult, ALU.mult,
                        YB + [("R", "grs"), "par"], [("M", "mix", g * 4 + fc)])
            if last and (LM & 4):
                dma("sp", ossd_d[r][:, :], stT[:], [("st", g) for g in range(4)], [("ossd", r)], final=True)
            if last and (LM & 8):
                dma("sp", ocv_d[r][:, :], halo[:].rearrange("p a b -> p (a b)"), [("halo", f) for f in range(24)],
                    [("ocv", r)], final=True)
            MIX_ALL = [("M", "mix", c_) for c_ in range(16)] + \
                      [("M", "mix", 16 + a, c_, e_) for a in range(16) for c_ in range(NCH) for e_ in range(2)]

            _stop(5)
            P.fence("R", lambda e: e.memset(small[:, 8:9], 0.0))
            for tc in range(NTC):
                rows = rows_of(tc)
                dma("sp", Rx[0:rows, tc, :], x_src[t0 + tc * 128:t0 + tc * 128 + rows, :], (), [("R", "x1", tc)])
            gpn = {"n": 0}

            def gate_piece(gi, cg):
                i = gpn["n"] % 2
                gpn["n"] += 1
                dma("sp", gp[i][:], grow_d[r:r + 1, gi, cg * 512:(cg + 1) * 512].to_broadcast([128, 512]), GROW,
                    [("gp", i)])
                return gp[i], ("gp", i)

            def proj_tokmajor(wd, kparts, lhs_of, lhs_keys, gi, cg, bset):
                banks = [bset * 4 + tc for tc in range(NTC)]
                nk_total = sum(n for _, n in kparts)
                kdone = 0
                for (kc0, nkc) in kparts:
                    Wt_, wk2 = wload([(lambda w, n=nkc: wview(w, n, 512), wcols(wd, cg * 512, 512, kc0, nkc))])
                    wv_ = wview(Wt_, nkc, 512)
                    for tc in range(NTC):
                        rows = rows_of(tc)
                        for kl in range(nkc):
                            kk = kdone + kl
                            mm(ps[banks[tc]][0:rows, :], lhs_of(kc0 + kl, tc, rows), wv_[:, kl, :], kk == 0,
                               kk == nk_total - 1, wk2 + lhs_keys, [("ps", banks[tc])])
                    kdone += nkc
                gpt, gk = gate_piece(gi, cg)
                for tc in range(NTC):
                    rows = rows_of(tc)
                    i = tc % 2
                    tt(sg[i][0:rows, :], ps[banks[tc]][0:rows, :], gpt[0:rows, :], ALU.mult, [("ps", banks[tc]), gk],
                       [("sg", i)])
                    tt(Rx[0:rows, tc, cg * 512:(cg + 1) * 512], Rx[0:rows, tc, cg * 512:(cg + 1) * 512], sg[i][0:rows, :],
                       ALU.add, [("sg", i), ("R", "x1", tc)], [("R", "x1", tc)])

            for cg in range(8):
                proj_tokmajor(wout_d, [(0, 16), (16, 16)],
                              lambda kc, tc, rows: mixT[:, kc, tc * 128:tc * 128 + rows], MIX_ALL, 0, cg, cg % 2)

            _stop(6)
            P.fence("M", lambda e: e.memset(small[:, 9:10], 0.0))
            Mf = M[:, :].bitcast(F32).rearrange("p (b f) -> p b f", f=4096)
            for tc in range(NTC):
                rows = rows_of(tc)
                i = tc % 2
                c_ = 4 + tc
                actf(Mf[0:rows, i, :], Rx[0:rows, tc, :], AF.Square, [("R", "x1", tc)], [("M", "xn", i), ("ssq", c_)],
                     accum=ssq[0:rows, c_:c_ + 1])
                actf(rstd[0:rows, c_:c_ + 1], ssq[0:rows, c_:c_ + 1], AF.Sqrt, [("ssq", c_), "epsc"], [("rs0", c_)],
                     bias=epsc[0:rows, 0:1], scale=1.0 / D)
                recip(rstd[0:rows, c_:c_ + 1], rstd[0:rows, c_:c_ + 1], [("rs0", c_)], [("rstd", c_)])
                ts(Mf[0:rows, i, :], Rx[0:rows, tc, :], rstd[0:rows, c_:c_ + 1], None, ALU.mult, None,
                   [("R", "x1", tc), ("rstd", c_), ("M", "xn", i)], [("M", "xn", i)])
                for fc in range(32):
                    b = fc % 4
                    bank, bk = ps[b], ("ps", b)
                    tr(bank[:, 0:rows], Mf[0:rows, i, fc * 128:(fc + 1) * 128], ident[0:rows, 0:rows],
                       [("M", "xn", i), "cst"], [bk])
                    if fc % 2 == 0:
                        actf(H[:, fc, tc * 128:tc * 128 + rows], bank[:, 0:rows], AF.Identity, [bk, "s2T"] + MT_ALL,
                             [("H", "h2", fc, tc)], bias=modT[:, 96 + fc, r:r + 1], scale=s2T[:, fc, r:r + 1])
                    else:
                        ts(H[:, fc, tc * 128:tc * 128 + rows], bank[:, 0:rows], s2T[:, fc, r:r + 1],
                           modT[:, 96 + fc, r:r + 1], ALU.mult, ALU.add, [bk, "s2T"] + MT_ALL, [("H", "h2", fc, tc)])
            H2 = lambda kc: [("H", "h2", kc, tc) for tc in range(NTC)]

            _stop(7)
            P.fence("M", lambda e: e.memset(small[:, 9:10], 0.0))
            actT = M[:, :].rearrange("p (c t) -> p c t", t=512)
            parts = [(0, 15), (15, 29), (29, 43)]
            for (q0, q1) in parts:
                nchk = (q1 - q0) * 2
                for q in range(q0, q1):
                    bs = (q % 2) * 4
                    Wg, wkg = wload([(lambda w: wview(w, 32, 256), wcols(wgu_d, q * 256, 256))])
                    wg = wview(Wg, 32, 256)
                    for u in range(2):
                        bank, bk = ps[bs + u], ("ps", bs + u)
                        for kc in range(32):
                            mm(bank[:, 0:T], wg[:, kc, u * 128:(u + 1) * 128], H[:, kc, 0:T], kc == 0, kc == 31,
                               wkg + H2(kc), [bk])
                    Wu, wku = wload([(lambda w: wview(w, 32, 256), wcols(wgu_d, DFF + q * 256, 256))])
                    wu = wview(Wu, 32, 256)
                    for u in range(2):
                        bank, bk = ps[bs + 2 + u], ("ps", bs + 2 + u)
                        for kc in range(32):
                            mm(bank[:, 0:T], wu[:, kc, u * 128:(u + 1) * 128], H[:, kc, 0:T], kc == 0, kc == 31,
                               wku + H2(kc), [bk])
                    for u in range(2):
                        lc = (q - q0) * 2 + u
                        actf(sg[u][:, 0:T], ps[bs + u][:, 0:T], AF.Silu, [("ps", bs + u)], [("sg", u)])
                        tt(actT[:, lc, 0:T], sg[u][:, 0:T], ps[bs + 2 + u][:, 0:T], ALU.mult,
                           [("sg", u), ("ps", bs + 2 + u)], [("M", "act", lc)])
                ACT_ALL = [("M", "act", lc) for lc in range(nchk)]
                kparts = []
                k_ = 0
                while k_ < nchk:
                    n_ = min(16, nchk - k_)
                    kparts.append((q0 * 2 + k_, n_))
                    k_ += n_
                for cg in range(8):
                    proj_tokmajor(wdn_d, kparts,
                                  lambda kc, tc, rows, q0=q0: actT[:, kc - q0 * 2, tc * 128:tc * 128 + rows],
                                  ACT_ALL, 1, cg, cg % 2)
            for tc in range(NTC):
                rows = rows_of(tc)
                dma("sp", y_dst[t0 + tc * 128:t0 + tc * 128 + rows, :], Rx[0:rows, tc, :], [("R", "x1", tc)],
                    [("y", r, t0, tc)], final=True)
            assert wstate["sid"] == NSLAB, wstate["sid"]
            wstate["pass0"] = False


        try:
            _stop(1)
            for ti in range(int(os.environ.get("KNT", n_prompt_tiles))):
                run_tile(0, ti * 512, 512, 64, ti == 0, ti == 3 or bool(os.environ.get("KLAST0")))
                _stop(9)
            if run_sample and not os.environ.get("KNOSAMPLE"):
                run_tile(1, 0, DEC, DEC, True, True)
        except _Stop:
            dma("sp", ys_d[0:2, 0:512], modrow[0][:], [("modrow", 0)], ["dbg"], final=True)

        with nc.Block() as block:
            P.build(nc, block, st)
        print("[kernel] ops=%d waits=%d %s" % (P.stats["ops"], P.stats["waits"], P.stats), flush=True)
    return nc


def _consts():
    c = np.zeros((128, NCONST), np.float32)
    p = np.arange(128)[:, None]
    j = np.arange(128)[None, :]
    c[:, 0:128] = (p == j)
    j64 = np.arange(64)[None, :]
    c[:, 128:192] = (p > j64)
    c[:, 192:256] = (p <= j64)
    c[:, 256:384] = ((p // 64) == (j // 64)) * (1.0 / 64.0)
    c[:, 384:448] = -(128.0 + j64 - p)
    c[:, 448:512] = -np.abs(j64 - p)
    c[:, 512:640] = 1.0
    c[:, 640:704] = c[:, 384:448]
    c[0:64, 640:704] = -1.0e9
    return c


def _params(i):
    f = lambda a: np.asarray(a, np.float32)
    fm = lambda v: f(v).reshape(-1, 128).T
    bc = lambda v: np.broadcast_to(f(v).reshape(1, -1), (128, f(v).size))
    cols = {
        "gmix": fm(i["g_mix"][0]), "gffn": fm(i["g_ffn"][0]), "bada": fm(i["b_ada"][0]),
        "cw": f(i["conv_w"][0]).reshape(4, 24, 128).transpose(2, 1, 0).reshape(128, 96),
        "cb": fm(i["conv_b"][0]),
        "dfeat": fm(np.repeat(f(i["d_skip"][0]), 64)), "nw": fm(i["ssd_norm_w"][0]),
        "wq": np.tile(f(i["q_norm_w"][0]), 2).reshape(128, 1), "wk": np.tile(f(i["k_norm_w"][0]), 2).reshape(128, 1),
        "dtb": bc(i["dt_bias"][0]), "alog": bc(i["a_log"][0]), "sinks": bc(i["sinks"][0]),
        "wqr": bc(i["q_norm_w"][0]), "wkr": bc(i["k_norm_w"][0]),
    }
    par = np.zeros((128, NPAR), np.float32)
    for n, (a, b) in PO.items():
        par[:, a:b] = cols[n]
    return par


_NC_CACHE = {}


def kernel(**inputs):
    i = {k: np.asarray(v) for k, v in inputs.items()}
    ncores = 8
    core_ids = list(range(ncores))
    dbg = os.environ.get("KDEBUG_CORES")
    if dbg:
        core_ids = list(range(int(dbg)))
    key = "main"
    if key not in _NC_CACHE:
        _NC_CACHE[key] = build_nc()
    nc = _NC_CACHE[key]
    par = _params(i)
    cst = _consts()
    f = lambda a: np.ascontiguousarray(a, dtype=np.float32)
    shared = {
        "par": par, "cst": cst, "badarow": f(i["b_ada"][0].reshape(1, -1)),
        "w_ada": f(i["w_ada"][0]), "w_in": f(i["w_in"][0]), "w_out": f(i["w_out"][0]),
        "w_gu": f(i["w_gate_up"][0]), "w_down": f(i["w_down"][0]),
    }
    in_maps = []
    for b in core_ids:
        ck = i["cache_k"][0, b]
        ckT = np.concatenate([ck.transpose(2, 1, 0)] * 2, axis=0)
        c2 = np.stack([i["c_prompt"][b], i["c_sample"][b]], axis=-1)
        m = dict(shared)
        m.update({
            "xp": f(i["x_prompt"][b]), "xs": f(i["x_sample"][b]),
            "sst": f(i["state_ssd"][0, b].reshape(2048, 128).T),
            "scv": f(i["state_conv"][0, b].reshape(3, 24, 128).transpose(2, 1, 0).reshape(128, 72)),
            "ckT": f(ckT.reshape(128, 512)), "cv": f(i["cache_v"][0, b].reshape(128, 256)),
            "c2T": f(c2.reshape(32, 128, 2).transpose(1, 0, 2).reshape(128, 64)),
        })
        in_maps.append(m)
    res = run_bass_kernel_spmd(nc, in_maps, core_ids=core_ids)
    rs = res.results
    nb = len(core_ids)
    B = 8
    yp = np.zeros((B, SEQ, D), np.float32)
    ys = np.zeros((B, DEC, D), np.float32)
    ssd_p = np.zeros((1, B, 32, 64, 128), np.float32)
    ssd_s = np.zeros((1, B, 32, 64, 128), np.float32)
    cv_p = np.zeros((1, B, 3, 3072), np.float32)
    cv_s = np.zeros((1, B, 3, 3072), np.float32)
    k_p = np.zeros((1, B, 128, 4, 64), np.float32)
    v_p = np.zeros((1, B, 128, 4, 64), np.float32)
    k_s = np.zeros((1, B, DEC, 4, 64), np.float32)
    v_s = np.zeros((1, B, DEC, 4, 64), np.float32)
    for b in range(nb):
        o = rs[b]
        yp[b] = o["yp"]
        ys[b] = o["ys"]
        ssd_p[0, b] = o["ossd_p"].T.reshape(32, 64, 128)
        ssd_s[0, b] = o["ossd_s"].T.reshape(32, 64, 128)
        cv_p[0, b] = o["ocv_p"].reshape(128, 24, 3).transpose(2, 1, 0).reshape(3, 3072)
        cv_s[0, b] = o["ocv_s"].reshape(128, 24, 3).transpose(2, 1, 0).reshape(3, 3072)
        k_p[0, b] = o["ok_p"].reshape(64, 4, 128).transpose(2, 1, 0)
        k_s[0, b] = o["ok_s"].reshape(64, 4, 128).transpose(2, 1, 0)[:DEC]
        v_p[0, b] = o["ov_p"].reshape(128, 4, 64)
        v_s[0, b] = o["ov_s"].reshape(128, 4, 64)[:DEC]
    return (yp, ys, ssd_p, cv_p, k_p, v_p, ssd_s, cv_s, k_s, v_s)
```

```python
import os
from contextlib import ExitStack

import numpy as np

import concourse.bass as bass
import concourse.mybir as mybir
from concourse.bass_utils import run_bass_kernel_spmd

F32 = mybir.dt.float32
BF16 = mybir.dt.bfloat16
AF = mybir.ActivationFunctionType
ALU = mybir.AluOpType

D = 4096
SEQ = 2048
DEC = 16
DIN = 7712
DFF = 11008
NMOD = 6
EPS = 1e-6
O_Z, O_X, O_B, O_C, O_DT, O_Q, O_K, O_V = 0, 2048, 4096, 4608, 5120, 5152, 7200, 7456
SLOPES = [float(2.0 ** (-8.0 * (h + 1) / 32.0)) for h in range(32)]

PO = {}
_o = 0
for _n, _w in [("gmix", 32), ("gffn", 32), ("bada", 192), ("cw", 96), ("cb", 24), ("dfeat", 16), ("nw", 16),
               ("wq", 1), ("wk", 1), ("dtb", 32), ("alog", 32), ("sinks", 32), ("wqr", 64), ("wkr", 64)]:
    PO[_n] = (_o, _o + _w)
    _o += _w
NPAR = _o
CO = {"ident": (0, 128), "SL": (128, 192), "U": (192, 256), "bd": (256, 384), "D1": (384, 448), "Dn": (448, 512),
      "ones": (512, 640), "D1m": (640, 704)}
NCONST = 704

ENGS = ("pe", "act", "dve", "pool", "sp")
SEM_LIMIT = 12000


class _Op:
    __slots__ = ("eng", "fn", "reads", "writes", "dma", "idx")

    def __init__(self, eng, fn, reads, writes, dma):
        self.eng = eng
        self.fn = fn
        self.reads = reads
        self.writes = writes
        self.dma = dma


class Prog:
    def __init__(self, n_rr=10):
        self.ops = []
        self.n_rr = n_rr
        self.finals = []
        self.phase = {}

    def add(self, eng, fn, reads=(), writes=(), dma=None):
        reads = list(reads)
        if eng != "pool":
            reads.append("__gb")
        op = _Op(eng, fn, tuple(reads), tuple(writes), dma)
        op.idx = len(self.ops)
        self.ops.append(op)
        return op

    def pe(self, fn, reads=(), writes=()):
        return self.add("pe", fn, reads, writes)

    def act(self, fn, reads=(), writes=()):
        return self.add("act", fn, reads, writes)

    def dve(self, fn, reads=(), writes=()):
        return self.add("dve", fn, reads, writes)

    def dma(self, eng, fn, reads=(), writes=(), group=None, final=False):
        op = self.add(eng, fn, reads, writes, dma=(group if group is not None else "__rr__"))
        if final:
            self.finals.append(op)
        return op

    def region(self, name):
        self.phase[name] = 0

    def fence(self, name, fn):
        op = _Op("dve", fn, (), ("__gb",), None)
        op.idx = len(self.ops)
        self.ops.append(op)

    def build(self, nc, block, st):
        ops = self.ops
        last_w = {}
        readers = {}
        deps = [None] * len(ops)
        for op in ops:
            d = set()
            for r in op.reads:
                w = last_w.get(r)
                if w is not None:
                    d.add(w)
            for w_ in op.writes:
                w = last_w.get(w_)
                if w is not None:
                    d.add(w)
                rl = readers.get(w_)
                if rl:
                    d.update(rl)
            d.discard(op.idx)
            latest = {}
            d2 = set()
            for j in d:
                p = ops[j]
                if p.dma is None:
                    if latest.get(p.eng, -1) < j:
                        latest[p.eng] = j
                else:
                    d2.add(j)
            d2.update(latest.values())
            d = d2
            dd = set()
            for j in d:
                p = ops[j]
                if p.eng == op.eng and p.dma is None and op.dma is None:
                    if op.eng == "pe":
                        continue
                    jr = -1
                    for r in op.reads:
                        w = last_w.get(r)
                        if w is not None and w != op.idx and ops[w].eng == op.eng and ops[w].dma is None and w > jr:
                            jr = w
                    if jr >= 0:
                        dd.add(jr)
                    continue
                dd.add(j)
            deps[op.idx] = dd
            for r in op.reads:
                readers.setdefault(r, []).append(op.idx)
            for w_ in op.writes:
                last_w[w_] = op.idx
                readers[w_] = []
        needs_sig = [False] * len(ops)
        for op in ops:
            for j in deps[op.idx]:
                needs_sig[j] = True
        for op in self.finals:
            needs_sig[op.idx] = True

        def newsem(name):
            return st.enter_context(nc.semaphore(name))

        eng_sem = {e: newsem("pg_%s_0" % e) for e in ENGS}
        eng_gen = {e: 0 for e in ENGS}
        eng_cnt = {e: 0 for e in ENGS}
        dma_sems = {}
        dma_cnt = {}
        rr_names = ["rr%d" % i for i in range(self.n_rr)]
        rr_last = {n: None for n in rr_names}
        rr_i = 0
        token = [None] * len(ops)
        extra = [None] * len(ops)
        semkey = {}

        def key(s):
            k = id(s)
            semkey[k] = s
            return k

        for op in ops:
            if op.dma is not None:
                g = op.dma
                if g == "__rr__":
                    g = rr_names[rr_i % len(rr_names)]
                    rr_i += 1
                    if rr_last[g] is not None:
                        extra[op.idx] = rr_last[g]
                if g not in dma_sems:
                    dma_sems[g] = newsem("pgd_" + g)
                    dma_cnt[g] = 0
                dma_cnt[g] += 16
                token[op.idx] = (key(dma_sems[g]), dma_cnt[g])
                if g in rr_last:
                    rr_last[g] = token[op.idx]
            elif needs_sig[op.idx]:
                if eng_cnt[op.eng] >= SEM_LIMIT:
                    eng_gen[op.eng] += 1
                    eng_sem[op.eng] = newsem("pg_%s_%d" % (op.eng, eng_gen[op.eng]))
                    eng_cnt[op.eng] = 0
                eng_cnt[op.eng] += 1
                token[op.idx] = (key(eng_sem[op.eng]), eng_cnt[op.eng])
        per_eng = {e: [] for e in ENGS}
        for op in ops:
            per_eng[op.eng].append(op)
        stats = {"waits": 0, "ops": len(ops)}
        stats["per_eng"] = {e: len(per_eng[e]) for e in ENGS}
        stats["gens"] = dict(eng_gen)
        stats["cnt"] = dict(eng_cnt)
        stats["dma"] = dict(dma_cnt)

        def emit_engine(ename, eng):
            waited = {}
            for op in per_eng[ename]:
                need = {}
                for j in deps[op.idx]:
                    k, v = token[j]
                    if waited.get(k, 0) >= v:
                        continue
                    if need.get(k, 0) < v:
                        need[k] = v
                if extra[op.idx] is not None:
                    k, v = extra[op.idx]
                    if waited.get(k, 0) < v and need.get(k, 0) < v:
                        need[k] = v
                for k, v in need.items():
                    eng.wait_ge(semkey[k], v)
                    waited[k] = v
                    stats["waits"] += 1
                ins = op.fn(eng)
                tk = token[op.idx]
                if tk is not None:
                    ins.then_inc(semkey[tk[0]], 16 if op.dma is not None else 1)
            if ename == "sp":
                for op in self.finals:
                    k, v = token[op.idx]
                    eng.wait_ge(semkey[k], v)

        @block.tensor
        def _(e):
            emit_engine("pe", e)

        @block.scalar
        def _(e):
            emit_engine("act", e)

        @block.vector
        def _(e):
            emit_engine("dve", e)

        @block.gpsimd
        def _(e):
            emit_engine("pool", e)

        @block.sync
        def _(e):
            emit_engine("sp", e)

        self.stats = stats


class _Stop(Exception):
    pass


KSTOP = float(os.environ.get("KSTOP", "99"))


def _stop(stage):
    if KSTOP <= stage:
        raise _Stop()


def build_nc(run_sample=True, n_prompt_tiles=4):
    nc = bass.Bass("TRN2", target_bir_lowering=False)
    din = lambda n, s: nc.dram_tensor(n, s, F32, kind="ExternalInput").ap()
    dout = lambda n, s: nc.dram_tensor(n, s, F32, kind="ExternalOutput").ap()
    xp_d = din("xp", [SEQ, D])
    xs_d = din("xs", [DEC, D])
    sst_d = din("sst", [128, 2048])
    scv_d = din("scv", [128, 72])
    ckT_d = din("ckT", [128, 512])
    cv_d = din("cv", [128, 256])
    c2T_d = din("c2T", [128, 64])
    par_d = din("par", [128, NPAR])
    cst_d = din("cst", [128, NCONST])
    wada_d = din("w_ada", [D, NMOD * D])
    win_d = din("w_in", [D, DIN])
    wout_d = din("w_out", [D, D])
    wgu_d = din("w_gu", [D, 2 * DFF])
    wdn_d = din("w_down", [DFF, D])
    yp_d = dout("yp", [SEQ, D])
    ys_d = dout("ys", [DEC, D])
    ossd_d = [dout("ossd_p", [128, 2048]), dout("ossd_s", [128, 2048])]
    ocv_d = [dout("ocv_p", [128, 72]), dout("ocv_s", [128, 72])]
    ok_d = [dout("ok_p", [64, 512]), dout("ok_s", [64, 512])]
    ov_d = [dout("ov_p", [128, 256]), dout("ov_s", [128, 256])]
    grow_d = nc.dram_tensor("grow", [2, 2, D], F32, kind="Internal").ap()
    NSLAB = 182
    _wsc = [nc.dram_tensor("wsc%d" % i_, [91, 128, 8192], BF16, kind="Internal").ap() for i_ in range(2)]

    class _WSC:
        def __getitem__(self, key):
            sid = key[0]
            return _wsc[sid // 91][sid % 91, :, :]
    wsc_d = _WSC()

    st = ExitStack()
    with st:
        sb = lambda n, s, dt: st.enter_context(nc.sbuf_tensor("sb_" + n, s, dt))
        par = sb("par", [128, NPAR], F32)
        cst = sb("cst", [128, NCONST], F32)
        R = sb("R", [128, 16384], F32)
        H = sb("H", [128, 32, 512], BF16)
        M = sb("M", [128, 16384], BF16)
        W = [sb("W0", [128, 8192], BF16), sb("W1", [128, 8192], BF16)]
        modT = sb("modT", [128, 192, 2], F32)
        s1T = sb("s1T", [128, 32, 2], F32)
        s2T = sb("s2T", [128, 32, 2], F32)
        scT = sb("scT", [128, 32, 2], BF16)
        c2T = sb("c2Ts", [128, 32, 2], F32)
        onesb = sb("onesb", [128, 128], BF16)
        epsc = sb("epsc", [128, 1], F32)
        negM = sb("negM", [128, 1], F32)
        Ebc = sb("Ebc", [128, 32], F32)
        aneg = sb("aneg", [128, 32], F32)
        wq8 = sb("wq8", [128, 1], F32)
        small = sb("small", [128, 16], F32)
        kcar = sb("kcar", [128, 4, 128], BF16)
        vcar = sb("vcar", [128, 2, 256], BF16)
        halo = sb("halo", [128, 24, 3], F32)
        stT = sb("stT", [128, 2048], F32)
        kf32 = sb("kf32", [128, 4, 128], F32)
        vout = sb("vout", [128, 256], F32)
        ssq = sb("ssq", [128, 8], F32)
        rstd = sb("rstd", [128, 8], F32)
        gp = [sb("gp0", [128, 512], F32), sb("gp1", [128, 512], F32)]
        sg = [sb("sg0", [128, 512], F32), sb("sg1", [128, 512], F32)]
        modrow = [sb("modrow0", [2, 512], F32), sb("modrow1", [2, 512], F32)]
        brow = [sb("brow0", [2, 512], F32), sb("brow1", [2, 512], F32)]
        ps = [st.enter_context(nc.psum_tensor("ps%d" % i, [128, 512], F32)) for i in range(8)]
        dt_x = sb("dt_x", [64, 256], F32)
        dt_e = sb("dt_e", [64, 256], F32)
        dt_t = sb("dt_t", [64, 256], F32)
        dt_a = sb("dt_a", [64, 256], F32)
        dt_s = sb("dt_s", [64, 256], F32)
        dt_c = sb("dt_c", [64, 256], F32)
        badarow_d = din("badarow", [1, NMOD * D])

        P = Prog()
        for rg in ("R", "H", "M"):
            P.region(rg)

        def cc(name):
            a, b = CO[name]
            return cst[:, a:b]

        def pc(name):
            a, b = PO[name]
            return par[:, a:b]

        ident = cc("ident")
        ones = cc("ones")
        SL = cc("SL")
        U = cc("U")
        bd = cc("bd")
        D1 = cc("D1")
        Dn = cc("Dn")
        D1m = cc("D1m")

        def mm(out, lhsT, rhs, start, stop, reads, writes):
            P.pe(lambda e: e.matmul(out, lhsT=lhsT, rhs=rhs, start=start, stop=stop), reads, writes)

        def tr(out, in_, idn, reads, writes):
            P.pe(lambda e: e.transpose(out, in_, idn), reads, writes)

        def actf(out, in_, func, reads, writes, bias=None, scale=None, accum=None):
            kw = {}
            if bias is not None:
                kw["bias"] = bias
            if scale is not None:
                kw["scale"] = scale
            if accum is not None:
                kw["accum_out"] = accum
            P.act(lambda e: e.activation(out=out, in_=in_, func=func, **kw), reads, writes)

        def tt(out, in0, in1, op, reads, writes):
            P.dve(lambda e: e.tensor_tensor(out=out, in0=in0, in1=in1, op=op), reads, writes)

        def ts(out, in0, s1, s2, op0, op1, reads, writes):
            if op1 is None:
                P.dve(lambda e: e.tensor_scalar(out=out, in0=in0, scalar1=s1, scalar2=None, op0=op0), reads, writes)
            else:
                P.dve(lambda e: e.tensor_scalar(out=out, in0=in0, scalar1=s1, scalar2=s2, op0=op0, op1=op1),
                      reads, writes)

        def stt(out, in0, scalar, in1, op0, op1, reads, writes):
            P.dve(lambda e: e.scalar_tensor_tensor(out=out, in0=in0, scalar=scalar, in1=in1, op0=op0, op1=op1),
                  reads, writes)

        def cpy(out, in_, reads, writes, eng="dve"):
            if eng == "dve":
                P.dve(lambda e: e.tensor_copy(out=out, in_=in_), reads, writes)
            else:
                P.act(lambda e: e.copy(out=out, in_=in_), reads, writes)

        def recip(out, in_, reads, writes):
            P.dve(lambda e: e.reciprocal(out=out, in_=in_), reads, writes)

        def mset(ap, val, writes):
            P.dve(lambda e: e.memset(ap, val), (), writes)

        def dma(eng, out, in_, reads, writes, group=None, final=False, slow=False):
            if slow:
                P.dma(eng, lambda e: e.dma_start(out=out, in_=in_, allow_slow_non_contiguous=True), reads, writes,
                      group, final)
            else:
                P.dma(eng, lambda e: e.dma_start(out=out, in_=in_), reads, writes, group, final)

        wstate = {"n": 0, "sid": None, "pass0": True, "nslab": None}

        def wload(pieces):
            slot = wstate["n"] % 2
            wstate["n"] += 1
            sid = wstate["sid"]
            if sid is not None:
                wstate["sid"] += 1
            if sid is None or wstate["pass0"]:
                keys = []
                for i, (dstf, src) in enumerate(pieces):
                    k = ("w", slot, i)
                    keys.append(k)
                    dma("pool", dstf(W[slot]), src, (), [k], group="w%d" % slot)
                if sid is not None:
                    dma("sp", wsc_d[sid, :, :], W[slot][:, :], keys, [("wsc", sid)], group="ws%d" % slot)
                return W[slot], keys
            k = ("w", slot, 0)
            dma("pool", W[slot][:, :], wsc_d[sid, :, :], [("wsc", sid)], [k], group="w%d" % slot)
            return W[slot], [k]

        def wcols(wd, c0, n, kc0=0, nkc=32):
            return wd[kc0 * 128:(kc0 + nkc) * 128, c0:c0 + n].rearrange("(kc p) m -> p kc m", p=128)

        def wview(Wt, nkc, n):
            return Wt[:, 0:nkc * n].rearrange("p (kc m) -> p kc m", m=n)

        dma("sp", par[:], par_d[:, :], (), ["par"])
        dma("sp", cst[:], cst_d[:, :], (), ["cst"])
        dma("sp", c2T[:].rearrange("p a b -> p (a b)"), c2T_d[:, :], (), ["c2T"])
        mset(epsc[:], EPS, ["epsc"])
        cpy(onesb[:], ones, ["cst"], ["onesb"])
        actf(scT[:].rearrange("p a b -> p (a b)"), c2T[:].rearrange("p a b -> p (a b)"), AF.Silu, ["c2T"], ["scT"])
        P.act(lambda e: e.mul(out=wq8[:], in_=pc("wq"), mul=0.125), ["par"], ["wq8"])
        actf(aneg[:], pc("alog"), AF.Exp, ["par"], ["aneg0"])
        ts(aneg[:], aneg[:], -1.0, None, ALU.mult, None, ["aneg0"], ["aneg"])
        P.dve(lambda e: e.tensor_reduce(out=small[:, 2:3], in_=pc("wqr"), axis=mybir.AxisListType.X, op=ALU.max,
                                        apply_absolute_value=True), ["par"], ["sm2"])
        P.dve(lambda e: e.tensor_reduce(out=small[:, 3:4], in_=pc("wkr"), axis=mybir.AxisListType.X, op=ALU.max,
                                        apply_absolute_value=True), ["par"], ["sm3"])
        tt(small[:, 4:5], small[:, 2:3], small[:, 3:4], ALU.mult, ["sm2", "sm3"], ["sm4"])
        ts(negM[:], small[:, 4:5], -8.0, None, ALU.mult, None, ["sm4"], ["negM"])
        actf(Ebc[:], pc("sinks"), AF.Exp, ["par", "negM"], ["Ebc"], bias=negM[:, 0:1], scale=1.0)

        for cg in range(48):
            bank = ps[cg % 2]
            bk = ("ps", cg % 2)
            for hf in range(2):
                Wt, wk_ = wload([(lambda w: wview(w, 16, 512), wcols(wada_d, cg * 512, 512, hf * 16, 16))])
                wv = wview(Wt, 16, 512)
                for kl in range(16):
                    kc = hf * 16 + kl
                    mm(bank[0:2, :], scT[:, kc, :], wv[:, kl, :], kc == 0, kc == 31, wk_ + ["scT"], [bk])
            i = cg % 2
            cpy(modrow[i][:], bank[0:2, :], [bk], [("modrow", i)], eng="act")
            which = cg // 8
            if which in (2, 5):
                gi = 0 if which == 2 else 1
                c0 = (cg % 8) * 512
                dma("sp", brow[i][:], badarow_d[0:1, cg * 512:(cg + 1) * 512].to_broadcast([2, 512]), (),
                    [("brow", i)])
                tt(modrow[i][:], modrow[i][:], brow[i][:], ALU.add, [("modrow", i), ("brow", i)], [("modrow2", i)])
                dma("sp", grow_d[:, gi, c0:c0 + 512], modrow[i][:], [("modrow2", i)], [("grow", gi, cg % 8)])
            else:
                tb = ps[2 + cg % 2]
                tbk = ("ps", 2 + cg % 2)
                for j in range(4):
                    tr(tb[:, j * 2:j * 2 + 2], modrow[i][:, j * 128:(j + 1) * 128], ident[0:2, 0:2],
                       [("modrow", i), "cst"], [tbk])
                a0 = PO["bada"][0] + cg * 4
                tt(modT[:, cg * 4:cg * 4 + 4, :], tb[:, 0:8].rearrange("p (j r) -> p j r", r=2),
                   par[:, a0:a0 + 4].unsqueeze(2).to_broadcast([128, 4, 2]), ALU.add, [tbk, "par"], [("modT", cg)])
        MT_ALL = [("modT", cg) for cg in range(48) if cg // 8 not in (2, 5)]
        stt(s1T[:], modT[:, 32:64, :], 1.0, pc("gmix").unsqueeze(2).to_broadcast([128, 32, 2]), ALU.add, ALU.mult,
            MT_ALL + ["par"], ["s1T"])
        stt(s2T[:], modT[:, 128:160, :], 1.0, pc("gffn").unsqueeze(2).to_broadcast([128, 32, 2]), ALU.add, ALU.mult,
            MT_ALL + ["par"], ["s2T"])
        GROW = [("grow", gi, j) for gi in range(2) for j in range(8)]

        def run_tile(r, t0, T, L, first, last):
            NCH = T // L
            NTC = (T + 127) // 128
            wstate["sid"] = 0
            x_src = xp_d if r == 0 else xs_d
            y_dst = yp_d if r == 0 else ys_d
            Rx = R[:, :].rearrange("p (tc f) -> p tc f", f=4096)
            rows_of = lambda tc: min(128, T - tc * 128)

            P.fence("R", lambda e: e.memset(small[:, 8:9], 0.0))
            Mj = M[:, 0:4096]
            for tc in range(NTC):
                rows = rows_of(tc)
                dma("sp", Rx[0:rows, tc, :], x_src[t0 + tc * 128:t0 + tc * 128 + rows, :], (), [("R", "x", tc)])
                actf(Mj[0:rows, :], Rx[0:rows, tc, :], AF.Square, [("R", "x", tc)], [("M", "junk"), ("ssq", tc)],
                     accum=ssq[0:rows, tc:tc + 1])
                actf(rstd[0:rows, tc:tc + 1], ssq[0:rows, tc:tc + 1], AF.Sqrt, [("ssq", tc), "epsc"], [("rs0", tc)],
                     bias=epsc[0:rows, 0:1], scale=1.0 / D)
                recip(rstd[0:rows, tc:tc + 1], rstd[0:rows, tc:tc + 1], [("rs0", tc)], [("rstd", tc)])
                ts(Rx[0:rows, tc, :], Rx[0:rows, tc, :], rstd[0:rows, tc:tc + 1], None, ALU.mult, None,
                   [("R", "x", tc), ("rstd", tc)], [("R", "xn", tc)])
            for fc in range(32):
                b = fc % 2
                bank = ps[b]
                bk = ("ps", b)
                for tc in range(NTC):
                    rows = rows_of(tc)
                    tr(bank[:, tc * 128:tc * 128 + rows], Rx[0:rows, tc, fc * 128:(fc + 1) * 128],
                       ident[0:rows, 0:rows], [("R", "xn", tc), "cst"], [bk])
                if fc % 2 == 0:
                    actf(H[:, fc, 0:T], bank[:, 0:T], AF.Identity, [bk, "s1T"] + MT_ALL, [("H", "h", fc)],
                         bias=modT[:, 0 + fc, r:r + 1], scale=s1T[:, fc, r:r + 1])
                else:
                    ts(H[:, fc, 0:T], bank[:, 0:T], s1T[:, fc, r:r + 1], modT[:, 0 + fc, r:r + 1], ALU.mult, ALU.add,
                       [bk, "s1T"] + MT_ALL, [("H", "h", fc)])
            HALL = [("H", "h", fc) for fc in range(32)]
            _stop(2)

            P.fence("R", lambda e: e.memset(small[:, 8:9], 0.0))
            Rb = R[:, :].bitcast(BF16)
            kTz = Rb[:, 0:5120].rearrange("p (h e t) -> p h e t", e=2, t=640)
            vv = Rb[:, 5120:7680].rearrange("p (s c) -> p s c", c=256)
            qT = Rb[:, 7680:11776].rearrange("p (b c t) -> p b c t", b=2, t=512)
            P1 = [Rb[:, 11776:12288], Rb[:, 12288:12800]]
            P2 = [Rb[:, 12800:13312], Rb[:, 13312:13824]]
            fo = 6912
            sq = [R[:, fo:fo + 512], R[:, fo + 512:fo + 1024]]
            rs = [R[:, fo + 1024:fo + 1536], R[:, fo + 1536:fo + 2048]]
            sb1 = [R[:, fo + 2048:fo + 2560], R[:, fo + 2560:fo + 3072]]
            sb2 = [R[:, fo + 3072:fo + 3584], R[:, fo + 3584:fo + 4096]]
            rec = R[:, fo + 4096:fo + 4608]
            ktmp = R[:, fo + 4608:fo + 5120]
            vtmp = R[:, fo + 5120:fo + 5376]
            mixT = M[:, :].rearrange("p (c t) -> p c t", t=512)

            def qknorm(bank, bk, i, wcol, wkeys):
                actf(sq[i][:, 0:T], bank[:, 0:T], AF.Square, [bk], [("R", "sq", i)])
                mm(ps[3][:, 0:T], bd, sq[i][:, 0:T], True, True, [("R", "sq", i), "cst"], [("ps", 3)])
                actf(rs[i][:, 0:T], ps[3][:, 0:T], AF.Sqrt, [("ps", 3), "epsc"], [("R", "rs0", i)],
                     bias=epsc[:, 0:1], scale=1.0)
                recip(rs[i][:, 0:T], rs[i][:, 0:T], [("R", "rs0", i)], [("R", "rs", i)])

            P.dve(lambda e: e.memset(Rb[:, 0:5120], 0.0), (), [("R", "kTz0")])
            if r == 1:
                dma("sp", ktmp[:, 0:512], ckT_d[:, :], (), [("R", "ktmp")])
                for e_ in range(2):
                    pr_ = slice(e_ * 64, (e_ + 1) * 64)
                    cpy(kTz[pr_, :, e_, 0:128], ktmp[pr_, 0:512].rearrange("p (h t) -> p h t", t=128),
                        [("R", "ktmp"), ("R", "kTz0")], [("R", "kTc")])
                dma("sp", vtmp[:, 0:256], cv_d[:, :], (), [("R", "vtmp")])
                cpy(vv[:, 0, :], vtmp[:, 0:256], [("R", "vtmp")], [("R", "vv", 0)])
            elif not first:
                for e_ in range(2):
                    pr_ = slice(e_ * 64, (e_ + 1) * 64)
                    cpy(kTz[pr_, :, e_, 0:128], kcar[pr_, :, :], ["kcar", ("R", "kTz0")], [("R", "kTc")])
                cpy(vv[:, 0, :], vcar[:, 0, :], ["vcar"], [("R", "vv", 0)])
                cpy(vv[0:64, 1, :], vcar[0:64, 1, :], ["vcar"], [("R", "vv", 1, "lo")])
            have_prev = (r == 1) or (not first)

            for kh in range(4):
                if kh % 2 == 0:
                    pcs = []
                    for hh in range(2):
                        for u_ in range(2):
                            pcs.append((lambda w, a=hh * 128 + u_ * 64: wview(w, 32, 256)[:, :, a:a + 64],
                                        wcols(win_d, O_K + (kh + hh) * 64, 64)))
                    Wt, wk_ = wload(pcs)
                    wv = wview(Wt, 32, 256)
                b = kh % 3
                bank, bk = ps[b], ("ps", b)
                for kc in range(32):
                    mm(bank[:, 0:T], wv[:, kc, (kh % 2) * 128:(kh % 2 + 1) * 128], H[:, kc, 0:T], kc == 0, kc == 31,
                       wk_ + [("H", "h", kc)], [bk])
                i = kh % 2
                qknorm(bank, bk, i, None, None)
                stt(ktmp[:, 0:T], bank[:, 0:T], pc("wk"), rs[i][:, 0:T], ALU.mult, ALU.mult,
                    [bk, ("R", "rs", i), "par"], [("R", "ktmp")])
                for e_ in range(2):
                    pr_ = slice(e_ * 64, (e_ + 1) * 64)
                    cpy(kTz[pr_, kh, e_, 128:128 + T], ktmp[pr_, 0:T], [("R", "ktmp"), ("R", "kTz0")],
                        [("R", "kT", kh)], eng="act")
                if last:
                    nk = min(128, T)
                    cpy(kf32[:, kh, 0:nk], ktmp[:, T - nk:T], [("R", "ktmp")], [("kf32", kh)], eng="act")
            Wt, wk_ = wload([(lambda w: wview(w, 32, 256), wcols(win_d, O_V, 256))])
            wv = wview(Wt, 32, 256)
            vb = 0
            for j in range(NCH):
                mrows = 2 * L if j < NCH - 1 else L
                b = vb % 3
                vb += 1
                bank, bk = ps[b], ("ps", b)
                for kc in range(32):
                    mm(bank[0:mrows, 0:256], H[:, kc, j * L:j * L + mrows], wv[:, kc, :], kc == 0, kc == 31,
                       wk_ + [("H", "h", kc)], [bk])
                cpy(vv[0:mrows, 2 + j, :], bank[0:mrows, 0:256], [bk], [("R", "vv", 2 + j)], eng="act")
                if last and ((r == 0 and j == NCH - 2) or (r == 1 and j == 0)):
                    cpy(vout[0:mrows, :], bank[0:mrows, 0:256], [bk], ["vout"], eng="act")
            if r == 0:
                if first:
                    P.dve(lambda e: e.memset(vv[0:64, 1, :], 0.0), (), [("R", "vv", 1, "lo")])
                b = vb % 3
                bank, bk = ps[b], ("ps", b)
                for kc in range(32):
                    mm(bank[64:128, 0:256], H[:, kc, 0:64], wv[:, kc, :], kc == 0, kc == 31,
                       wk_ + [("H", "h", kc)], [bk])
                cpy(vv[64:128, 1, :], bank[64:128, 0:256], [bk], [("R", "vv", 1, "hi")], eng="act")
            Wt, wk_ = wload([(lambda w: wview(w, 32, 32), wcols(win_d, O_DT, 32))])
            wv = wview(Wt, 32, 32)
            dtb = ps[7]
            for j in range(NCH):
                for kc in range(32):
                    mm(dtb[0:L, j * 32:(j + 1) * 32], H[:, kc, j * L:(j + 1) * L], wv[:, kc, :], kc == 0, kc == 31,
                       wk_ + [("H", "h", kc)], [("ps", 7)])
            n32 = NCH * 32
            dtv = lambda t_: t_[0:L, 0:n32].rearrange("p (c h) -> p c h", h=32)
            tt(dtv(dt_x), dtv(dtb), pc("dtb")[0:L, :].unsqueeze(1).to_broadcast([L, NCH, 32]), ALU.add,
               [("ps", 7), "par"], ["dt_x"])
            ts(dt_e[0:L, 0:n32], dt_x[0:L, 0:n32], 30.0, None, ALU.min, None, ["dt_x"], ["dt_e0"])
            actf(dt_e[0:L, 0:n32], dt_e[0:L, 0:n32], AF.Exp, ["dt_e0"], ["dt_e1"])
            actf(dt_e[0:L, 0:n32], dt_e[0:L, 0:n32], AF.Ln, ["dt_e1"], ["dt_e2"], bias=1.0)
            tt(dt_t[0:L, 0:n32], dt_e[0:L, 0:n32], dt_x[0:L, 0:n32], ALU.max, ["dt_e2", "dt_x"], ["dt"])
            tt(dtv(dt_a), dtv(dt_t), aneg[0:L, :].unsqueeze(1).to_broadcast([L, NCH, 32]), ALU.mult,
               ["dt", "aneg"], ["dt_a"])
            mm(ps[7][0:L, 256:256 + n32], SL[0:L, 0:L], dt_a[0:L, 0:n32], True, True, ["dt_a", "cst"], [("ps", 7, "b")])
            actf(dt_s[0:L, 0:n32], ps[7][0:L, 256:256 + n32], AF.Exp, [("ps", 7, "b")], ["dte"])
            tt(dt_s[0:L, 0:n32], dt_s[0:L, 0:n32], dt_t[0:L, 0:n32], ALU.mult, ["dte", "dt"], ["dt_s"])
            mm(ps[7][0:L, 256:256 + n32], U[0:L, 0:L], dt_a[0:L, 0:n32], True, True, ["dt_a", "cst"], [("ps", 7, "b")])
            cpy(dt_c[0:L, 0:n32], ps[7][0:L, 256:256 + n32], [("ps", 7, "b")], ["dt_c"], eng="act")

            _stop(3)

            def q_group(kvh, qb):
                for s2 in range(2):
                    c0 = O_Q + kvh * 512 + s2 * 256
                    Wt_, wk2 = wload([(lambda w: wview(w, 32, 256), wcols(win_d, c0, 256))])
                    wv_ = wview(Wt_, 32, 256)
                    for u in range(2):
                        ch = s2 * 2 + u
                        b = ch % 3
                        bank, bk = ps[b], ("ps", b)
                        for kc in range(32):
                            mm(bank[:, 0:T], wv_[:, kc, u * 128:(u + 1) * 128], H[:, kc, 0:T], kc == 0, kc == 31,
                               wk2 + [("H", "h", kc)], [bk])
                        i = ch % 2
                        qknorm(bank, bk, i, None, None)
                        stt(qT[:, qb, ch, 0:T], bank[:, 0:T], wq8[:, 0:1], rs[i][:, 0:T], ALU.mult, ALU.mult,
                            [bk, ("R", "rs", i), "wq8"], [("R", "qT", qb, ch)])

            KT_ALL = [("R", "kT", kh) for kh in range(4)] + [("R", "kTc"), ("R", "kTz0")]
            VV_ALL = [("R", "vv", s) for s in range(10)] + [("R", "vv", 1, "lo"), ("R", "vv", 1, "hi")]

            def attn_qk(kvh, qb, c, i):
                v2 = have_prev or c >= 2
                v1 = have_prev or c >= 1
                p0 = 0
                D1u = D1 if v2 else D1m
                qk = [("R", "qT", qb, ch) for ch in range(4)]
                for h in range(8):
                    e_ = h % 2
                    qs = qT[:, qb, h // 2, c * L:(c + 1) * L]
                    if v1:
                        kbase = c * L if r == 0 else 0
                        mm(ps[4][p0:128, h * L:(h + 1) * L], kTz[:, kvh, e_, kbase + p0:kbase + 128], qs,
                           True, True, KT_ALL + qk, [("ps", 4)])
                    mm(ps[5][0:L, h * L:(h + 1) * L], kTz[:, kvh, e_, 128 + c * L:128 + (c + 1) * L], qs,
                       True, True, KT_ALL + qk, [("ps", 5)])
                for h in range(8):
                    sl = SLOPES[kvh * 8 + h]
                    if v1:
                        stt(sb1[i][p0:128, h * L:(h + 1) * L], D1u[p0:128, 0:L], sl, ps[4][p0:128, h * L:(h + 1) * L],
                            ALU.mult, ALU.add, [("ps", 4), "cst"], [("R", "sb1", i)])
                    stt(sb2[i][0:L, h * L:(h + 1) * L], Dn[0:L, 0:L], sl, ps[5][0:L, h * L:(h + 1) * L],
                        ALU.mult, ALU.add, [("ps", 5), "cst"], [("R", "sb2", i)])
                if v1:
                    actf(P1[i][p0:128, 0:8 * L], sb1[i][p0:128, 0:8 * L], AF.Exp, [("R", "sb1", i), "negM"],
                         [("R", "P1", i)], bias=negM[p0:128, 0:1], scale=1.0)
                actf(P2[i][0:L, 0:8 * L], sb2[i][0:L, 0:8 * L], AF.Exp, [("R", "sb2", i), "negM"], [("R", "P2", i)],
                     bias=negM[0:L, 0:1], scale=1.0)

            def attn_pv(kvh, c, i):
                v2 = have_prev or c >= 2
                v1 = have_prev or c >= 1
                p0 = 0
                for h in range(8):
                    e_ = h % 2
                    o = ps[7][e_ * 64:(e_ + 1) * 64, (h // 2) * L:(h // 2 + 1) * L]
                    if v1:
                        mm(o, vv[p0:128, c, kvh * 64:(kvh + 1) * 64], P1[i][p0:128, h * L:(h + 1) * L], True, False,
                           VV_ALL + [("R", "P1", i)], [("ps", 7)])
                    mm(o, vv[0:L, c + 2, kvh * 64:(kvh + 1) * 64], P2[i][0:L, h * L:(h + 1) * L], not v1, True,
                       VV_ALL + [("R", "P2", i)], [("ps", 7)])
                if v1:
                    mm(ps[6][:, 0:8 * L], onesb[p0:128, :], P1[i][p0:128, 0:8 * L], True, False,
                       [("R", "P1", i), "onesb"], [("ps", 6)])
                mm(ps[6][:, 0:8 * L], onesb[0:L, :], P2[i][0:L, 0:8 * L], not v1, True, [("R", "P2", i), "onesb"],
                   [("ps", 6)])
                tt(rec[:, 0:8 * L].rearrange("p (h t) -> p h t", t=L), ps[6][:, 0:8 * L].rearrange("p (h t) -> p h t", t=L),
                   Ebc[:, kvh * 8:(kvh + 1) * 8].unsqueeze(2).to_broadcast([128, 8, L]), ALU.add, [("ps", 6), "Ebc"],
                   [("R", "rec0")])
                recip(rec[:, 0:8 * L], rec[:, 0:8 * L], [("R", "rec0")], [("R", "rec")])
                for e_ in range(2):
                    pr = slice(e_ * 64, (e_ + 1) * 64)
                    tt(mixT[pr, 16 + kvh * 4:16 + kvh * 4 + 4, c * L:(c + 1) * L],
                       ps[7][pr, 0:4 * L].rearrange("p (a t) -> p a t", t=L),
                       rec[pr, 0:8 * L].rearrange("p (a two t) -> p a two t", two=2, t=L)[:, :, e_, :], ALU.mult,
                       [("ps", 7), ("R", "rec")], [("M", "mix", 16 + kvh * 4, c, e_)])

            q_group(0, 0)
            _stop(3.2)
            for kvh in range(4):
                if kvh + 1 < 4:
                    q_group(kvh + 1, (kvh + 1) % 2)
                qb = kvh % 2
                attn_qk(kvh, qb, 0, 0)
                _stop(3.4)
                for c in range(NCH):
                    if c + 1 < NCH:
                        attn_qk(kvh, qb, c + 1, (c + 1) % 2)
                        _stop(3.5)
                    attn_pv(kvh, c, c % 2)
                    _stop(3.6)
            if r == 0 and not last:
                for e_ in range(2):
                    pr_ = slice(e_ * 64, (e_ + 1) * 64)
                    cpy(kcar[pr_, :, :], kTz[pr_, :, e_, 512:640], KT_ALL, ["kcar"])
                cpy(vcar[:, 0, :], vv[:, 8, :], VV_ALL, ["vcar"])
                cpy(vcar[0:64, 1, :], vv[0:64, 9, :], VV_ALL, ["vcar"])
            LM = int(os.environ.get("KLASTMASK", "15"))
            if last and (LM & 1):
                dma("sp", ok_d[r][:, :], kf32[0:64, :, :].rearrange("p h t -> p (h t)"),
                    [("kf32", kh) for kh in range(4)], [("ok", r)], final=True)
            if last and (LM & 2):
                dma("sp", ov_d[r][:, :], vout[:], ["vout"], [("ov", r)], final=True)

            _stop(4)
            P.fence("R", lambda e: e.memset(small[:, 8:9], 0.0))
            o_ = 0

            def carve(n, dt=F32):
                nonlocal o_
                a = R[:, o_:o_ + n]
                o_ += n
                return a if dt == F32 else a.bitcast(BF16)
            xraw = [carve(516), carve(516)]
            xsT = carve(2048).rearrange("p (c t) -> p c t", t=512)
            ybuf = carve(2048).rearrange("p (c t) -> p c t", t=512)
            BTf = carve(512)
            CTf = carve(512)
            zs = carve(1024, BF16).rearrange("p (c t) -> p c t", t=512)
            xbar = [carve(256, BF16), carve(256, BF16)]
            xbd = [carve(256, BF16), carve(256, BF16)]
            Btok = [carve(64, BF16), carve(64, BF16)]
            CBm = [carve(64), carve(64)]
            lseg = carve(512)
            LT = carve(512)
            GT = [carve(256, BF16), carve(256, BF16)]
            Ecs2 = [carve(512), carve(512)]
            Cdec = [carve(256, BF16), carve(256, BF16)]
            stbf = carve(256, BF16)
            cacc = carve(512)
            gsq = [carve(512), carve(512)]
            grs = carve(512)

            if first and r == 0:
                mset(stT[:], 0.0, [("st", g) for g in range(4)])
                mset(halo[:], 0.0, [("halo", f) for f in range(24)])
            elif r == 1:
                dma("sp", stT[:], sst_d[:, :], (), [("st", g) for g in range(4)])
                dma("sp", halo[:].rearrange("p a b -> p (a b)"), scv_d[:, :], (), [("halo", f) for f in range(24)])

            cvn = {"n": 0}

            def conv_chunk(bank, bk, fcg, out_ap, okey, out_is_act=True):
                i = cvn["n"] % 2
                cvn["n"] += 1
                xr = xraw[i]
                cpy(xr[:, 3:3 + T], bank[:, 0:T], [bk], [("R", "xraw", i)], eng="act")
                cpy(xr[:, 0:3], halo[:, fcg, :], [("halo", fcg)], [("R", "xrawh", i)])
                cw0 = PO["cw"][0] + fcg * 4
                cb0 = PO["cb"][0] + fcg
                XR = [("R", "xraw", i), ("R", "xrawh", i)]
                actf(cacc[:, 0:T], xr[:, 3:3 + T], AF.Identity, XR + ["par"], [("R", "cacc")],
                     bias=par[:, cb0:cb0 + 1], scale=par[:, cw0 + 3:cw0 + 4])
                for j in range(3):
                    stt(cacc[:, 0:T], xr[:, j:j + T], par[:, cw0 + j:cw0 + j + 1], cacc[:, 0:T], ALU.mult, ALU.add,
                        XR + ["par", ("R", "cacc")], [("R", "cacc")])
                cpy(halo[:, fcg, :], xr[:, T:T + 3], XR, [("halo", fcg)])
                actf(out_ap, cacc[:, 0:T], AF.Silu, [("R", "cacc")], [okey])

            for g in range(4):
                gh = g * 8
                Wt, wk_ = wload([(lambda w: wview(w, 32, 256)[:, :, 0:128], wcols(win_d, O_B + g * 128, 128)),
                                 (lambda w: wview(w, 32, 256)[:, :, 128:256], wcols(win_d, O_C + g * 128, 128))])
                wv = wview(Wt, 32, 256)
                for u, (dst, key_, fcg) in enumerate([(BTf, ("R", "BT"), 16 + g), (CTf, ("R", "CT"), 20 + g)]):
                    bank, bk = ps[u], ("ps", u)
                    for kc in range(32):
                        mm(bank[:, 0:T], wv[:, kc, u * 128:(u + 1) * 128], H[:, kc, 0:T], kc == 0, kc == 31,
                           wk_ + [("H", "h", kc)], [bk])
                    conv_chunk(bank, bk, fcg, dst[:, 0:T], key_)
                for s2 in range(2):
                    Wt, wk_ = wload([(lambda w: wview(w, 32, 256), wcols(win_d, O_X + g * 512 + s2 * 256, 256))])
                    wv = wview(Wt, 32, 256)
                    for u in range(2):
                        ch = s2 * 2 + u
                        b = ch % 3
                        bank, bk = ps[b], ("ps", b)
                        for kc in range(32):
                            mm(bank[:, 0:T], wv[:, kc, u * 128:(u + 1) * 128], H[:, kc, 0:T], kc == 0, kc == 31,
                               wk_ + [("H", "h", kc)], [bk])
                        conv_chunk(bank, bk, g * 4 + ch, xsT[:, ch, 0:T], ("R", "xsT", ch))
                        d0 = PO["dfeat"][0] + g * 4 + ch
                        P.act(lambda e, o=ybuf[:, ch, 0:T], i_=xsT[:, ch, 0:T], m=par[:, d0:d0 + 1]:
                              e.mul(out=o, in_=i_, mul=m), [("R", "xsT", ch), "par"], [("R", "ybuf", ch)])
                for s2 in range(2):
                    Wt, wk_ = wload([(lambda w: wview(w, 32, 256), wcols(win_d, O_Z + g * 512 + s2 * 256, 256))])
                    wv = wview(Wt, 32, 256)
                    for u in range(2):
                        ch = s2 * 2 + u
                        b = ch % 3
                        bank, bk = ps[b], ("ps", b)
                        for kc in range(32):
                            mm(bank[:, 0:T], wv[:, kc, u * 128:(u + 1) * 128], H[:, kc, 0:T], kc == 0, kc == 31,
                               wk_ + [("H", "h", kc)], [bk])
                        actf(zs[:, ch, 0:T], bank[:, 0:T], AF.Silu, [bk], [("R", "zs", ch)])
                XS = [("R", "xsT", ch) for ch in range(4)]
                YB = [("R", "ybuf", ch) for ch in range(4)]
                cpy(stbf[:, 0:512], stT[:, g * 512:(g + 1) * 512], [("st", g)], [("R", "stbf")], eng="act")
                def ssd_pre(c, g=g, gh=gh):
                    i = c % 2
                    cs_ = slice(c * L, (c + 1) * L)
                    dsl = slice(c * 32 + gh, c * 32 + gh + 8)
                    for fc in range(4):
                        tr(ps[3][0:L, fc * 128:(fc + 1) * 128], xsT[:, fc, cs_], ident, XS + ["cst"], [("ps", 3)])
                    tt(xbar[i][0:L, 0:512].rearrange("p (h d) -> p h d", d=64),
                       ps[3][0:L, 0:512].rearrange("p (h d) -> p h d", d=64),
                       dt_t[0:L, dsl].unsqueeze(2).to_broadcast([L, 8, 64]), ALU.mult, [("ps", 3), "dt"],
                       [("R", "xbar", i)])
                    tt(xbd[i][0:L, 0:512].rearrange("p (h d) -> p h d", d=64),
                       ps[3][0:L, 0:512].rearrange("p (h d) -> p h d", d=64),
                       dt_s[0:L, dsl].unsqueeze(2).to_broadcast([L, 8, 64]), ALU.mult, [("ps", 3), "dt_s"],
                       [("R", "xbd", i)])
                    tr(ps[4][0:L, 0:128], BTf[:, cs_], ident, [("R", "BT"), "cst"], [("ps", 4, "a")])
                    cpy(Btok[i][0:L, 0:128], ps[4][0:L, 0:128], [("ps", 4, "a")], [("R", "Btok", i)])
                    mm(ps[4][0:L, 128:128 + L], BTf[:, cs_], CTf[:, cs_], True, True, [("R", "BT"), ("R", "CT")],
                       [("ps", 4, "b")])
                    tt(CBm[i][0:L, 0:L], ps[4][0:L, 128:128 + L], U[0:L, 0:L], ALU.mult, [("ps", 4, "b"), "cst"],
                       [("R", "CBm", i)])
                    tt(lseg[0:L, 0:8 * L].rearrange("p (h s) -> p h s", s=L),
                       ident[0:L, 0:L].unsqueeze(1).to_broadcast([L, 8, L]),
                       dt_c[0:L, dsl].unsqueeze(2).to_broadcast([L, 8, L]), ALU.mult, ["dt_c", "cst"], [("R", "lseg")])
                    mm(ps[5][0:L, 0:8 * L], ones[0:L, 0:L], lseg[0:L, 0:8 * L], True, True, [("R", "lseg"), "cst"],
                       [("ps", 5)])
                    mm(ps[6][:, 0:8 * L], ones[0:L, 0:128], lseg[0:L, 0:8 * L], True, True, [("R", "lseg"), "cst"],
                       [("ps", 6)])
                    stt(LT[0:L, 0:8 * L].rearrange("p (h l) -> p h l", l=L),
                        ps[5][0:L, 0:8 * L].rearrange("p (h l) -> p h l", l=L), 0.0,
                        dt_c[0:L, dsl].unsqueeze(2).to_broadcast([L, 8, L]), ALU.add, ALU.subtract,
                        [("ps", 5), "dt_c"], [("R", "LT0")])
                    tt(LT[0:L, 0:8 * L].rearrange("p (h l) -> p h l", l=L),
                       LT[0:L, 0:8 * L].rearrange("p (h l) -> p h l", l=L),
                       U[0:L, 0:L].unsqueeze(1).to_broadcast([L, 8, L]), ALU.mult, [("R", "LT0"), "cst"], [("R", "LT1")])
                    actf(LT[0:L, 0:8 * L], LT[0:L, 0:8 * L], AF.Exp, [("R", "LT1")], [("R", "LT")])
                    tt(GT[i][0:L, 0:8 * L].rearrange("p (h l) -> p h l", l=L),
                       LT[0:L, 0:8 * L].rearrange("p (h l) -> p h l", l=L),
                       CBm[i][0:L, 0:L].unsqueeze(1).to_broadcast([L, 8, L]), ALU.mult, [("R", "LT"), ("R", "CBm", i)],
                       [("R", "GT", i)])
                    actf(Ecs2[i][:, 0:8 * L], ps[6][:, 0:8 * L], AF.Exp, [("ps", 6)], [("R", "Ecs", i)])
                    tt(Cdec[i][:, 0:8 * L].rearrange("p (h l) -> p h l", l=L),
                       Ecs2[i][:, 0:8 * L].rearrange("p (h l) -> p h l", l=L),
                       CTf[:, cs_].unsqueeze(1).to_broadcast([128, 8, L]), ALU.mult, [("R", "Ecs", i), ("R", "CT")],
                       [("R", "Cdec", i)])

                def ssd_post(c, g=g, gh=gh):
                    i = c % 2
                    cs_ = slice(c * L, (c + 1) * L)
                    dsl = slice(c * 32 + gh, c * 32 + gh + 8)
                    mm(ps[7][:, 0:512], Btok[i][0:L, 0:128], xbd[i][0:L, 0:512], True, True,
                       [("R", "Btok", i), ("R", "xbd", i)], [("ps", 7)])
                    for h in range(8):
                        e_ = h % 2
                        o = ps[4][e_ * 64:(e_ + 1) * 64, 256 + (h // 2) * L:256 + (h // 2 + 1) * L]
                        mm(o, xbar[i][0:L, h * 64:(h + 1) * 64], GT[i][0:L, h * L:(h + 1) * L], True, False,
                           [("R", "xbar", i), ("R", "GT", i)], [("ps", 4, "y")])
                        mm(o, stbf[:, h * 64:(h + 1) * 64], Cdec[i][:, h * L:(h + 1) * L], False, True,
                           [("R", "stbf"), ("R", "Cdec", i)], [("ps", 4, "y")])
                    tt(ybuf[:, :, cs_], ybuf[:, :, cs_], ps[4][:, 256:256 + 4 * L].rearrange("p (a t) -> p a t", t=L),
                       ALU.add, YB + [("ps", 4, "y")], YB)
                    tt(stT[:, g * 512:(g + 1) * 512].rearrange("p (h d) -> p h d", d=64),
                       stT[:, g * 512:(g + 1) * 512].rearrange("p (h d) -> p h d", d=64),
                       Ecs2[i][:, 0:8 * L].rearrange("p (h l) -> p h l", l=L)[:, :, L - 1:L].to_broadcast([128, 8, 64]),
                       ALU.mult, [("st", g), ("R", "Ecs", i)], [("st", g)])
                    tt(stT[:, g * 512:(g + 1) * 512], stT[:, g * 512:(g + 1) * 512], ps[7][:, 0:512], ALU.add,
                       [("st", g), ("ps", 7)], [("st", g)])
                    if c + 1 < NCH:
                        cpy(stbf[:, 0:512], stT[:, g * 512:(g + 1) * 512], [("st", g)], [("R", "stbf")], eng="act")

                ssd_pre(0)
                for c in range(NCH):
                    if c + 1 < NCH:
                        ssd_pre(c + 1)
                    ssd_post(c)
                tt(ybuf[:, :, 0:T], ybuf[:, :, 0:T], zs[:, :, 0:T], ALU.mult, YB + [("R", "zs", ch) for ch in range(4)],
                   YB)
                for fc in range(4):
                    i = fc % 2
                    actf(gsq[i][:, 0:T], ybuf[:, fc, 0:T], AF.Square, YB, [("R", "gsq", i)])
                    mm(ps[3][:, 0:T], ones, gsq[i][:, 0:T], fc == 0, fc == 3, [("R", "gsq", i), "cst"], [("ps", 3)])
                actf(grs[:, 0:T], ps[3][:, 0:T], AF.Sqrt, [("ps", 3), "epsc"], [("R", "grs0")], bias=epsc[:, 0:1],
                     scale=1.0 / 512.0)
                recip(grs[:, 0:T], grs[:, 0:T], [("R", "grs0")], [("R", "grs")])
                for fc in range(4):
                    n0 = PO["nw"][0] + g * 4 + fc
                    stt(mixT[:, g * 4 + fc, 0:T], ybuf[:, fc, 0:T], par[:, n0:n0 + 1], grs[:, 0:T], ALU.mult, ALU.mult,
                        YB + [("R", "grs"), "par"], [("M", "mix", g * 4 + fc)])
            if last and (LM & 4):
                dma("sp", ossd_d[r][:, :], stT[:], [("st", g) for g in range(4)], [("ossd", r)], final=True)
            if last and (LM & 8):
                dma("sp", ocv_d[r][:, :], halo[:].rearrange("p a b -> p (a b)"), [("halo", f) for f in range(24)],
                    [("ocv", r)], final=True)
            MIX_ALL = [("M", "mix", c_) for c_ in range(16)] + \
                      [("M", "mix", 16 + a, c_, e_) for a in range(16) for c_ in range(NCH) for e_ in range(2)]

            _stop(5)
            P.fence("R", lambda e: e.memset(small[:, 8:9], 0.0))
            for tc in range(NTC):
                rows = rows_of(tc)
                dma("sp", Rx[0:rows, tc, :], x_src[t0 + tc * 128:t0 + tc * 128 + rows, :], (), [("R", "x1", tc)])
            gpn = {"n": 0}

            def gate_piece(gi, cg):
                i = gpn["n"] % 2
                gpn["n"] += 1
                dma("sp", gp[i][:], grow_d[r:r + 1, gi, cg * 512:(cg + 1) * 512].to_broadcast([128, 512]), GROW,
                    [("gp", i)])
                return gp[i], ("gp", i)

            def proj_tokmajor(wd, kparts, lhs_of, lhs_keys, gi, cg, bset):
                banks = [bset * 4 + tc for tc in range(NTC)]
                nk_total = sum(n for _, n in kparts)
                kdone = 0
                for (kc0, nkc) in kparts:
                    Wt_, wk2 = wload([(lambda w, n=nkc: wview(w, n, 512), wcols(wd, cg * 512, 512, kc0, nkc))])
                    wv_ = wview(Wt_, nkc, 512)
                    for tc in range(NTC):
                        rows = rows_of(tc)
                        for kl in range(nkc):
                            kk = kdone + kl
                            mm(ps[banks[tc]][0:rows, :], lhs_of(kc0 + kl, tc, rows), wv_[:, kl, :], kk == 0,
                               kk == nk_total - 1, wk2 + lhs_keys, [("ps", banks[tc])])
                    kdone += nkc
                gpt, gk = gate_piece(gi, cg)
                for tc in range(NTC):
                    rows = rows_of(tc)
                    i = tc % 2
                    tt(sg[i][0:rows, :], ps[banks[tc]][0:rows, :], gpt[0:rows, :], ALU.mult, [("ps", banks[tc]), gk],
                       [("sg", i)])
                    tt(Rx[0:rows, tc, cg * 512:(cg + 1) * 512], Rx[0:rows, tc, cg * 512:(cg + 1) * 512], sg[i][0:rows, :],
                       ALU.add, [("sg", i), ("R", "x1", tc)], [("R", "x1", tc)])

            for cg in range(8):
                proj_tokmajor(wout_d, [(0, 16), (16, 16)],
                              lambda kc, tc, rows: mixT[:, kc, tc * 128:tc * 128 + rows], MIX_ALL, 0, cg, cg % 2)

            _stop(6)
            P.fence("M", lambda e: e.memset(small[:, 9:10], 0.0))
            Mf = M[:, :].bitcast(F32).rearrange("p (b f) -> p b f", f=4096)
            for tc in range(NTC):
                rows = rows_of(tc)
                i = tc % 2
                c_ = 4 + tc
                actf(Mf[0:rows, i, :], Rx[0:rows, tc, :], AF.Square, [("R", "x1", tc)], [("M", "xn", i), ("ssq", c_)],
                     accum=ssq[0:rows, c_:c_ + 1])
                actf(rstd[0:rows, c_:c_ + 1], ssq[0:rows, c_:c_ + 1], AF.Sqrt, [("ssq", c_), "epsc"], [("rs0", c_)],
                     bias=epsc[0:rows, 0:1], scale=1.0 / D)
                recip(rstd[0:rows, c_:c_ + 1], rstd[0:rows, c_:c_ + 1], [("rs0", c_)], [("rstd", c_)])
                ts(Mf[0:rows, i, :], Rx[0:rows, tc, :], rstd[0:rows, c_:c_ + 1], None, ALU.mult, None,
                   [("R", "x1", tc), ("rstd", c_), ("M", "xn", i)], [("M", "xn", i)])
                for fc in range(32):
                    b = fc % 4
                    bank, bk = ps[b], ("ps", b)
                    tr(bank[:, 0:rows], Mf[0:rows, i, fc * 128:(fc + 1) * 128], ident[0:rows, 0:rows],
                       [("M", "xn", i), "cst"], [bk])
                    if fc % 2 == 0:
                        actf(H[:, fc, tc * 128:tc * 128 + rows], bank[:, 0:rows], AF.Identity, [bk, "s2T"] + MT_ALL,
                             [("H", "h2", fc, tc)], bias=modT[:, 96 + fc, r:r + 1], scale=s2T[:, fc, r:r + 1])
                    else:
                        ts(H[:, fc, tc * 128:tc * 128 + rows], bank[:, 0:rows], s2T[:, fc, r:r + 1],
                           modT[:, 96 + fc, r:r + 1], ALU.mult, ALU.add, [bk, "s2T"] + MT_ALL, [("H", "h2", fc, tc)])
            H2 = lambda kc: [("H", "h2", kc, tc) for tc in range(NTC)]

            _stop(7)
            P.fence("M", lambda e: e.memset(small[:, 9:10], 0.0))
            actT = M[:, :].rearrange("p (c t) -> p c t", t=512)
            parts = [(0, 15), (15, 29), (29, 43)]
            for (q0, q1) in parts:
                nchk = (q1 - q0) * 2
                for q in range(q0, q1):
                    bs = (q % 2) * 4
                    Wg, wkg = wload([(lambda w: wview(w, 32, 256), wcols(wgu_d, q * 256, 256))])
                    wg = wview(Wg, 32, 256)
                    for u in range(2):
                        bank, bk = ps[bs + u], ("ps", bs + u)
                        for kc in range(32):
                            mm(bank[:, 0:T], wg[:, kc, u * 128:(u + 1) * 128], H[:, kc, 0:T], kc == 0, kc == 31,
                               wkg + H2(kc), [bk])
                    Wu, wku = wload([(lambda w: wview(w, 32, 256), wcols(wgu_d, DFF + q * 256, 256))])
                    wu = wview(Wu, 32, 256)
                    for u in range(2):
                        bank, bk = ps[bs + 2 + u], ("ps", bs + 2 + u)
                        for kc in range(32):
                            mm(bank[:, 0:T], wu[:, kc, u * 128:(u + 1) * 128], H[:, kc, 0:T], kc == 0, kc == 31,
                               wku + H2(kc), [bk])
                    for u in range(2):
                        lc = (q - q0) * 2 + u
                        actf(sg[u][:, 0:T], ps[bs + u][:, 0:T], AF.Silu, [("ps", bs + u)], [("sg", u)])
                        tt(actT[:, lc, 0:T], sg[u][:, 0:T], ps[bs + 2 + u][:, 0:T], ALU.mult,
                           [("sg", u), ("ps", bs + 2 + u)], [("M", "act", lc)])
                ACT_ALL = [("M", "act", lc) for lc in range(nchk)]
                kparts = []
                k_ = 0
                while k_ < nchk:
                    n_ = min(16, nchk - k_)
                    kparts.append((q0 * 2 + k_, n_))
                    k_ += n_
                for cg in range(8):
                    proj_tokmajor(wdn_d, kparts,
                                  lambda kc, tc, rows, q0=q0: actT[:, kc - q0 * 2, tc * 128:tc * 128 + rows],
                                  ACT_ALL, 1, cg, cg % 2)
            for tc in range(NTC):
                rows = rows_of(tc)
                dma("sp", y_dst[t0 + tc * 128:t0 + tc * 128 + rows, :], Rx[0:rows, tc, :], [("R", "x1", tc)],
                    [("y", r, t0, tc)], final=True)
            assert wstate["sid"] == NSLAB, wstate["sid"]
            wstate["pass0"] = False


        try:
            _stop(1)
            for ti in range(int(os.environ.get("KNT", n_prompt_tiles))):
                run_tile(0, ti * 512, 512, 64, ti == 0, ti == 3 or bool(os.environ.get("KLAST0")))
                _stop(9)
            if run_sample and not os.environ.get("KNOSAMPLE"):
                run_tile(1, 0, DEC, DEC, True, True)
        except _Stop:
            dma("sp", ys_d[0:2, 0:512], modrow[0][:], [("modrow", 0)], ["dbg"], final=True)

        with nc.Block() as block:
            P.build(nc, block, st)
        print("[kernel] ops=%d waits=%d %s" % (P.stats["ops"], P.stats["waits"], P.stats), flush=True)
    return nc


def _consts():
    c = np.zeros((128, NCONST), np.float32)
    p = np.arange(128)[:, None]
    j = np.arange(128)[None, :]
    c[:, 0:128] = (p == j)
    j64 = np.arange(64)[None, :]
    c[:, 128:192] = (p > j64)
    c[:, 192:256] = (p <= j64)
    c[:, 256:384] = ((p // 64) == (j // 64)) * (1.0 / 64.0)
    c[:, 384:448] = -(128.0 + j64 - p)
    c[:, 448:512] = -np.abs(j64 - p)
    c[:, 512:640] = 1.0
    c[:, 640:704] = c[:, 384:448]
    c[0:64, 640:704] = -1.0e9
    return c


def _params(i):
    f = lambda a: np.asarray(a, np.float32)
    fm = lambda v: f(v).reshape(-1, 128).T
    bc = lambda v: np.broadcast_to(f(v).reshape(1, -1), (128, f(v).size))
    cols = {
        "gmix": fm(i["g_mix"][0]), "gffn": fm(i["g_ffn"][0]), "bada": fm(i["b_ada"][0]),
        "cw": f(i["conv_w"][0]).reshape(4, 24, 128).transpose(2, 1, 0).reshape(128, 96),
        "cb": fm(i["conv_b"][0]),
        "dfeat": fm(np.repeat(f(i["d_skip"][0]), 64)), "nw": fm(i["ssd_norm_w"][0]),
        "wq": np.tile(f(i["q_norm_w"][0]), 2).reshape(128, 1), "wk": np.tile(f(i["k_norm_w"][0]), 2).reshape(128, 1),
        "dtb": bc(i["dt_bias"][0]), "alog": bc(i["a_log"][0]), "sinks": bc(i["sinks"][0]),
        "wqr": bc(i["q_norm_w"][0]), "wkr": bc(i["k_norm_w"][0]),
    }
    par = np.zeros((128, NPAR), np.float32)
    for n, (a, b) in PO.items():
        par[:, a:b] = cols[n]
    return par


_NC_CACHE = {}


def kernel(**inputs):
    i = {k: np.asarray(v) for k, v in inputs.items()}
    ncores = 8
    core_ids = list(range(ncores))
    dbg = os.environ.get("KDEBUG_CORES")
    if dbg:
        core_ids = list(range(int(dbg)))
    key = "main"
    if key not in _NC_CACHE:
        _NC_CACHE[key] = build_nc()
    nc = _NC_CACHE[key]
    par = _params(i)
    cst = _consts()
    f = lambda a: np.ascontiguousarray(a, dtype=np.float32)
    shared = {
        "par": par, "cst": cst, "badarow": f(i["b_ada"][0].reshape(1, -1)),
        "w_ada": f(i["w_ada"][0]), "w_in": f(i["w_in"][0]), "w_out": f(i["w_out"][0]),
        "w_gu": f(i["w_gate_up"][0]), "w_down": f(i["w_down"][0]),
    }
    in_maps = []
    for b in core_ids:
        ck = i["cache_k"][0, b]
        ckT = np.concatenate([ck.transpose(2, 1, 0)] * 2, axis=0)
        c2 = np.stack([i["c_prompt"][b], i["c_sample"][b]], axis=-1)
        m = dict(shared)
        m.update({
            "xp": f(i["x_prompt"][b]), "xs": f(i["x_sample"][b]),
            "sst": f(i["state_ssd"][0, b].reshape(2048, 128).T),
            "scv": f(i["state_conv"][0, b].reshape(3, 24, 128).transpose(2, 1, 0).reshape(128, 72)),
            "ckT": f(ckT.reshape(128, 512)), "cv": f(i["cache_v"][0, b].reshape(128, 256)),
            "c2T": f(c2.reshape(32, 128, 2).transpose(1, 0, 2).reshape(128, 64)),
        })
        in_maps.append(m)
    res = run_bass_kernel_spmd(nc, in_maps, core_ids=core_ids)
    rs = res.results
    nb = len(core_ids)
    B = 8
    yp = np.zeros((B, SEQ, D), np.float32)
    ys = np.zeros((B, DEC, D), np.float32)
    ssd_p = np.zeros((1, B, 32, 64, 128), np.float32)
    ssd_s = np.zeros((1, B, 32, 64, 128), np.float32)
    cv_p = np.zeros((1, B, 3, 3072), np.float32)
    cv_s = np.zeros((1, B, 3, 3072), np.float32)
    k_p = np.zeros((1, B, 128, 4, 64), np.float32)
    v_p = np.zeros((1, B, 128, 4, 64), np.float32)
    k_s = np.zeros((1, B, DEC, 4, 64), np.float32)
    v_s = np.zeros((1, B, DEC, 4, 64), np.float32)
    for b in range(nb):
        o = rs[b]
        yp[b] = o["yp"]
        ys[b] = o["ys"]
        ssd_p[0, b] = o["ossd_p"].T.reshape(32, 64, 128)
        ssd_s[0, b] = o["ossd_s"].T.reshape(32, 64, 128)
        cv_p[0, b] = o["ocv_p"].reshape(128, 24, 3).transpose(2, 1, 0).reshape(3, 3072)
        cv_s[0, b] = o["ocv_s"].reshape(128, 24, 3).transpose(2, 1, 0).reshape(3, 3072)
        k_p[0, b] = o["ok_p"].reshape(64, 4, 128).transpose(2, 1, 0)
        k_s[0, b] = o["ok_s"].reshape(64, 4, 128).transpose(2, 1, 0)[:DEC]
        v_p[0, b] = o["ov_p"].reshape(128, 4, 64)
        v_s[0, b] = o["ov_s"].reshape(128, 4, 64)[:DEC]
    return (yp, ys, ssd_p, cv_p, k_p, v_p, ssd_s, cv_s, k_s, v_s)
```

```python
import os
from contextlib import ExitStack

import numpy as np

import concourse.bass as bass
import concourse.mybir as mybir
from concourse.bass_utils import run_bass_kernel_spmd

F32 = mybir.dt.float32
BF16 = mybir.dt.bfloat16
AF = mybir.ActivationFunctionType
ALU = mybir.AluOpType

D = 4096
SEQ = 2048
DEC = 16
DIN = 7712
DFF = 11008
NMOD = 6
EPS = 1e-6
O_Z, O_X, O_B, O_C, O_DT, O_Q, O_K, O_V = 0, 2048, 4096, 4608, 5120, 5152, 7200, 7456
SLOPES = [float(2.0 ** (-8.0 * (h + 1) / 32.0)) for h in range(32)]

PO = {}
_o = 0
for _n, _w in [("gmix", 32), ("gffn", 32), ("bada", 192), ("cw", 96), ("cb", 24), ("dfeat", 16), ("nw", 16),
               ("wq", 1), ("wk", 1), ("dtb", 32), ("alog", 32), ("sinks", 32), ("wqr", 64), ("wkr", 64)]:
    PO[_n] = (_o, _o + _w)
    _o += _w
NPAR = _o
CO = {"ident": (0, 128), "SL": (128, 192), "U": (192, 256), "bd": (256, 384), "D1": (384, 448), "Dn": (448, 512),
      "ones": (512, 640), "D1m": (640, 704)}
NCONST = 704

ENGS = ("pe", "act", "dve", "pool", "sp")
SEM_LIMIT = 12000


class _Op:
    __slots__ = ("eng", "fn", "reads", "writes", "dma", "idx")

    def __init__(self, eng, fn, reads, writes, dma):
        self.eng = eng
        self.fn = fn
        self.reads = reads
        self.writes = writes
        self.dma = dma


class Prog:
    def __init__(self, n_rr=10):
        self.ops = []
        self.n_rr = n_rr
        self.finals = []
        self.phase = {}

    def add(self, eng, fn, reads=(), writes=(), dma=None):
        reads = list(reads)
        if eng != "pool":
            reads.append("__gb")
        op = _Op(eng, fn, tuple(reads), tuple(writes), dma)
        op.idx = len(self.ops)
        self.ops.append(op)
        return op

    def pe(self, fn, reads=(), writes=()):
        return self.add("pe", fn, reads, writes)

    def act(self, fn, reads=(), writes=()):
        return self.add("act", fn, reads, writes)

    def dve(self, fn, reads=(), writes=()):
        return self.add("dve", fn, reads, writes)

    def dma(self, eng, fn, reads=(), writes=(), group=None, final=False):
        op = self.add(eng, fn, reads, writes, dma=(group if group is not None else "__rr__"))
        if final:
            self.finals.append(op)
        return op

    def region(self, name):
        self.phase[name] = 0

    def fence(self, name, fn):
        op = _Op("dve", fn, (), ("__gb",), None)
        op.idx = len(self.ops)
        self.ops.append(op)

    def build(self, nc, block, st):
        ops = self.ops
        last_w = {}
        readers = {}
        deps = [None] * len(ops)
        for op in ops:
            d = set()
            for r in op.reads:
                w = last_w.get(r)
                if w is not None:
                    d.add(w)
            for w_ in op.writes:
                w = last_w.get(w_)
                if w is not None:
                    d.add(w)
                rl = readers.get(w_)
                if rl:
                    d.update(rl)
            d.discard(op.idx)
            latest = {}
            d2 = set()
            for j in d:
                p = ops[j]
                if p.dma is None:
                    if latest.get(p.eng, -1) < j:
                        latest[p.eng] = j
                else:
                    d2.add(j)
            d2.update(latest.values())
            d = d2
            dd = set()
            for j in d:
                p = ops[j]
                if p.eng == op.eng and p.dma is None and op.dma is None:
                    if op.eng == "pe":
                        continue
                    jr = -1
                    for r in op.reads:
                        w = last_w.get(r)
                        if w is not None and w != op.idx and ops[w].eng == op.eng and ops[w].dma is None and w > jr:
                            jr = w
                    if jr >= 0:
                        dd.add(jr)
                    continue
                dd.add(j)
            deps[op.idx] = dd
            for r in op.reads:
                readers.setdefault(r, []).append(op.idx)
            for w_ in op.writes:
                last_w[w_] = op.idx
                readers[w_] = []
        needs_sig = [False] * len(ops)
        for op in ops:
            for j in deps[op.idx]:
                needs_sig[j] = True
        for op in self.finals:
            needs_sig[op.idx] = True

        def newsem(name):
            return st.enter_context(nc.semaphore(name))

        eng_sem = {e: newsem("pg_%s_0" % e) for e in ENGS}
        eng_gen = {e: 0 for e in ENGS}
        eng_cnt = {e: 0 for e in ENGS}
        dma_sems = {}
        dma_cnt = {}
        rr_names = ["rr%d" % i for i in range(self.n_rr)]
        rr_last = {n: None for n in rr_names}
        rr_i = 0
        token = [None] * len(ops)
        extra = [None] * len(ops)
        semkey = {}

        def key(s):
            k = id(s)
            semkey[k] = s
            return k

        for op in ops:
            if op.dma is not None:
                g = op.dma
                if g == "__rr__":
                    g = rr_names[rr_i % len(rr_names)]
                    rr_i += 1
                    if rr_last[g] is not None:
                        extra[op.idx] = rr_last[g]
                if g not in dma_sems:
                    dma_sems[g] = newsem("pgd_" + g)
                    dma_cnt[g] = 0
                dma_cnt[g] += 16
                token[op.idx] = (key(dma_sems[g]), dma_cnt[g])
                if g in rr_last:
                    rr_last[g] = token[op.idx]
            elif needs_sig[op.idx]:
                if eng_cnt[op.eng] >= SEM_LIMIT:
                    eng_gen[op.eng] += 1
                    eng_sem[op.eng] = newsem("pg_%s_%d" % (op.eng, eng_gen[op.eng]))
                    eng_cnt[op.eng] = 0
                eng_cnt[op.eng] += 1
                token[op.idx] = (key(eng_sem[op.eng]), eng_cnt[op.eng])
        per_eng = {e: [] for e in ENGS}
        for op in ops:
            per_eng[op.eng].append(op)
        stats = {"waits": 0, "ops": len(ops)}
        stats["per_eng"] = {e: len(per_eng[e]) for e in ENGS}
        stats["gens"] = dict(eng_gen)
        stats["cnt"] = dict(eng_cnt)
        stats["dma"] = dict(dma_cnt)

        def emit_engine(ename, eng):
            waited = {}
            for op in per_eng[ename]:
                need = {}
                for j in deps[op.idx]:
                    k, v = token[j]
                    if waited.get(k, 0) >= v:
                        continue
                    if need.get(k, 0) < v:
                        need[k] = v
                if extra[op.idx] is not None:
                    k, v = extra[op.idx]
                    if waited.get(k, 0) < v and need.get(k, 0) < v:
                        need[k] = v
                for k, v in need.items():
                    eng.wait_ge(semkey[k], v)
                    waited[k] = v
                    stats["waits"] += 1
                ins = op.fn(eng)
                tk = token[op.idx]
                if tk is not None:
                    ins.then_inc(semkey[tk[0]], 16 if op.dma is not None else 1)
            if ename == "sp":
                for op in self.finals:
                    k, v = token[op.idx]
                    eng.wait_ge(semkey[k], v)

        @block.tensor
        def _(e):
            emit_engine("pe", e)

        @block.scalar
        def _(e):
            emit_engine("act", e)

        @block.vector
        def _(e):
            emit_engine("dve", e)

        @block.gpsimd
        def _(e):
            emit_engine("pool", e)

        @block.sync
        def _(e):
            emit_engine("sp", e)

        self.stats = stats


class _Stop(Exception):
    pass


KSTOP = float(os.environ.get("KSTOP", "99"))


def _stop(stage):
    if KSTOP <= stage:
        raise _Stop()


def build_nc(run_sample=True, n_prompt_tiles=4):
    nc = bass.Bass("TRN2", target_bir_lowering=False)
    din = lambda n, s: nc.dram_tensor(n, s, F32, kind="ExternalInput").ap()
    dout = lambda n, s: nc.dram_tensor(n, s, F32, kind="ExternalOutput").ap()
    xp_d = din("xp", [SEQ, D])
    xs_d = din("xs", [DEC, D])
    sst_d = din("sst", [128, 2048])
    scv_d = din("scv", [128, 72])
    ckT_d = din("ckT", [128, 512])
    cv_d = din("cv", [128, 256])
    c2T_d = din("c2T", [128, 64])
    par_d = din("par", [128, NPAR])
    cst_d = din("cst", [128, NCONST])
    wada_d = din("w_ada", [D, NMOD * D])
    win_d = din("w_in", [D, DIN])
    wout_d = din("w_out", [D, D])
    wgu_d = din("w_gu", [D, 2 * DFF])
    wdn_d = din("w_down", [DFF, D])
    yp_d = dout("yp", [SEQ, D])
    ys_d = dout("ys", [DEC, D])
    ossd_d = [dout("ossd_p", [128, 2048]), dout("ossd_s", [128, 2048])]
    ocv_d = [dout("ocv_p", [128, 72]), dout("ocv_s", [128, 72])]
    ok_d = [dout("ok_p", [64, 512]), dout("ok_s", [64, 512])]
    ov_d = [dout("ov_p", [128, 256]), dout("ov_s", [128, 256])]
    grow_d = nc.dram_tensor("grow", [2, 2, D], F32, kind="Internal").ap()
    NSLAB = 182
    _wsc = [nc.dram_tensor("wsc%d" % i_, [91, 128, 8192], BF16, kind="Internal").ap() for i_ in range(2)]

    class _WSC:
        def __getitem__(self, key):
            sid = key[0]
            return _wsc[sid // 91][sid % 91, :, :]
    wsc_d = _WSC()

    st = ExitStack()
    with st:
        sb = lambda n, s, dt: st.enter_context(nc.sbuf_tensor("sb_" + n, s, dt))
        par = sb("par", [128, NPAR], F32)
        cst = sb("cst", [128, NCONST], F32)
        R = sb("R", [128, 16384], F32)
        H = sb("H", [128, 32, 512], BF16)
        M = sb("M", [128, 16384], BF16)
        W = [sb("W0", [128, 8192], BF16), sb("W1", [128, 8192], BF16), sb("W2", [128, 8192], BF16)]
        modT = sb("modT", [128, 192, 2], F32)
        s1T = sb("s1T", [128, 32, 2], F32)
        s2T = sb("s2T", [128, 32, 2], F32)
        scT = sb("scT", [128, 32, 2], BF16)
        c2T = sb("c2Ts", [128, 32, 2], F32)
        onesb = sb("onesb", [128, 128], BF16)
        epsc = sb("epsc", [128, 1], F32)
        negM = sb("negM", [128, 1], F32)
        Ebc = sb("Ebc", [128, 32], F32)
        aneg = sb("aneg", [128, 32], F32)
        wq8 = sb("wq8", [128, 1], F32)
        small = sb("small", [128, 16], F32)
        kcar = sb("kcar", [128, 4, 128], BF16)
        vcar = sb("vcar", [128, 2, 256], BF16)
        halo = sb("halo", [128, 24, 3], F32)
        stT = sb("stT", [128, 2048], F32)
        kf32 = sb("kf32", [128, 4, 128], F32)
        vout = sb("vout", [128, 256], F32)
        ssq = sb("ssq", [128, 8], F32)
        rstd = sb("rstd", [128, 8], F32)
        gp = [sb("gp0", [128, 512], F32), sb("gp1", [128, 512], F32)]
        sg = [sb("sg0", [128, 512], F32), sb("sg1", [128, 512], F32)]
        modrow = [R[0:2, 0:512], R[0:2, 512:1024]]
        brow = [R[0:2, 1024:1536], R[0:2, 1536:2048]]
        ps = [st.enter_context(nc.psum_tensor("ps%d" % i, [128, 512], F32)) for i in range(8)]
        dt_x = R[0:64, 12288:12544]
        dt_e = R[0:64, 12544:12800]
        dt_t = R[0:64, 15332:15588]
        dt_a = R[0:64, 15588:15844]
        dt_s = R[0:64, 15844:16100]
        dt_c = R[0:64, 16100:16356]
        badarow_d = din("badarow", [1, NMOD * D])

        P = Prog()
        for rg in ("R", "H", "M"):
            P.region(rg)

        def cc(name):
            a, b = CO[name]
            return cst[:, a:b]

        def pc(name):
            a, b = PO[name]
            return par[:, a:b]

        ident = cc("ident")
        ones = cc("ones")
        SL = cc("SL")
        U = cc("U")
        bd = cc("bd")
        D1 = cc("D1")
        Dn = cc("Dn")
        D1m = cc("D1m")

        def mm(out, lhsT, rhs, start, stop, reads, writes):
            P.pe(lambda e: e.matmul(out, lhsT=lhsT, rhs=rhs, start=start, stop=stop), reads, writes)

        def tr(out, in_, idn, reads, writes):
            P.pe(lambda e: e.transpose(out, in_, idn), reads, writes)

        def actf(out, in_, func, reads, writes, bias=None, scale=None, accum=None):
            kw = {}
            if bias is not None:
                kw["bias"] = bias
            if scale is not None:
                kw["scale"] = scale
            if accum is not None:
                kw["accum_out"] = accum
            P.act(lambda e: e.activation(out=out, in_=in_, func=func, **kw), reads, writes)

        def tt(out, in0, in1, op, reads, writes):
            P.dve(lambda e: e.tensor_tensor(out=out, in0=in0, in1=in1, op=op), reads, writes)

        def ts(out, in0, s1, s2, op0, op1, reads, writes):
            if op1 is None:
                P.dve(lambda e: e.tensor_scalar(out=out, in0=in0, scalar1=s1, scalar2=None, op0=op0), reads, writes)
            else:
                P.dve(lambda e: e.tensor_scalar(out=out, in0=in0, scalar1=s1, scalar2=s2, op0=op0, op1=op1),
                      reads, writes)

        def stt(out, in0, scalar, in1, op0, op1, reads, writes):
            P.dve(lambda e: e.scalar_tensor_tensor(out=out, in0=in0, scalar=scalar, in1=in1, op0=op0, op1=op1),
                  reads, writes)

        def cpy(out, in_, reads, writes, eng="dve"):
            if eng == "dve":
                P.dve(lambda e: e.tensor_copy(out=out, in_=in_), reads, writes)
            else:
                P.act(lambda e: e.copy(out=out, in_=in_), reads, writes)

        def recip(out, in_, reads, writes):
            P.dve(lambda e: e.reciprocal(out=out, in_=in_), reads, writes)

        def mset(ap, val, writes):
            P.dve(lambda e: e.memset(ap, val), (), writes)

        def dma(eng, out, in_, reads, writes, group=None, final=False, slow=False):
            if slow:
                P.dma(eng, lambda e: e.dma_start(out=out, in_=in_, allow_slow_non_contiguous=True), reads, writes,
                      group, final)
            else:
                P.dma(eng, lambda e: e.dma_start(out=out, in_=in_), reads, writes, group, final)

        wstate = {"n": 0, "sid": None, "pass0": True, "nslab": None}

        def wload(pieces):
            slot = wstate["n"] % 3
            wstate["n"] += 1
            sid = wstate["sid"]
            if sid is not None:
                wstate["sid"] += 1
            if sid is None or wstate["pass0"]:
                keys = []
                for i, (dstf, src) in enumerate(pieces):
                    k = ("w", slot, i)
                    keys.append(k)
                    dma("pool", dstf(W[slot]), src, (), [k], group="w%d" % slot)
                if sid is not None:
                    dma("sp", wsc_d[sid, :, :], W[slot][:, :], keys, [("wsc", sid)], group="ws%d" % slot)
                return W[slot], keys
            k = ("w", slot, 0)
            dma("pool", W[slot][:, :], wsc_d[sid, :, :], [("wsc", sid)], [k], group="w%d" % slot)
            return W[slot], [k]

        def wcols(wd, c0, n, kc0=0, nkc=32):
            return wd[kc0 * 128:(kc0 + nkc) * 128, c0:c0 + n].rearrange("(kc p) m -> p kc m", p=128)

        def wview(Wt, nkc, n):
            return Wt[:, 0:nkc * n].rearrange("p (kc m) -> p kc m", m=n)

        dma("sp", par[:], par_d[:, :], (), ["par"])
        dma("sp", cst[:], cst_d[:, :], (), ["cst"])
        dma("sp", c2T[:].rearrange("p a b -> p (a b)"), c2T_d[:, :], (), ["c2T"])
        mset(epsc[:], EPS, ["epsc"])
        cpy(onesb[:], ones, ["cst"], ["onesb"])
        actf(scT[:].rearrange("p a b -> p (a b)"), c2T[:].rearrange("p a b -> p (a b)"), AF.Silu, ["c2T"], ["scT"])
        P.act(lambda e: e.mul(out=wq8[:], in_=pc("wq"), mul=0.125), ["par"], ["wq8"])
        actf(aneg[:], pc("alog"), AF.Exp, ["par"], ["aneg0"])
        ts(aneg[:], aneg[:], -1.0, None, ALU.mult, None, ["aneg0"], ["aneg"])
        P.dve(lambda e: e.tensor_reduce(out=small[:, 2:3], in_=pc("wqr"), axis=mybir.AxisListType.X, op=ALU.max,
                                        apply_absolute_value=True), ["par"], ["sm2"])
        P.dve(lambda e: e.tensor_reduce(out=small[:, 3:4], in_=pc("wkr"), axis=mybir.AxisListType.X, op=ALU.max,
                                        apply_absolute_value=True), ["par"], ["sm3"])
        tt(small[:, 4:5], small[:, 2:3], small[:, 3:4], ALU.mult, ["sm2", "sm3"], ["sm4"])
        ts(negM[:], small[:, 4:5], -8.0, None, ALU.mult, None, ["sm4"], ["negM"])
        actf(Ebc[:], pc("sinks"), AF.Exp, ["par", "negM"], ["Ebc"], bias=negM[:, 0:1], scale=1.0)

        for cg in range(48):
            bank = ps[cg % 2]
            bk = ("ps", cg % 2)
            for hf in range(2):
                Wt, wk_ = wload([(lambda w: wview(w, 16, 512), wcols(wada_d, cg * 512, 512, hf * 16, 16))])
                wv = wview(Wt, 16, 512)
                for kl in range(16):
                    kc = hf * 16 + kl
                    mm(bank[0:2, :], scT[:, kc, :], wv[:, kl, :], kc == 0, kc == 31, wk_ + ["scT"], [bk])
            i = cg % 2
            cpy(modrow[i][:], bank[0:2, :], [bk], [("modrow", i)], eng="act")
            which = cg // 8
            if which in (2, 5):
                gi = 0 if which == 2 else 1
                c0 = (cg % 8) * 512
                dma("sp", brow[i][:], badarow_d[0:1, cg * 512:(cg + 1) * 512].to_broadcast([2, 512]), (),
                    [("brow", i)])
                tt(modrow[i][:], modrow[i][:], brow[i][:], ALU.add, [("modrow", i), ("brow", i)], [("modrow2", i)])
                dma("sp", grow_d[:, gi, c0:c0 + 512], modrow[i][:], [("modrow2", i)], [("grow", gi, cg % 8)])
            else:
                tb = ps[2 + cg % 2]
                tbk = ("ps", 2 + cg % 2)
                for j in range(4):
                    tr(tb[:, j * 2:j * 2 + 2], modrow[i][:, j * 128:(j + 1) * 128], ident[0:2, 0:2],
                       [("modrow", i), "cst"], [tbk])
                a0 = PO["bada"][0] + cg * 4
                tt(modT[:, cg * 4:cg * 4 + 4, :], tb[:, 0:8].rearrange("p (j r) -> p j r", r=2),
                   par[:, a0:a0 + 4].unsqueeze(2).to_broadcast([128, 4, 2]), ALU.add, [tbk, "par"], [("modT", cg)])
        MT_ALL = [("modT", cg) for cg in range(48) if cg // 8 not in (2, 5)]
        stt(s1T[:], modT[:, 32:64, :], 1.0, pc("gmix").unsqueeze(2).to_broadcast([128, 32, 2]), ALU.add, ALU.mult,
            MT_ALL + ["par"], ["s1T"])
        stt(s2T[:], modT[:, 128:160, :], 1.0, pc("gffn").unsqueeze(2).to_broadcast([128, 32, 2]), ALU.add, ALU.mult,
            MT_ALL + ["par"], ["s2T"])
        GROW = [("grow", gi, j) for gi in range(2) for j in range(8)]

        def run_tile(r, t0, T, L, first, last):
            NCH = T // L
            NTC = (T + 127) // 128
            wstate["sid"] = 0
            x_src = xp_d if r == 0 else xs_d
            y_dst = yp_d if r == 0 else ys_d
            Rx = R[:, :].rearrange("p (tc f) -> p tc f", f=4096)
            rows_of = lambda tc: min(128, T - tc * 128)

            P.fence("R", lambda e: e.memset(small[:, 8:9], 0.0))
            Mj = M[:, 0:4096]
            for tc in range(NTC):
                rows = rows_of(tc)
                dma("sp", Rx[0:rows, tc, :], x_src[t0 + tc * 128:t0 + tc * 128 + rows, :], (), [("R", "x", tc)])
                actf(Mj[0:rows, :], Rx[0:rows, tc, :], AF.Square, [("R", "x", tc)], [("M", "junk"), ("ssq", tc)],
                     accum=ssq[0:rows, tc:tc + 1])
                actf(rstd[0:rows, tc:tc + 1], ssq[0:rows, tc:tc + 1], AF.Sqrt, [("ssq", tc), "epsc"], [("rs0", tc)],
                     bias=epsc[0:rows, 0:1], scale=1.0 / D)
                recip(rstd[0:rows, tc:tc + 1], rstd[0:rows, tc:tc + 1], [("rs0", tc)], [("rstd", tc)])
                ts(Rx[0:rows, tc, :], Rx[0:rows, tc, :], rstd[0:rows, tc:tc + 1], None, ALU.mult, None,
                   [("R", "x", tc), ("rstd", tc)], [("R", "xn", tc)])
            for fc in range(32):
                b = fc % 2
                bank = ps[b]
                bk = ("ps", b)
                for tc in range(NTC):
                    rows = rows_of(tc)
                    tr(bank[:, tc * 128:tc * 128 + rows], Rx[0:rows, tc, fc * 128:(fc + 1) * 128],
                       ident[0:rows, 0:rows], [("R", "xn", tc), "cst"], [bk])
                if fc % 2 == 0:
                    actf(H[:, fc, 0:T], bank[:, 0:T], AF.Identity, [bk, "s1T"] + MT_ALL, [("H", "h", fc)],
                         bias=modT[:, 0 + fc, r:r + 1], scale=s1T[:, fc, r:r + 1])
                else:
                    ts(H[:, fc, 0:T], bank[:, 0:T], s1T[:, fc, r:r + 1], modT[:, 0 + fc, r:r + 1], ALU.mult, ALU.add,
                       [bk, "s1T"] + MT_ALL, [("H", "h", fc)])
            HALL = [("H", "h", fc) for fc in range(32)]
            _stop(2)

            P.fence("R", lambda e: e.memset(small[:, 8:9], 0.0))
            Rb = R[:, :].bitcast(BF16)
            kTz = Rb[:, 0:5120].rearrange("p (h e t) -> p h e t", e=2, t=640)
            vv = Rb[:, 5120:7680].rearrange("p (s c) -> p s c", c=256)
            qT = Rb[:, 7680:11776].rearrange("p (b c t) -> p b c t", b=2, t=512)
            P1 = [Rb[:, 11776:12288], Rb[:, 12288:12800]]
            P2 = [Rb[:, 12800:13312], Rb[:, 13312:13824]]
            fo = 6912
            sq = [R[:, fo:fo + 512], R[:, fo + 512:fo + 1024]]
            rs = [R[:, fo + 1024:fo + 1536], R[:, fo + 1536:fo + 2048]]
            sb1 = [R[:, fo + 2048:fo + 2560], R[:, fo + 2560:fo + 3072]]
            sb2 = [R[:, fo + 3072:fo + 3584], R[:, fo + 3584:fo + 4096]]
            rec = R[:, fo + 4096:fo + 4608]
            ktmp = R[:, fo + 4608:fo + 5120]
            vtmp = R[:, fo + 5120:fo + 5376]
            mixT = M[:, :].rearrange("p (c t) -> p c t", t=512)

            def qknorm(bank, bk, i, wcol, wkeys):
                actf(sq[i][:, 0:T], bank[:, 0:T], AF.Square, [bk], [("R", "sq", i)])
                mm(ps[3][:, 0:T], bd, sq[i][:, 0:T], True, True, [("R", "sq", i), "cst"], [("ps", 3)])
                actf(rs[i][:, 0:T], ps[3][:, 0:T], AF.Sqrt, [("ps", 3), "epsc"], [("R", "rs0", i)],
                     bias=epsc[:, 0:1], scale=1.0)
                recip(rs[i][:, 0:T], rs[i][:, 0:T], [("R", "rs0", i)], [("R", "rs", i)])

            P.dve(lambda e: e.memset(Rb[:, 0:5120], 0.0), (), [("R", "kTz0")])
            if r == 1:
                dma("sp", ktmp[:, 0:512], ckT_d[:, :], (), [("R", "ktmp")])
                for e_ in range(2):
                    pr_ = slice(e_ * 64, (e_ + 1) * 64)
                    cpy(kTz[pr_, :, e_, 0:128], ktmp[pr_, 0:512].rearrange("p (h t) -> p h t", t=128),
                        [("R", "ktmp"), ("R", "kTz0")], [("R", "kTc")])
                dma("sp", vtmp[:, 0:256], cv_d[:, :], (), [("R", "vtmp")])
                cpy(vv[:, 0, :], vtmp[:, 0:256], [("R", "vtmp")], [("R", "vv", 0)])
            elif not first:
                for e_ in range(2):
                    pr_ = slice(e_ * 64, (e_ + 1) * 64)
                    cpy(kTz[pr_, :, e_, 0:128], kcar[pr_, :, :], ["kcar", ("R", "kTz0")], [("R", "kTc")])
                cpy(vv[:, 0, :], vcar[:, 0, :], ["vcar"], [("R", "vv", 0)])
                cpy(vv[0:64, 1, :], vcar[0:64, 1, :], ["vcar"], [("R", "vv", 1, "lo")])
            have_prev = (r == 1) or (not first)

            for kh in range(4):
                if kh % 2 == 0:
                    pcs = []
                    for hh in range(2):
                        for u_ in range(2):
                            pcs.append((lambda w, a=hh * 128 + u_ * 64: wview(w, 32, 256)[:, :, a:a + 64],
                                        wcols(win_d, O_K + (kh + hh) * 64, 64)))
                    Wt, wk_ = wload(pcs)
                    wv = wview(Wt, 32, 256)
                b = kh % 3
                bank, bk = ps[b], ("ps", b)
                for kc in range(32):
                    mm(bank[:, 0:T], wv[:, kc, (kh % 2) * 128:(kh % 2 + 1) * 128], H[:, kc, 0:T], kc == 0, kc == 31,
                       wk_ + [("H", "h", kc)], [bk])
                i = kh % 2
                qknorm(bank, bk, i, None, None)
                stt(ktmp[:, 0:T], bank[:, 0:T], pc("wk"), rs[i][:, 0:T], ALU.mult, ALU.mult,
                    [bk, ("R", "rs", i), "par"], [("R", "ktmp")])
                for e_ in range(2):
                    pr_ = slice(e_ * 64, (e_ + 1) * 64)
                    cpy(kTz[pr_, kh, e_, 128:128 + T], ktmp[pr_, 0:T], [("R", "ktmp"), ("R", "kTz0")],
                        [("R", "kT", kh)], eng="act")
                if last:
                    nk = min(128, T)
                    cpy(kf32[:, kh, 0:nk], ktmp[:, T - nk:T], [("R", "ktmp")], [("kf32", kh)], eng="act")
            Wt, wk_ = wload([(lambda w: wview(w, 32, 256), wcols(win_d, O_V, 256))])
            wv = wview(Wt, 32, 256)
            vb = 0
            for j in range(NCH):
                mrows = 2 * L if j < NCH - 1 else L
                b = vb % 3
                vb += 1
                bank, bk = ps[b], ("ps", b)
                for kc in range(32):
                    mm(bank[0:mrows, 0:256], H[:, kc, j * L:j * L + mrows], wv[:, kc, :], kc == 0, kc == 31,
                       wk_ + [("H", "h", kc)], [bk])
                cpy(vv[0:mrows, 2 + j, :], bank[0:mrows, 0:256], [bk], [("R", "vv", 2 + j)], eng="act")
                if last and ((r == 0 and j == NCH - 2) or (r == 1 and j == 0)):
                    cpy(vout[0:mrows, :], bank[0:mrows, 0:256], [bk], ["vout"], eng="act")
            if r == 0:
                if first:
                    P.dve(lambda e: e.memset(vv[0:64, 1, :], 0.0), (), [("R", "vv", 1, "lo")])
                b = vb % 3
                bank, bk = ps[b], ("ps", b)
                for kc in range(32):
                    mm(bank[64:128, 0:256], H[:, kc, 0:64], wv[:, kc, :], kc == 0, kc == 31,
                       wk_ + [("H", "h", kc)], [bk])
                cpy(vv[64:128, 1, :], bank[64:128, 0:256], [bk], [("R", "vv", 1, "hi")], eng="act")
            Wt, wk_ = wload([(lambda w: wview(w, 32, 32), wcols(win_d, O_DT, 32))])
            wv = wview(Wt, 32, 32)
            dtb = ps[7]
            for j in range(NCH):
                for kc in range(32):
                    mm(dtb[0:L, j * 32:(j + 1) * 32], H[:, kc, j * L:(j + 1) * L], wv[:, kc, :], kc == 0, kc == 31,
                       wk_ + [("H", "h", kc)], [("ps", 7)])
            n32 = NCH * 32
            dtv = lambda t_: t_[0:L, 0:n32].rearrange("p (c h) -> p c h", h=32)
            tt(dtv(dt_x), dtv(dtb), pc("dtb")[0:L, :].unsqueeze(1).to_broadcast([L, NCH, 32]), ALU.add,
               [("ps", 7), "par"], ["dt_x"])
            ts(dt_e[0:L, 0:n32], dt_x[0:L, 0:n32], 30.0, None, ALU.min, None, ["dt_x"], ["dt_e0"])
            actf(dt_e[0:L, 0:n32], dt_e[0:L, 0:n32], AF.Exp, ["dt_e0"], ["dt_e1"])
            actf(dt_e[0:L, 0:n32], dt_e[0:L, 0:n32], AF.Ln, ["dt_e1"], ["dt_e2"], bias=1.0)
            tt(dt_t[0:L, 0:n32], dt_e[0:L, 0:n32], dt_x[0:L, 0:n32], ALU.max, ["dt_e2", "dt_x"], ["dt"])
            tt(dtv(dt_a), dtv(dt_t), aneg[0:L, :].unsqueeze(1).to_broadcast([L, NCH, 32]), ALU.mult,
               ["dt", "aneg"], ["dt_a"])
            mm(ps[7][0:L, 256:256 + n32], SL[0:L, 0:L], dt_a[0:L, 0:n32], True, True, ["dt_a", "cst"], [("ps", 7, "b")])
            actf(dt_s[0:L, 0:n32], ps[7][0:L, 256:256 + n32], AF.Exp, [("ps", 7, "b")], ["dte"])
            tt(dt_s[0:L, 0:n32], dt_s[0:L, 0:n32], dt_t[0:L, 0:n32], ALU.mult, ["dte", "dt"], ["dt_s"])
            mm(ps[7][0:L, 256:256 + n32], U[0:L, 0:L], dt_a[0:L, 0:n32], True, True, ["dt_a", "cst"], [("ps", 7, "b")])
            cpy(dt_c[0:L, 0:n32], ps[7][0:L, 256:256 + n32], [("ps", 7, "b")], ["dt_c"], eng="act")

            _stop(3)

            def q_group(kvh, qb):
                for s2 in range(2):
                    c0 = O_Q + kvh * 512 + s2 * 256
                    Wt_, wk2 = wload([(lambda w: wview(w, 32, 256), wcols(win_d, c0, 256))])
                    wv_ = wview(Wt_, 32, 256)
                    for u in range(2):
                        ch = s2 * 2 + u
                        b = ch % 3
                        bank, bk = ps[b], ("ps", b)
                        for kc in range(32):
                            mm(bank[:, 0:T], wv_[:, kc, u * 128:(u + 1) * 128], H[:, kc, 0:T], kc == 0, kc == 31,
                               wk2 + [("H", "h", kc)], [bk])
                        i = ch % 2
                        qknorm(bank, bk, i, None, None)
                        stt(qT[:, qb, ch, 0:T], bank[:, 0:T], wq8[:, 0:1], rs[i][:, 0:T], ALU.mult, ALU.mult,
                            [bk, ("R", "rs", i), "wq8"], [("R", "qT", qb, ch)])

            KT_ALL = [("R", "kT", kh) for kh in range(4)] + [("R", "kTc"), ("R", "kTz0")]
            VV_ALL = [("R", "vv", s) for s in range(10)] + [("R", "vv", 1, "lo"), ("R", "vv", 1, "hi")]

            def attn_qk(kvh, qb, c, i):
                v2 = have_prev or c >= 2
                v1 = have_prev or c >= 1
                p0 = 0
                D1u = D1 if v2 else D1m
                qk = [("R", "qT", qb, ch) for ch in range(4)]
                for h in range(8):
                    e_ = h % 2
                    qs = qT[:, qb, h // 2, c * L:(c + 1) * L]
                    if v1:
                        kbase = c * L if r == 0 else 0
                        mm(ps[4][p0:128, h * L:(h + 1) * L], kTz[:, kvh, e_, kbase + p0:kbase + 128], qs,
                           True, True, KT_ALL + qk, [("ps", 4)])
                    mm(ps[5][0:L, h * L:(h + 1) * L], kTz[:, kvh, e_, 128 + c * L:128 + (c + 1) * L], qs,
                       True, True, KT_ALL + qk, [("ps", 5)])
                for h in range(8):
                    sl = SLOPES[kvh * 8 + h]
                    if v1:
                        stt(sb1[i][p0:128, h * L:(h + 1) * L], D1u[p0:128, 0:L], sl, ps[4][p0:128, h * L:(h + 1) * L],
                            ALU.mult, ALU.add, [("ps", 4), "cst"], [("R", "sb1", i)])
                    stt(sb2[i][0:L, h * L:(h + 1) * L], Dn[0:L, 0:L], sl, ps[5][0:L, h * L:(h + 1) * L],
                        ALU.mult, ALU.add, [("ps", 5), "cst"], [("R", "sb2", i)])
                if v1:
                    actf(P1[i][p0:128, 0:8 * L], sb1[i][p0:128, 0:8 * L], AF.Exp, [("R", "sb1", i), "negM"],
                         [("R", "P1", i)], bias=negM[p0:128, 0:1], scale=1.0)
                actf(P2[i][0:L, 0:8 * L], sb2[i][0:L, 0:8 * L], AF.Exp, [("R", "sb2", i), "negM"], [("R", "P2", i)],
                     bias=negM[0:L, 0:1], scale=1.0)

            def attn_pv(kvh, c, i):
                v2 = have_prev or c >= 2
                v1 = have_prev or c >= 1
                p0 = 0
                for h in range(8):
                    e_ = h % 2
                    o = ps[7][e_ * 64:(e_ + 1) * 64, (h // 2) * L:(h // 2 + 1) * L]
                    if v1:
                        mm(o, vv[p0:128, c, kvh * 64:(kvh + 1) * 64], P1[i][p0:128, h * L:(h + 1) * L], True, False,
                           VV_ALL + [("R", "P1", i)], [("ps", 7)])
                    mm(o, vv[0:L, c + 2, kvh * 64:(kvh + 1) * 64], P2[i][0:L, h * L:(h + 1) * L], not v1, True,
                       VV_ALL + [("R", "P2", i)], [("ps", 7)])
                if v1:
                    mm(ps[6][:, 0:8 * L], onesb[p0:128, :], P1[i][p0:128, 0:8 * L], True, False,
                       [("R", "P1", i), "onesb"], [("ps", 6)])
                mm(ps[6][:, 0:8 * L], onesb[0:L, :], P2[i][0:L, 0:8 * L], not v1, True, [("R", "P2", i), "onesb"],
                   [("ps", 6)])
                tt(rec[:, 0:8 * L].rearrange("p (h t) -> p h t", t=L), ps[6][:, 0:8 * L].rearrange("p (h t) -> p h t", t=L),
                   Ebc[:, kvh * 8:(kvh + 1) * 8].unsqueeze(2).to_broadcast([128, 8, L]), ALU.add, [("ps", 6), "Ebc"],
                   [("R", "rec0")])
                recip(rec[:, 0:8 * L], rec[:, 0:8 * L], [("R", "rec0")], [("R", "rec")])
                for e_ in range(2):
                    pr = slice(e_ * 64, (e_ + 1) * 64)
                    tt(mixT[pr, 16 + kvh * 4:16 + kvh * 4 + 4, c * L:(c + 1) * L],
                       ps[7][pr, 0:4 * L].rearrange("p (a t) -> p a t", t=L),
                       rec[pr, 0:8 * L].rearrange("p (a two t) -> p a two t", two=2, t=L)[:, :, e_, :], ALU.mult,
                       [("ps", 7), ("R", "rec")], [("M", "mix", 16 + kvh * 4, c, e_)])

            q_group(0, 0)
            _stop(3.2)
            for kvh in range(4):
                if kvh + 1 < 4:
                    q_group(kvh + 1, (kvh + 1) % 2)
                qb = kvh % 2
                attn_qk(kvh, qb, 0, 0)
                _stop(3.4)
                for c in range(NCH):
                    if c + 1 < NCH:
                        attn_qk(kvh, qb, c + 1, (c + 1) % 2)
                        _stop(3.5)
                    attn_pv(kvh, c, c % 2)
                    _stop(3.6)
            if r == 0 and not last:
                for e_ in range(2):
                    pr_ = slice(e_ * 64, (e_ + 1) * 64)
                    cpy(kcar[pr_, :, :], kTz[pr_, :, e_, 512:640], KT_ALL, ["kcar"])
                cpy(vcar[:, 0, :], vv[:, 8, :], VV_ALL, ["vcar"])
                cpy(vcar[0:64, 1, :], vv[0:64, 9, :], VV_ALL, ["vcar"])
            LM = int(os.environ.get("KLASTMASK", "15"))
            if last and (LM & 1):
                dma("sp", ok_d[r][:, :], kf32[0:64, :, :].rearrange("p h t -> p (h t)"),
                    [("kf32", kh) for kh in range(4)], [("ok", r)], final=True)
            if last and (LM & 2):
                dma("sp", ov_d[r][:, :], vout[:], ["vout"], [("ov", r)], final=True)

            _stop(4)
            P.fence("R", lambda e: e.memset(small[:, 8:9], 0.0))
            o_ = 0

            def carve(n, dt=F32):
                nonlocal o_
                a = R[:, o_:o_ + n]
                o_ += n
                return a if dt == F32 else a.bitcast(BF16)
            xraw = [carve(516), carve(516)]
            xsT = carve(2048).rearrange("p (c t) -> p c t", t=512)
            ybuf = carve(2048).rearrange("p (c t) -> p c t", t=512)
            BTf = carve(512)
            CTf = carve(512)
            zs = carve(1024, BF16).rearrange("p (c t) -> p c t", t=512)
            xbar = [carve(256, BF16), carve(256, BF16)]
            xbd = [carve(256, BF16), carve(256, BF16)]
            Btok = [carve(64, BF16), carve(64, BF16)]
            CBm = [carve(64), carve(64)]
            lseg = carve(512)
            LT = carve(512)
            GT = [carve(256, BF16), carve(256, BF16)]
            Ecs2 = [carve(512), carve(512)]
            Cdec = [carve(256, BF16), carve(256, BF16)]
            stbf = carve(256, BF16)
            cacc = carve(512)
            gsq = [carve(512), carve(512)]
            grs = carve(512)

            if first and r == 0:
                mset(stT[:], 0.0, [("st", g) for g in range(4)])
                mset(halo[:], 0.0, [("halo", f) for f in range(24)])
            elif r == 1:
                dma("sp", stT[:], sst_d[:, :], (), [("st", g) for g in range(4)])
                dma("sp", halo[:].rearrange("p a b -> p (a b)"), scv_d[:, :], (), [("halo", f) for f in range(24)])

            cvn = {"n": 0}

            def conv_chunk(bank, bk, fcg, out_ap, okey, out_is_act=True):
                i = cvn["n"] % 2
                cvn["n"] += 1
                xr = xraw[i]
                cpy(xr[:, 3:3 + T], bank[:, 0:T], [bk], [("R", "xraw", i)], eng="act")
                cpy(xr[:, 0:3], halo[:, fcg, :], [("halo", fcg)], [("R", "xrawh", i)])
                cw0 = PO["cw"][0] + fcg * 4
                cb0 = PO["cb"][0] + fcg
                XR = [("R", "xraw", i), ("R", "xrawh", i)]
                actf(cacc[:, 0:T], xr[:, 3:3 + T], AF.Identity, XR + ["par"], [("R", "cacc")],
                     bias=par[:, cb0:cb0 + 1], scale=par[:, cw0 + 3:cw0 + 4])
                for j in range(3):
                    stt(cacc[:, 0:T], xr[:, j:j + T], par[:, cw0 + j:cw0 + j + 1], cacc[:, 0:T], ALU.mult, ALU.add,
                        XR + ["par", ("R", "cacc")], [("R", "cacc")])
                cpy(halo[:, fcg, :], xr[:, T:T + 3], XR, [("halo", fcg)])
                actf(out_ap, cacc[:, 0:T], AF.Silu, [("R", "cacc")], [okey])

            for g in range(4):
                gh = g * 8
                Wt, wk_ = wload([(lambda w: wview(w, 32, 256)[:, :, 0:128], wcols(win_d, O_B + g * 128, 128)),
                                 (lambda w: wview(w, 32, 256)[:, :, 128:256], wcols(win_d, O_C + g * 128, 128))])
                wv = wview(Wt, 32, 256)
                for u, (dst, key_, fcg) in enumerate([(BTf, ("R", "BT"), 16 + g), (CTf, ("R", "CT"), 20 + g)]):
                    bank, bk = ps[u], ("ps", u)
                    for kc in range(32):
                        mm(bank[:, 0:T], wv[:, kc, u * 128:(u + 1) * 128], H[:, kc, 0:T], kc == 0, kc == 31,
                           wk_ + [("H", "h", kc)], [bk])
                    conv_chunk(bank, bk, fcg, dst[:, 0:T], key_)
                for s2 in range(2):
                    Wt, wk_ = wload([(lambda w: wview(w, 32, 256), wcols(win_d, O_X + g * 512 + s2 * 256, 256))])
                    wv = wview(Wt, 32, 256)
                    for u in range(2):
                        ch = s2 * 2 + u
                        b = ch % 3
                        bank, bk = ps[b], ("ps", b)
                        for kc in range(32):
                            mm(bank[:, 0:T], wv[:, kc, u * 128:(u + 1) * 128], H[:, kc, 0:T], kc == 0, kc == 31,
                               wk_ + [("H", "h", kc)], [bk])
                        conv_chunk(bank, bk, g * 4 + ch, xsT[:, ch, 0:T], ("R", "xsT", ch))
                        d0 = PO["dfeat"][0] + g * 4 + ch
                        P.act(lambda e, o=ybuf[:, ch, 0:T], i_=xsT[:, ch, 0:T], m=par[:, d0:d0 + 1]:
                              e.mul(out=o, in_=i_, mul=m), [("R", "xsT", ch), "par"], [("R", "ybuf", ch)])
                for s2 in range(2):
                    Wt, wk_ = wload([(lambda w: wview(w, 32, 256), wcols(win_d, O_Z + g * 512 + s2 * 256, 256))])
                    wv = wview(Wt, 32, 256)
                    for u in range(2):
                        ch = s2 * 2 + u
                        b = ch % 3
                        bank, bk = ps[b], ("ps", b)
                        for kc in range(32):
                            mm(bank[:, 0:T], wv[:, kc, u * 128:(u + 1) * 128], H[:, kc, 0:T], kc == 0, kc == 31,
                               wk_ + [("H", "h", kc)], [bk])
                        actf(zs[:, ch, 0:T], bank[:, 0:T], AF.Silu, [bk], [("R", "zs", ch)])
                XS = [("R", "xsT", ch) for ch in range(4)]
                YB = [("R", "ybuf", ch) for ch in range(4)]
                cpy(stbf[:, 0:512], stT[:, g * 512:(g + 1) * 512], [("st", g)], [("R", "stbf")], eng="act")
                def ssd_pre(c, g=g, gh=gh):
                    i = c % 2
                    cs_ = slice(c * L, (c + 1) * L)
                    dsl = slice(c * 32 + gh, c * 32 + gh + 8)
                    for fc in range(4):
                        tr(ps[3][0:L, fc * 128:(fc + 1) * 128], xsT[:, fc, cs_], ident, XS + ["cst"], [("ps", 3)])
                    tt(xbar[i][0:L, 0:512].rearrange("p (h d) -> p h d", d=64),
                       ps[3][0:L, 0:512].rearrange("p (h d) -> p h d", d=64),
                       dt_t[0:L, dsl].unsqueeze(2).to_broadcast([L, 8, 64]), ALU.mult, [("ps", 3), "dt"],
                       [("R", "xbar", i)])
                    tt(xbd[i][0:L, 0:512].rearrange("p (h d) -> p h d", d=64),
                       ps[3][0:L, 0:512].rearrange("p (h d) -> p h d", d=64),
                       dt_s[0:L, dsl].unsqueeze(2).to_broadcast([L, 8, 64]), ALU.mult, [("ps", 3), "dt_s"],
                       [("R", "xbd", i)])
                    tr(ps[4][0:L, 0:128], BTf[:, cs_], ident, [("R", "BT"), "cst"], [("ps", 4, "a")])
                    cpy(Btok[i][0:L, 0:128], ps[4][0:L, 0:128], [("ps", 4, "a")], [("R", "Btok", i)])
                    mm(ps[4][0:L, 128:128 + L], BTf[:, cs_], CTf[:, cs_], True, True, [("R", "BT"), ("R", "CT")],
                       [("ps", 4, "b")])
                    tt(CBm[i][0:L, 0:L], ps[4][0:L, 128:128 + L], U[0:L, 0:L], ALU.mult, [("ps", 4, "b"), "cst"],
                       [("R", "CBm", i)])
                    tt(lseg[0:L, 0:8 * L].rearrange("p (h s) -> p h s", s=L),
                       ident[0:L, 0:L].unsqueeze(1).to_broadcast([L, 8, L]),
                       dt_c[0:L, dsl].unsqueeze(2).to_broadcast([L, 8, L]), ALU.mult, ["dt_c", "cst"], [("R", "lseg")])
                    mm(ps[5][0:L, 0:8 * L], ones[0:L, 0:L], lseg[0:L, 0:8 * L], True, True, [("R", "lseg"), "cst"],
                       [("ps", 5)])
                    mm(ps[6][:, 0:8 * L], ones[0:L, 0:128], lseg[0:L, 0:8 * L], True, True, [("R", "lseg"), "cst"],
                       [("ps", 6)])
                    stt(LT[0:L, 0:8 * L].rearrange("p (h l) -> p h l", l=L),
                        ps[5][0:L, 0:8 * L].rearrange("p (h l) -> p h l", l=L), 0.0,
                        dt_c[0:L, dsl].unsqueeze(2).to_broadcast([L, 8, L]), ALU.add, ALU.subtract,
                        [("ps", 5), "dt_c"], [("R", "LT0")])
                    tt(LT[0:L, 0:8 * L].rearrange("p (h l) -> p h l", l=L),
                       LT[0:L, 0:8 * L].rearrange("p (h l) -> p h l", l=L),
                       U[0:L, 0:L].unsqueeze(1).to_broadcast([L, 8, L]), ALU.mult, [("R", "LT0"), "cst"], [("R", "LT1")])
                    actf(LT[0:L, 0:8 * L], LT[0:L, 0:8 * L], AF.Exp, [("R", "LT1")], [("R", "LT")])
                    tt(GT[i][0:L, 0:8 * L].rearrange("p (h l) -> p h l", l=L),
                       LT[0:L, 0:8 * L].rearrange("p (h l) -> p h l", l=L),
                       CBm[i][0:L, 0:L].unsqueeze(1).to_broadcast([L, 8, L]), ALU.mult, [("R", "LT"), ("R", "CBm", i)],
                       [("R", "GT", i)])
                    actf(Ecs2[i][:, 0:8 * L], ps[6][:, 0:8 * L], AF.Exp, [("ps", 6)], [("R", "Ecs", i)])
                    tt(Cdec[i][:, 0:8 * L].rearrange("p (h l) -> p h l", l=L),
                       Ecs2[i][:, 0:8 * L].rearrange("p (h l) -> p h l", l=L),
                       CTf[:, cs_].unsqueeze(1).to_broadcast([128, 8, L]), ALU.mult, [("R", "Ecs", i), ("R", "CT")],
                       [("R", "Cdec", i)])

                def ssd_post(c, g=g, gh=gh):
                    i = c % 2
                    cs_ = slice(c * L, (c + 1) * L)
                    dsl = slice(c * 32 + gh, c * 32 + gh + 8)
                    mm(ps[7][:, 0:512], Btok[i][0:L, 0:128], xbd[i][0:L, 0:512], True, True,
                       [("R", "Btok", i), ("R", "xbd", i)], [("ps", 7)])
                    for h in range(8):
                        e_ = h % 2
                        o = ps[4][e_ * 64:(e_ + 1) * 64, 256 + (h // 2) * L:256 + (h // 2 + 1) * L]
                        mm(o, xbar[i][0:L, h * 64:(h + 1) * 64], GT[i][0:L, h * L:(h + 1) * L], True, False,
                           [("R", "xbar", i), ("R", "GT", i)], [("ps", 4, "y")])
                        mm(o, stbf[:, h * 64:(h + 1) * 64], Cdec[i][:, h * L:(h + 1) * L], False, True,
                           [("R", "stbf"), ("R", "Cdec", i)], [("ps", 4, "y")])
                    tt(ybuf[:, :, cs_], ybuf[:, :, cs_], ps[4][:, 256:256 + 4 * L].rearrange("p (a t) -> p a t", t=L),
                       ALU.add, YB + [("ps", 4, "y")], YB)
                    tt(stT[:, g * 512:(g + 1) * 512].rearrange("p (h d) -> p h d", d=64),
                       stT[:, g * 512:(g + 1) * 512].rearrange("p (h d) -> p h d", d=64),
                       Ecs2[i][:, 0:8 * L].rearrange("p (h l) -> p h l", l=L)[:, :, L - 1:L].to_broadcast([128, 8, 64]),
                       ALU.mult, [("st", g), ("R", "Ecs", i)], [("st", g)])
                    tt(stT[:, g * 512:(g + 1) * 512], stT[:, g * 512:(g + 1) * 512], ps[7][:, 0:512], ALU.add,
                       [("st", g), ("ps", 7)], [("st", g)])
                    if c + 1 < NCH:
                        cpy(stbf[:, 0:512], stT[:, g * 512:(g + 1) * 512], [("st", g)], [("R", "stbf")], eng="act")

                ssd_pre(0)
                for c in range(NCH):
                    if c + 1 < NCH:
                        ssd_pre(c + 1)
                    ssd_post(c)
                tt(ybuf[:, :, 0:T], ybuf[:, :, 0:T], zs[:, :, 0:T], ALU.mult, YB + [("R", "zs", ch) for ch in range(4)],
                   YB)
                for fc in range(4):
                    i = fc % 2
                    actf(gsq[i][:, 0:T], ybuf[:, fc, 0:T], AF.Square, YB, [("R", "gsq", i)])
                    mm(ps[3][:, 0:T], ones, gsq[i][:, 0:T], fc == 0, fc == 3, [("R", "gsq", i), "cst"], [("ps", 3)])
                actf(grs[:, 0:T], ps[3][:, 0:T], AF.Sqrt, [("ps", 3), "epsc"], [("R", "grs0")], bias=epsc[:, 0:1],
                     scale=1.0 / 512.0)
                recip(grs[:, 0:T], grs[:, 0:T], [("R", "grs0")], [("R", "grs")])
                for fc in range(4):
                    n0 = PO["nw"][0] + g * 4 + fc
                    stt(mixT[:, g * 4 + fc, 0:T], ybuf[:, fc, 0:T], par[:, n0:n0 + 1], grs[:, 0:T], ALU.mult, ALU.mult,
                        YB + [("R", "grs"), "par"], [("M", "mix", g * 4 + fc)])
            if last and (LM & 4):
                dma("sp", ossd_d[r][:, :], stT[:], [("st", g) for g in range(4)], [("ossd", r)], final=True)
            if last and (LM & 8):
                dma("sp", ocv_d[r][:, :], halo[:].rearrange("p a b -> p (a b)"), [("halo", f) for f in range(24)],
                    [("ocv", r)], final=True)
            MIX_ALL = [("M", "mix", c_) for c_ in range(16)] + \
                      [("M", "mix", 16 + a, c_, e_) for a in range(16) for c_ in range(NCH) for e_ in range(2)]

            _stop(5)
            P.fence("R", lambda e: e.memset(small[:, 8:9], 0.0))
            for tc in range(NTC):
                rows = rows_of(tc)
                dma("sp", Rx[0:rows, tc, :], x_src[t0 + tc * 128:t0 + tc * 128 + rows, :], (), [("R", "x1", tc)])
            gpn = {"n": 0}

            def gate_piece(gi, cg):
                i = gpn["n"] % 2
                gpn["n"] += 1
                dma("sp", gp[i][:], grow_d[r:r + 1, gi, cg * 512:(cg + 1) * 512].to_broadcast([128, 512]), GROW,
                    [("gp", i)])
                return gp[i], ("gp", i)

            def proj_tokmajor(wd, kparts, lhs_of, lhs_keys, gi, cg, bset):
                banks = [bset * 4 + tc for tc in range(NTC)]
                nk_total = sum(n for _, n in kparts)
                kdone = 0
                for (kc0, nkc) in kparts:
                    Wt_, wk2 = wload([(lambda w, n=nkc: wview(w, n, 512), wcols(wd, cg * 512, 512, kc0, nkc))])
                    wv_ = wview(Wt_, nkc, 512)
                    for tc in range(NTC):
                        rows = rows_of(tc)
                        for kl in range(nkc):
                            kk = kdone + kl
                            mm(ps[banks[tc]][0:rows, :], lhs_of(kc0 + kl, tc, rows), wv_[:, kl, :], kk == 0,
                               kk == nk_total - 1, wk2 + lhs_keys, [("ps", banks[tc])])
                    kdone += nkc
                gpt, gk = gate_piece(gi, cg)
                for tc in range(NTC):
                    rows = rows_of(tc)
                    i = tc % 2
                    tt(sg[i][0:rows, :], ps[banks[tc]][0:rows, :], gpt[0:rows, :], ALU.mult, [("ps", banks[tc]), gk],
                       [("sg", i)])
                    tt(Rx[0:rows, tc, cg * 512:(cg + 1) * 512], Rx[0:rows, tc, cg * 512:(cg + 1) * 512], sg[i][0:rows, :],
                       ALU.add, [("sg", i), ("R", "x1", tc)], [("R", "x1", tc)])

            for cg in range(8):
                proj_tokmajor(wout_d, [(0, 16), (16, 16)],
                              lambda kc, tc, rows: mixT[:, kc, tc * 128:tc * 128 + rows], MIX_ALL, 0, cg, cg % 2)

            _stop(6)
            P.fence("M", lambda e: e.memset(small[:, 9:10], 0.0))
            Mf = M[:, :].bitcast(F32).rearrange("p (b f) -> p b f", f=4096)
            for tc in range(NTC):
                rows = rows_of(tc)
                i = tc % 2
                c_ = 4 + tc
                actf(Mf[0:rows, i, :], Rx[0:rows, tc, :], AF.Square, [("R", "x1", tc)], [("M", "xn", i), ("ssq", c_)],
                     accum=ssq[0:rows, c_:c_ + 1])
                actf(rstd[0:rows, c_:c_ + 1], ssq[0:rows, c_:c_ + 1], AF.Sqrt, [("ssq", c_), "epsc"], [("rs0", c_)],
                     bias=epsc[0:rows, 0:1], scale=1.0 / D)
                recip(rstd[0:rows, c_:c_ + 1], rstd[0:rows, c_:c_ + 1], [("rs0", c_)], [("rstd", c_)])
                ts(Mf[0:rows, i, :], Rx[0:rows, tc, :], rstd[0:rows, c_:c_ + 1], None, ALU.mult, None,
                   [("R", "x1", tc), ("rstd", c_), ("M", "xn", i)], [("M", "xn", i)])
                for fc in range(32):
                    b = fc % 4
                    bank, bk = ps[b], ("ps", b)
                    tr(bank[:, 0:rows], Mf[0:rows, i, fc * 128:(fc + 1) * 128], ident[0:rows, 0:rows],
                       [("M", "xn", i), "cst"], [bk])
                    if fc % 2 == 0:
                        actf(H[:, fc, tc * 128:tc * 128 + rows], bank[:, 0:rows], AF.Identity, [bk, "s2T"] + MT_ALL,
                             [("H", "h2", fc, tc)], bias=modT[:, 96 + fc, r:r + 1], scale=s2T[:, fc, r:r + 1])
                    else:
                        ts(H[:, fc, tc * 128:tc * 128 + rows], bank[:, 0:rows], s2T[:, fc, r:r + 1],
                           modT[:, 96 + fc, r:r + 1], ALU.mult, ALU.add, [bk, "s2T"] + MT_ALL, [("H", "h2", fc, tc)])
            H2 = lambda kc: [("H", "h2", kc, tc) for tc in range(NTC)]

            _stop(7)
            P.fence("M", lambda e: e.memset(small[:, 9:10], 0.0))
            actT = M[:, :].rearrange("p (c t) -> p c t", t=512)
            parts = [(0, 15), (15, 29), (29, 43)]
            for (q0, q1) in parts:
                nchk = (q1 - q0) * 2
                for q in range(q0, q1):
                    bs = (q % 2) * 4
                    Wg, wkg = wload([(lambda w: wview(w, 32, 256), wcols(wgu_d, q * 256, 256))])
                    wg = wview(Wg, 32, 256)
                    for u in range(2):
                        bank, bk = ps[bs + u], ("ps", bs + u)
                        for kc in range(32):
                            mm(bank[:, 0:T], wg[:, kc, u * 128:(u + 1) * 128], H[:, kc, 0:T], kc == 0, kc == 31,
                               wkg + H2(kc), [bk])
                    Wu, wku = wload([(lambda w: wview(w, 32, 256), wcols(wgu_d, DFF + q * 256, 256))])
                    wu = wview(Wu, 32, 256)
                    for u in range(2):
                        bank, bk = ps[bs + 2 + u], ("ps", bs + 2 + u)
                        for kc in range(32):
                            mm(bank[:, 0:T], wu[:, kc, u * 128:(u + 1) * 128], H[:, kc, 0:T], kc == 0, kc == 31,
                               wku + H2(kc), [bk])
                    for u in range(2):
                        lc = (q - q0) * 2 + u
                        actf(sg[u][:, 0:T], ps[bs + u][:, 0:T], AF.Silu, [("ps", bs + u)], [("sg", u)])
                        tt(actT[:, lc, 0:T], sg[u][:, 0:T], ps[bs + 2 + u][:, 0:T], ALU.mult,
                           [("sg", u), ("ps", bs + 2 + u)], [("M", "act", lc)])
                ACT_ALL = [("M", "act", lc) for lc in range(nchk)]
                kparts = []
                k_ = 0
                while k_ < nchk:
                    n_ = min(16, nchk - k_)
                    kparts.append((q0 * 2 + k_, n_))
                    k_ += n_
                for cg in range(8):
                    proj_tokmajor(wdn_d, kparts,
                                  lambda kc, tc, rows, q0=q0: actT[:, kc - q0 * 2, tc * 128:tc * 128 + rows],
                                  ACT_ALL, 1, cg, cg % 2)
            for tc in range(NTC):
                rows = rows_of(tc)
                dma("sp", y_dst[t0 + tc * 128:t0 + tc * 128 + rows, :], Rx[0:rows, tc, :], [("R", "x1", tc)],
                    [("y", r, t0, tc)], final=True)
            assert wstate["sid"] == NSLAB, wstate["sid"]
            wstate["pass0"] = False


        try:
            _stop(1)
            for ti in range(int(os.environ.get("KNT", n_prompt_tiles))):
                run_tile(0, ti * 512, 512, 64, ti == 0, ti == 3 or bool(os.environ.get("KLAST0")))
                _stop(9)
            if run_sample and not os.environ.get("KNOSAMPLE"):
                run_tile(1, 0, DEC, DEC, True, True)
        except _Stop:
            dma("sp", ys_d[0:2, 0:512], modrow[0][:], [("modrow", 0)], ["dbg"], final=True)

        with nc.Block() as block:
            P.build(nc, block, st)
        print("[kernel] ops=%d waits=%d %s" % (P.stats["ops"], P.stats["waits"], P.stats), flush=True)
    return nc


def _consts():
    c = np.zeros((128, NCONST), np.float32)
    p = np.arange(128)[:, None]
    j = np.arange(128)[None, :]
    c[:, 0:128] = (p == j)
    j64 = np.arange(64)[None, :]
    c[:, 128:192] = (p > j64)
    c[:, 192:256] = (p <= j64)
    c[:, 256:384] = ((p // 64) == (j // 64)) * (1.0 / 64.0)
    c[:, 384:448] = -(128.0 + j64 - p)
    c[:, 448:512] = -np.abs(j64 - p)
    c[:, 512:640] = 1.0
    c[:, 640:704] = c[:, 384:448]
    c[0:64, 640:704] = -1.0e9
    return c


def _params(i):
    f = lambda a: np.asarray(a, np.float32)
    fm = lambda v: f(v).reshape(-1, 128).T
    bc = lambda v: np.broadcast_to(f(v).reshape(1, -1), (128, f(v).size))
    cols = {
        "gmix": fm(i["g_mix"][0]), "gffn": fm(i["g_ffn"][0]), "bada": fm(i["b_ada"][0]),
        "cw": f(i["conv_w"][0]).reshape(4, 24, 128).transpose(2, 1, 0).reshape(128, 96),
        "cb": fm(i["conv_b"][0]),
        "dfeat": fm(np.repeat(f(i["d_skip"][0]), 64)), "nw": fm(i["ssd_norm_w"][0]),
        "wq": np.tile(f(i["q_norm_w"][0]), 2).reshape(128, 1), "wk": np.tile(f(i["k_norm_w"][0]), 2).reshape(128, 1),
        "dtb": bc(i["dt_bias"][0]), "alog": bc(i["a_log"][0]), "sinks": bc(i["sinks"][0]),
        "wqr": bc(i["q_norm_w"][0]), "wkr": bc(i["k_norm_w"][0]),
    }
    par = np.zeros((128, NPAR), np.float32)
    for n, (a, b) in PO.items():
        par[:, a:b] = cols[n]
    return par


_NC_CACHE = {}


def kernel(**inputs):
    i = {k: np.asarray(v) for k, v in inputs.items()}
    ncores = 8
    core_ids = list(range(ncores))
    dbg = os.environ.get("KDEBUG_CORES")
    if dbg:
        core_ids = list(range(int(dbg)))
    key = "main"
    if key not in _NC_CACHE:
        _NC_CACHE[key] = build_nc()
    nc = _NC_CACHE[key]
    par = _params(i)
    cst = _consts()
    f = lambda a: np.ascontiguousarray(a, dtype=np.float32)
    shared = {
        "par": par, "cst": cst, "badarow": f(i["b_ada"][0].reshape(1, -1)),
        "w_ada": f(i["w_ada"][0]), "w_in": f(i["w_in"][0]), "w_out": f(i["w_out"][0]),
        "w_gu": f(i["w_gate_up"][0]), "w_down": f(i["w_down"][0]),
    }
    in_maps = []
    for b in core_ids:
        ck = i["cache_k"][0, b]
        ckT = np.concatenate([ck.transpose(2, 1, 0)] * 2, axis=0)
        c2 = np.stack([i["c_prompt"][b], i["c_sample"][b]], axis=-1)
        m = dict(shared)
        m.update({
            "xp": f(i["x_prompt"][b]), "xs": f(i["x_sample"][b]),
            "sst": f(i["state_ssd"][0, b].reshape(2048, 128).T),
            "scv": f(i["state_conv"][0, b].reshape(3, 24, 128).transpose(2, 1, 0).reshape(128, 72)),
            "ckT": f(ckT.reshape(128, 512)), "cv": f(i["cache_v"][0, b].reshape(128, 256)),
            "c2T": f(c2.reshape(32, 128, 2).transpose(1, 0, 2).reshape(128, 64)),
        })
        in_maps.append(m)
    res = run_bass_kernel_spmd(nc, in_maps, core_ids=core_ids)
    rs = res.results
    nb = len(core_ids)
    B = 8
    yp = np.zeros((B, SEQ, D), np.float32)
    ys = np.zeros((B, DEC, D), np.float32)
    ssd_p = np.zeros((1, B, 32, 64, 128), np.float32)
    ssd_s = np.zeros((1, B, 32, 64, 128), np.float32)
    cv_p = np.zeros((1, B, 3, 3072), np.float32)
    cv_s = np.zeros((1, B, 3, 3072), np.float32)
    k_p = np.zeros((1, B, 128, 4, 64), np.float32)
    v_p = np.zeros((1, B, 128, 4, 64), np.float32)
    k_s = np.zeros((1, B, DEC, 4, 64), np.float32)
    v_s = np.zeros((1, B, DEC, 4, 64), np.float32)
    for b in range(nb):
        o = rs[b]
        yp[b] = o["yp"]
        ys[b] = o["ys"]
        ssd_p[0, b] = o["ossd_p"].T.reshape(32, 64, 128)
        ssd_s[0, b] = o["ossd_s"].T.reshape(32, 64, 128)
        cv_p[0, b] = o["ocv_p"].reshape(128, 24, 3).transpose(2, 1, 0).reshape(3, 3072)
        cv_s[0, b] = o["ocv_s"].reshape(128, 24, 3).transpose(2, 1, 0).reshape(3, 3072)
        k_p[0, b] = o["ok_p"].reshape(64, 4, 128).transpose(2, 1, 0)
        k_s[0, b] = o["ok_s"].reshape(64, 4, 128).transpose(2, 1, 0)[:DEC]
        v_p[0, b] = o["ov_p"].reshape(128, 4, 64)
        v_s[0, b] = o["ov_s"].reshape(128, 4, 64)[:DEC]
    return (yp, ys, ssd_p, cv_p, k_p, v_p, ssd_s, cv_s, k_s, v_s)
```
